# Optimizing a Trainium2 kernel written in Bass

```python
import jax, jax.numpy as jnp
from jax import lax
import numpy as np

D_MODEL = 2048
BATCH = 4
SEQ = 4096
DEPTH = 1
DEC_BATCH = 1
DEC_SEQ = 16384
PAST_LEN = 128

HEAD_DIM = 128
N_ATTN_HEADS = 8
N_KV_HEADS = 2
ATTN_WIDTH = N_ATTN_HEADS * HEAD_DIM
KV_WIDTH = N_KV_HEADS * HEAD_DIM
Q_BLOCK = 128
ROPE_THETA = 10000.0
ROPE_AXIS_DIM = HEAD_DIM // 2
GRID_W = 64
N_GLA_HEADS = 4
GLA_DK = 128
GLA_DV = 256
GLA_K_WIDTH = N_GLA_HEADS * GLA_DK
GLA_V_WIDTH = N_GLA_HEADS * GLA_DV
GATE_RANK = 16
GATE_TAU = 16.0
GLA_CHUNK = 64
MIX_WIDTH = ATTN_WIDTH + GLA_V_WIDTH
IN_SIZES = (ATTN_WIDTH, KV_WIDTH, KV_WIDTH, GLA_K_WIDTH, GLA_K_WIDTH, GLA_V_WIDTH, GLA_V_WIDTH, GATE_RANK, GATE_RANK)
D_IN = ATTN_WIDTH + 2 * KV_WIDTH + 2 * GLA_K_WIDTH + 2 * GLA_V_WIDTH + 2 * GATE_RANK
D_FF = -(-8 * D_MODEL // (3 * 256)) * 256
D_PLE = 256
EPS = 1e-6

kernel_name = "hymba_gqa_gla_bidir_encoder"


def rmsnorm(x, g):
    xf = x.astype(jnp.float32)
    y = xf * lax.rsqrt(jnp.mean(xf * xf, axis=-1, keepdims=True) + EPS)
    return (y * g.astype(jnp.float32)).astype(x.dtype)


def split_cols(z, sizes):
    out = []
    off = 0
    for s in sizes:
        out.append(z[..., off:off + s])
        off += s
    return out


def axial_rope_tables(seq_len):
    n_rows = seq_len // GRID_W
    row = jnp.repeat(jnp.arange(n_rows, dtype=jnp.float32), GRID_W)
    col = jnp.tile(jnp.arange(GRID_W, dtype=jnp.float32), n_rows)
    freqs = ROPE_THETA ** (-jnp.arange(0, ROPE_AXIS_DIM, 2, dtype=jnp.float32) / ROPE_AXIS_DIM)
    ang_r = row[:, None] * freqs[None, :]
    ang_c = col[:, None] * freqs[None, :]
    return (jnp.cos(ang_r)[:, None, :], jnp.sin(ang_r)[:, None, :],
            jnp.cos(ang_c)[:, None, :], jnp.sin(ang_c)[:, None, :])


def rotate_half(x, cos, sin):
    h = x.shape[-1] // 2
    x1, x2 = x[..., :h], x[..., h:]
    return jnp.concatenate([x1 * cos - x2 * sin, x2 * cos + x1 * sin], axis=-1)


def apply_axial_rope(x, rope):
    cos_r, sin_r, cos_c, sin_c = rope
    xf = x.astype(jnp.float32)
    xr = rotate_half(xf[..., :ROPE_AXIS_DIM], cos_r, sin_r)
    xc = rotate_half(xf[..., ROPE_AXIS_DIM:], cos_c, sin_c)
    return jnp.concatenate([xr, xc], axis=-1).astype(x.dtype)


def block_attention(q, k, v):
    B, S, _, D = q.shape
    nb = S // Q_BLOCK
    G = N_ATTN_HEADS // N_KV_HEADS
    qb = q.reshape(B, nb, Q_BLOCK, N_KV_HEADS, G, D).transpose(1, 0, 3, 4, 2, 5)
    kt = k.transpose(0, 2, 1, 3)
    vt = v.transpose(0, 2, 1, 3)
    scale = D ** -0.5

    def one_block(qblk):
        s = jnp.einsum('bhgqd,bhkd->bhgqk', qblk, kt).astype(jnp.float32) * scale
        p = jax.nn.softmax(s, axis=-1).astype(vt.dtype)
        return jnp.einsum('bhgqk,bhkd->bhgqd', p, vt)

    o = lax.map(one_block, qb)
    return o.transpose(1, 0, 4, 2, 3, 5).reshape(B, S, N_ATTN_HEADS * D)


def gla_chunked(q, k, v, log_a):
    B, H, S, DK = q.shape
    DV = v.shape[-1]
    C = GLA_CHUNK
    N = S // C
    q = q.reshape(B, H, N, C, DK)
    k = k.reshape(B, H, N, C, DK)
    v = v.reshape(B, H, N, C, DV)
    b = jnp.cumsum(log_a.reshape(B, H, N, C, DK), axis=3)
    b_last = b[:, :, :, -1:, :]
    q_dec = q * jnp.exp(b)
    k_dec = k * jnp.exp(-b)
    mask = jnp.tril(jnp.ones((C, C), dtype=bool))
    A = jnp.where(mask, jnp.einsum('bhnid,bhnjd->bhnij', q_dec, k_dec), 0.0)
    o_intra = jnp.einsum('bhnij,bhnjv->bhniv', A, v)
    chunk_kv = jnp.einsum('bhncd,bhncv->nbhdv', k * jnp.exp(b_last - b), v)
    chunk_decay = jnp.moveaxis(jnp.exp(b_last[:, :, :, 0, :]), 2, 0)

    def step(state, inp):
        kv, dec = inp
        return dec[..., None] * state + kv, state

    _, states = lax.scan(step, jnp.zeros((B, H, DK, DV), jnp.float32), (chunk_kv, chunk_decay))
    o_inter = jnp.einsum('bhncd,nbhdv->bhncv', q_dec, states)
    return (o_intra + o_inter).reshape(B, H, S, DV)


def to_heads(t, n_heads):
    B, S, W = t.shape
    return t.reshape(B, S, n_heads, W // n_heads).transpose(0, 2, 1, 3)


def gla_mixer(q, k, v, g_out, gf_low, gb_low, w_gf_up, b_gf, w_gb_up, b_gb, g_gla_norm):
    B, S, _ = q.shape
    f32 = jnp.float32
    qh = to_heads(q, N_GLA_HEADS).astype(f32) * (GLA_DK ** -0.5)
    kh = to_heads(k, N_GLA_HEADS).astype(f32)
    vh = to_heads(v, N_GLA_HEADS).astype(f32)
    la_f = to_heads(jax.nn.log_sigmoid((gf_low @ w_gf_up + b_gf).astype(f32)) / GATE_TAU, N_GLA_HEADS)
    la_b = to_heads(jax.nn.log_sigmoid((gb_low @ w_gb_up + b_gb).astype(f32)) / GATE_TAU, N_GLA_HEADS)
    o_f = gla_chunked(qh, kh, vh, la_f)
    flip = lambda t: jnp.flip(t, axis=2)
    o_b = flip(gla_chunked(flip(qh), flip(kh), flip(vh), flip(la_b)))
    diag = jnp.sum(qh * kh, axis=-1, keepdims=True) * vh
    o = (o_f + o_b - diag).transpose(0, 2, 1, 3)
    o = rmsnorm(o, g_gla_norm) * jax.nn.silu(g_out.astype(f32)).reshape(B, S, N_GLA_HEADS, GLA_DV)
    return o.reshape(B, S, GLA_V_WIDTH).astype(q.dtype)


def encoder_layer(x, p, rope, g_pre_mix, w_in, g_q, g_k, w_gf_up, b_gf, w_gb_up, b_gb, g_gla_norm,
                  w_out, g_post_mix, g_pre_ffn, w_gate_up, w_down, g_post_ffn,
                  g_ple_pre, w_ple_gate, w_ple_proj, g_ple_post):
    B, S, _ = x.shape
    h = rmsnorm(x, g_pre_mix)
    z = h @ w_in
    qa, ka, va, qg, kg, vg, og, gf_low, gb_low = split_cols(z, IN_SIZES)
    qa = apply_axial_rope(rmsnorm(qa.reshape(B, S, N_ATTN_HEADS, HEAD_DIM), g_q), rope)
    ka = apply_axial_rope(rmsnorm(ka.reshape(B, S, N_KV_HEADS, HEAD_DIM), g_k), rope)
    va = va.reshape(B, S, N_KV_HEADS, HEAD_DIM)
    attn_out = block_attention(qa, ka, va)
    gla_out = gla_mixer(qg, kg, vg, og, gf_low, gb_low, w_gf_up, b_gf, w_gb_up, b_gb, g_gla_norm)
    mix = jnp.concatenate([attn_out, gla_out], axis=-1) @ w_out
    x = x + rmsnorm(mix, g_post_mix)
    gate, up = jnp.split(rmsnorm(x, g_pre_ffn) @ w_gate_up, 2, axis=-1)
    x = x + rmsnorm((jax.nn.silu(gate) * up) @ w_down, g_post_ffn)
    pg = jax.nn.sigmoid(rmsnorm(x, g_ple_pre) @ w_ple_gate)
    x = x + rmsnorm((p @ w_ple_proj) * pg, g_ple_post)
    return x


def setup_inputs(seed: int = 0) -> dict:
    key = jax.random.key(seed)
    ks = iter(jax.random.split(key, 32))
    nrm = lambda shape: jax.random.normal(next(ks), shape, jnp.float32)
    gain = lambda n: 1.0 + 0.02 * nrm((DEPTH, n))
    return {
        "x_prompt": nrm((BATCH, SEQ, D_MODEL)),
        "x_sample": nrm((DEC_BATCH, DEC_SEQ, D_MODEL)),
        "p_prompt": nrm((DEPTH, BATCH, SEQ, D_PLE)),
        "p_sample": nrm((DEPTH, DEC_BATCH, DEC_SEQ, D_PLE)),
        "g_pre_mix": gain(D_MODEL),
        "w_in": nrm((DEPTH, D_MODEL, D_IN)) * D_MODEL ** -0.5,
        "g_q": gain(HEAD_DIM),
        "g_k": gain(HEAD_DIM),
        "w_gf_up": nrm((DEPTH, GATE_RANK, GLA_K_WIDTH)) * GATE_RANK ** -0.5,
        "b_gf": 1.0 + 0.1 * nrm((DEPTH, GLA_K_WIDTH)),
        "w_gb_up": nrm((DEPTH, GATE_RANK, GLA_K_WIDTH)) * GATE_RANK ** -0.5,
        "b_gb": 1.0 + 0.1 * nrm((DEPTH, GLA_K_WIDTH)),
        "g_gla_norm": gain(GLA_DV),
        "w_out": nrm((DEPTH, MIX_WIDTH, D_MODEL)) * MIX_WIDTH ** -0.5,
        "g_post_mix": gain(D_MODEL),
        "g_pre_ffn": gain(D_MODEL),
        "w_gate_up": nrm((DEPTH, D_MODEL, 2 * D_FF)) * D_MODEL ** -0.5,
        "w_down": nrm((DEPTH, D_FF, D_MODEL)) * D_FF ** -0.5,
        "g_post_ffn": gain(D_MODEL),
        "g_ple_pre": gain(D_MODEL),
        "w_ple_gate": nrm((DEPTH, D_MODEL, D_MODEL)) * D_MODEL ** -0.5,
        "w_ple_proj": nrm((DEPTH, D_PLE, D_MODEL)) * D_PLE ** -0.5,
        "g_ple_post": gain(D_MODEL),
    }


def reference(x_prompt, x_sample, p_prompt, p_sample, g_pre_mix, w_in, g_q, g_k, w_gf_up, b_gf,
              w_gb_up, b_gb, g_gla_norm, w_out, g_post_mix, g_pre_ffn, w_gate_up, w_down, g_post_ffn,
              g_ple_pre, w_ple_gate, w_ple_proj, g_ple_post):
    rope_prompt = axial_rope_tables(x_prompt.shape[1])
    rope_sample = axial_rope_tables(x_sample.shape[1])
    y_prompt = x_prompt
    y_sample = x_sample
    for i in range(DEPTH):
        w = (g_pre_mix[i], w_in[i], g_q[i], g_k[i], w_gf_up[i], b_gf[i], w_gb_up[i], b_gb[i],
             g_gla_norm[i], w_out[i], g_post_mix[i], g_pre_ffn[i], w_gate_up[i], w_down[i],
             g_post_ffn[i], g_ple_pre[i], w_ple_gate[i], w_ple_proj[i], g_ple_post[i])
        y_prompt = encoder_layer(y_prompt, p_prompt[i], rope_prompt, *w)
        y_sample = encoder_layer(y_sample, p_sample[i], rope_sample, *w)
    return (y_prompt, y_sample)
```

```python
import os
import numpy as np
from contextlib import ExitStack
import concourse.bass as bass
import concourse.mybir as mybir
from concourse.bass_utils import run_bass_kernel_spmd

F32 = mybir.dt.float32
BF16 = mybir.dt.bfloat16
I32 = mybir.dt.int32
AF = mybir.ActivationFunctionType
ALU = mybir.AluOpType
AX = mybir.AxisListType

D = 2048
DIN = 4640
DFF = 5632
DPLE = 256
NOWN = 4096
NCTX = 20480
EPS = 1e-6
PI = float(np.pi)


class Buf:
    __slots__ = ("name", "lw", "rd")

    def __init__(self, name):
        self.name = name
        self.lw = None
        self.rd = {}


class Tile:
    __slots__ = ("ap", "bufs")

    def __init__(self, ap, bufs):
        self.ap = ap
        self.bufs = bufs


class Op:
    __slots__ = ("eng", "fn", "waits", "signal", "semkey", "pos", "semval", "isdma")


def _flat(lst):
    out = []
    for x in lst:
        if x is None:
            continue
        if isinstance(x, Buf):
            out.append(x)
        elif isinstance(x, Tile):
            out.extend(x.bufs)
        else:
            out.extend(_flat(x))
    return out


class Sched:
    ENGS = ("pe", "dve", "act", "pool", "sp")

    def __init__(self):
        self.streams = {e: [] for e in self.ENGS}
        self.dma_count = {}
        self.seen = {e: {} for e in self.ENGS}
        self.final = []

    def add(self, eng, fn, reads=(), writes=(), dma=None):
        reads = _flat(reads)
        writes = _flat(writes)
        op = Op()
        op.eng = eng
        op.fn = fn
        op.signal = False
        op.isdma = dma is not None
        op.waits = []
        op.semval = None
        stream = self.streams[eng]
        deps = {}
        seen = self.seen[eng]
        pe_compute = (eng == "pe") and not op.isdma

        def need(d):
            if d is None:
                return
            if pe_compute and (not d.isdma) and d.eng == "pe":
                return
            k = d.semkey
            p = self.dma_count[k[1]] if d.isdma else d.pos
            if p <= seen.get(k, -1):
                return
            cur = deps.get(k)
            if cur is None or p > cur[0]:
                deps[k] = (p, d)
            elif (not d.isdma) and d.pos > cur[1].pos:
                deps[k] = (p, d)

        for b in reads:
            need(b.lw)
        for b in writes:
            need(b.lw)
            for r in b.rd.values():
                need(r)
        for k, (p, d) in deps.items():
            if d.isdma:
                op.waits.append((k[1], p * 16))
            else:
                d.signal = True
                op.waits.append(d)
            seen[k] = p
        if op.isdma:
            n = self.dma_count.get(dma, 0) + 1
            self.dma_count[dma] = n
            op.semkey = ("dma", dma)
            op.pos = n
        else:
            op.semkey = ("eng", eng)
            op.pos = len(stream)
        stream.append(op)
        for b in writes:
            b.lw = op
            b.rd = {}
        for b in reads:
            b.rd[op.semkey] = op
        return op

    def emit(self, nc, es):
        engsem = {e: es.enter_context(nc.semaphore("s_" + e)) for e in ("pe", "dve", "act", "pool")}
        dmasem = {k: es.enter_context(nc.semaphore("d_" + k)) for k in self.dma_count}
        for e, stream in self.streams.items():
            c = 0
            for op in stream:
                if (not op.isdma) and op.signal:
                    c += 1
                    op.semval = c
        block = es.enter_context(nc.Block())
        streams = self.streams
        final_waits = [(k, self.dma_count[k] * 16) for k in self.final if k in self.dma_count]

        def run(engname, eng):
            for op in streams[engname]:
                for w in op.waits:
                    if isinstance(w, Op):
                        eng.wait_ge(engsem[w.eng], w.semval)
                    else:
                        eng.wait_ge(dmasem[w[0]], w[1])
                inst = op.fn(eng)
                if op.isdma:
                    inst.then_inc(dmasem[op.semkey[1]], 16)
                elif op.signal:
                    inst.then_inc(engsem[engname], 1)

        @block.tensor
        def _(e):
            run("pe", e)

        @block.vector
        def _(e):
            run("dve", e)

        @block.scalar
        def _(e):
            run("act", e)

        @block.gpsimd
        def _(e):
            run("pool", e)

        @block.sync
        def _(e):
            run("sp", e)
            for k, v in final_waits:
                e.wait_ge(dmasem[k], v)


GRAN = 2048


class Arena:
    def __init__(self, nc, es, name, nbytes):
        assert nbytes % 4 == 0
        self.nbytes = nbytes
        self.t = es.enter_context(nc.sbuf_tensor(name, [128, nbytes // 4], F32))
        self.gr = [Buf("%s_%d" % (name, i)) for i in range((nbytes + GRAN - 1) // GRAN)]

    def view(self, off, shape, dt, parts=128):
        esz = 2 if dt == BF16 else 4
        n = 1
        for s in shape:
            n *= s
        nb = n * esz
        assert off % 4 == 0 and nb % 4 == 0 and off + nb <= self.nbytes, (off, nb, self.nbytes)
        ap = self.t[0:parts, off // 4:(off + nb) // 4]
        if dt != F32:
            ap = ap.bitcast(dt)
        if len(shape) == 2:
            ap = ap.rearrange("p (a b) -> p a b", a=shape[0])
        elif len(shape) == 3:
            ap = ap.rearrange("p (a b c) -> p a b c", a=shape[0], b=shape[1])
        elif len(shape) == 4:
            ap = ap.rearrange("p (a b c d) -> p a b c d", a=shape[0], b=shape[1], c=shape[2])
        return Tile(ap, self.gr[off // GRAN:(off + nb - 1) // GRAN + 1])


KB = 1024
IN_GROUP_COLS = [0, 512, 1024, 1536, 2048, 2560, 3072, 3584, 4096]


def build_program(debug=(), n_ctx_blocks=40, n_own_blocks=8):
    nc = bass.Bass("TRN2", target_bir_lowering=False)
    S = Sched()
    dbg_outs = {}

    def din(name, shape):
        return nc.dram_tensor(name, list(shape), F32, kind="ExternalInput")

    x_own = din("x_own", [NOWN, D])
    p_own = din("p_own", [NOWN, DPLE])
    x_ctx = din("x_ctx", [NCTX, D])
    pos_all = din("pos_all", [NOWN + NCTX, 2])
    masks_d = din("masks", [1, 400])
    g_pre_mix = din("g_pre_mix", [1, D])
    w_in = din("w_in", [D, DIN])
    g_q = din("g_q", [1, 128])
    g_k = din("g_k", [1, 128])
    w_gf_up = din("w_gf_up", [16, 512])
    b_gf = din("b_gf", [1, 512])
    w_gb_up = din("w_gb_up", [16, 512])
    b_gb = din("b_gb", [1, 512])
    g_gla_norm = din("g_gla_norm", [1, 256])
    w_out = din("w_out", [D, D])
    g_post_mix = din("g_post_mix", [1, D])
    g_pre_ffn = din("g_pre_ffn", [1, D])
    w_gate_up = din("w_gate_up", [D, 2 * DFF])
    w_down = din("w_down", [DFF, D])
    g_post_ffn = din("g_post_ffn", [1, D])
    g_ple_pre = din("g_ple_pre", [1, D])
    w_ple_gate = din("w_ple_gate", [D, D])
    w_ple_proj = din("w_ple_proj", [DPLE, D])
    g_ple_post = din("g_ple_post", [1, D])
    y_own = nc.dram_tensor("y_own", [NOWN, D], F32, kind="ExternalOutput")

    def dscr(name, shape, dt):
        return nc.dram_tensor(name, list(shape), dt, kind="Internal")

    win_r = dscr("win_r", [9, 128, 16 * 512], BF16)
    wgates_r = dscr("wgates_r", [128, 16 * 32], BF16)
    wout_r = dscr("wout_r", [4, 128, 16 * 512], BF16)
    wgu_r = dscr("wgu_r", [22, 128, 2 * 16 * 256], BF16)
    wdn_r = dscr("wdn_r", [16, 128, 11 * 512], BF16)
    wpg_r = dscr("wpg_r", [4, 128, 16 * 512], BF16)
    wpp_r = dscr("wpp_r", [128, 2 * 2048], BF16)
    kT_ctx = dscr("kT_ctx", [2, 128, NCTX], BF16)
    v_r = dscr("v_r", [2, 128, 160 * 128], BF16)
    rope_d = dscr("rope_d", [NOWN + NCTX, 128], F32)
    sfb_d = dscr("sfb_d", [2, 5, 128, 1024], F32)

    B_win = [Buf("win%d" % i) for i in range(9)]
    B_wgates = Buf("wgates_r")
    B_wout = [Buf("wout%d" % i) for i in range(4)]
    B_wgu = [Buf("wgu%d" % i) for i in range(22)]
    B_wdn = [Buf("wdn%d" % i) for i in range(16)]
    B_wpg = [Buf("wpg%d" % i) for i in range(4)]
    B_wpp = Buf("wpp")
    B_kT = [Buf("kTctx%d" % i) for i in range(40)]
    B_v = [Buf("vctx%d" % i) for i in range(40)]
    B_rope = [Buf("rope%d" % i) for i in range(48)]
    B_sfb = [[Buf("sfb%d_%d" % (s, j)) for j in range(5)] for s in range(2)]
    B_yout = Buf("yout")

    es = ExitStack()
    with es:
        def sbt(name, shape, dt):
            t = es.enter_context(nc.sbuf_tensor(name, list(shape), dt))
            return Tile(t[:], [Buf(name)])

        add = S.add

        def TT(out, in0, in1, op, r, w, eng="dve"):
            add(eng, lambda e: e.tensor_tensor(out=out, in0=in0, in1=in1, op=op), r, w)

        def TS(out, in0, s1, op0, r, w, s2=None, op1=None, eng="dve"):
            if op1 is None:
                add(eng, lambda e: e.tensor_scalar(out=out, in0=in0, scalar1=s1, scalar2=None, op0=op0), r, w)
            else:
                add(eng, lambda e: e.tensor_scalar(out=out, in0=in0, scalar1=s1, scalar2=s2, op0=op0, op1=op1), r, w)

        def STT(out, in0, scalar, in1, op0, op1, r, w):
            add("dve", lambda e: e.scalar_tensor_tensor(out=out, in0=in0, scalar=scalar, in1=in1, op0=op0, op1=op1), r, w)

        def ACT(out, in_, func, r, w, scale=None, bias=None, accum=None):
            kw = {}
            if scale is not None:
                kw["scale"] = scale
            if bias is not None:
                kw["bias"] = bias
            if accum is not None:
                kw["accum_out"] = accum
            add("act", lambda e: e.activation(out=out, in_=in_, func=func, **kw), r, w)

        def MM(out, lhsT, rhs, start, stop, r, w):
            add("pe", lambda e: e.matmul(out, lhsT=lhsT, rhs=rhs, start=start, stop=stop), r, w)

        def TR(out, in_, r, w):
            add("pe", lambda e: e.transpose(out=out, in_=in_, identity=ident.ap), list(r) + [ident], w)

        def DMA(q, out, in_, r, w, key, slow=False):
            if slow:
                add(q, lambda e: e.dma_start(out=out, in_=in_, allow_slow_non_contiguous=True), r, w, key)
            else:
                add(q, lambda e: e.dma_start(out=out, in_=in_), r, w, key)

        ident = sbt("ident", [128, 128], BF16)
        ones_f = sbt("ones_f", [128, 512], F32)
        ones_b = sbt("ones_b", [128, 128], BF16)
        maskf = sbt("maskf", [128, 128], F32)
        maskb = sbt("maskb", [128, 128], F32)
        rmask = sbt("rmask", [128, 512], F32)
        g_fm = sbt("g_fm", [128, 3, 16], F32)
        gq_bc = sbt("gq_bc", [128, 128], F32)
        gk_bc = sbt("gk_bc", [128, 128], F32)
        ggla_bc = sbt("ggla_bc", [128, 256], F32)
        nbg = sbt("nbg", [128, 2, 4], F32)
        wg_up = sbt("wg_up", [16, 2, 512], BF16)
        masks = sbt("masks_sb", [128, 40, 10], F32)
        negshift = sbt("negshift", [128, 1], F32)
        wgates = sbt("wgates", [128, 16, 32], BF16)
        stat_t = es.enter_context(nc.sbuf_tensor("stat", [128, 1104], F32))
        stat2_t = es.enter_context(nc.sbuf_tensor("stat2", [128, 1104], F32))
        small = sbt("small", [128, 64], F32)
        posT = sbt("posT", [128, 192, 2], F32)
        freq4 = sbt("freq4", [128, 128], F32)
        phase4 = sbt("phase4", [128, 128], F32)
        fi = sbt("fi", [128, 32], I32)
        dacc = sbt("dacc", [128, 4, 4], F32)
        dtmp = sbt("dtmp", [128, 4, 4], F32)
        dtmp2 = sbt("dtmp2", [128, 4, 4], F32)
        Sf_own = sbt("Sf_own", [128, 4, 256], F32)
        dtl = sbt("dtl", [128, 2, 4, 4], F32)

        T16a = Arena(nc, es, "T16a", 16 * KB)
        T16b = Arena(nc, es, "T16b", 16 * KB)
        W = Arena(nc, es, "W", 48 * KB)
        R = Arena(nc, es, "R", 86 * KB)
        XN = Arena(nc, es, "XN", 8 * KB)

        hTa = T16a.view(0, [16, 512], BF16)
        hTb = T16b.view(0, [16, 512], BF16)
        wslot = [W.view(i * 16 * KB, [16, 512], BF16) for i in range(3)]
        xn_bf = [XN.view(i * 4 * KB, [2048], BF16) for i in range(2)]
        gtmp = XN.view(0, [2048], F32)

        banks = []
        dbanks = []
        for k in range(4):
            t = es.enter_context(nc.psum_tensor("dbank%d" % k, [128, 1024], F32))
            b0, b1 = Buf("bank%d" % (2 * k)), Buf("bank%d" % (2 * k + 1))
            banks.append(Tile(t[:, 0:512], [b0]))
            banks.append(Tile(t[:, 512:1024], [b1]))
            dbanks.append(Tile(t[:], [b0, b1]))

        def bank_bf(i):
            return banks[i].ap.bitcast(BF16)

        stat_col = [0]
        stat_memset = [None]

        def new_stat(n=1):
            c = stat_col[0]
            stat_col[0] += n
            assert stat_col[0] <= 1104
            b1 = Buf("st%d" % c)
            b1.lw = stat_memset[0]
            return Tile(stat_t[:, c:c + n], [b1]), Tile(stat2_t[:, c:c + n], [Buf("st2_%d" % c)])

        def dbg(name, tile, ap, shape):
            if name not in debug:
                return
            t = nc.dram_tensor("dbg_" + name, list(shape), F32, kind="ExternalOutput")
            dbg_outs[name] = shape
            add("pool", lambda e: e.dma_start(out=t.ap(), in_=ap), reads=[tile], dma="dbg_" + name)
            S.final.append("dbg_" + name)

        add("pool", lambda e: e.memset(ones_f.ap, 1.0), writes=[ones_f])
        add("pool", lambda e: e.memset(ones_b.ap, 1.0), writes=[ones_b])
        stat_memset[0] = add("pool", lambda e: e.memset(stat_t[:], 0.0), writes=[])
        add("pool", lambda e: e.memset(rmask.ap, 1.0), writes=[rmask])
        add("pool", lambda e: e.memset(rmask.ap.rearrange("p (c t) -> p c t", t=128)[:, :, 0:1], 0.0), writes=[rmask])
        add("pool", lambda e: e.affine_select(out=maskf.ap, in_=ones_f.ap[:, 0:128], pattern=[[1, 128]], compare_op=ALU.is_ge,
                                              fill=0.0, base=0, channel_multiplier=-1), reads=[ones_f], writes=[maskf])
        add("pool", lambda e: e.affine_select(out=maskb.ap, in_=ones_f.ap[:, 0:128], pattern=[[-1, 128]], compare_op=ALU.is_ge,
                                              fill=0.0, base=-1, channel_multiplier=1), reads=[ones_f], writes=[maskb])
        add("pool", lambda e: e.affine_select(out=freq4.ap, in_=ones_f.ap[:, 0:128], pattern=[[-1, 128]], compare_op=ALU.is_equal,
                                              fill=0.0, base=0, channel_multiplier=1), reads=[ones_f], writes=[freq4])
        add("dve", lambda e: e.tensor_copy(out=ident.ap, in_=freq4.ap), reads=[freq4], writes=[ident])

        add("sp", lambda e: e.dma_start(out=gq_bc.ap, in_=g_q.ap().partition_broadcast(128)[:, 0, :]), writes=[gq_bc], dma="c0")
        add("sp", lambda e: e.dma_start(out=gk_bc.ap, in_=g_k.ap().partition_broadcast(128)[:, 0, :]), writes=[gk_bc], dma="c0")
        add("sp", lambda e: e.dma_start(out=ggla_bc.ap, in_=g_gla_norm.ap().partition_broadcast(128)[:, 0, :]), writes=[ggla_bc], dma="c0")
        add("sp", lambda e: e.dma_start(out=masks.ap.rearrange("p a b -> p (a b)"), in_=masks_d.ap().partition_broadcast(128)[:, 0, :]),
            writes=[masks], dma="c0")
        for i, gsrc in enumerate((g_pre_mix, g_pre_ffn, g_ple_pre)):
            add("sp", lambda e, i=i, gsrc=gsrc: e.dma_start(out=g_fm.ap[:, i, :], in_=gsrc.ap().rearrange("o (c p) -> p (o c)", p=128),
                                                            allow_slow_non_contiguous=True), writes=[g_fm], dma="c0")
        for i, bsrc in enumerate((b_gf, b_gb)):
            add("sp", lambda e, i=i, bsrc=bsrc: e.dma_start(out=nbg.ap[:, i, :], in_=bsrc.ap().rearrange("o (c p) -> p (o c)", p=128),
                                                            allow_slow_non_contiguous=True), writes=[nbg], dma="c0")
        add("dve", lambda e: e.tensor_scalar(out=nbg.ap, in0=nbg.ap, scalar1=-1.0, scalar2=None, op0=ALU.mult), reads=[nbg], writes=[nbg])
        add("sp", lambda e: e.dma_start(out=posT.ap, in_=pos_all.ap().rearrange("(t p) c -> p t c", p=128),
                                        allow_slow_non_contiguous=True), writes=[posT], dma="c0")
        add("pool", lambda e: e.dma_start(out=wg_up.ap[:, 0, :], in_=w_gf_up.ap()), writes=[wg_up], dma="c1")
        add("pool", lambda e: e.dma_start(out=wg_up.ap[:, 1, :], in_=w_gb_up.ap()), writes=[wg_up], dma="c1")

        conv_list = []
        win_v = w_in.ap().rearrange("(kc p) n -> p kc n", p=128)
        conv_list.append((wgates_r.ap().rearrange("p (kc n) -> p kc n", n=32), win_v[:, :, 4608:4640], B_wgates))
        for gi in (2, 4, 5, 6, 3, 7, 8, 0, 1):
            c0 = IN_GROUP_COLS[gi]
            conv_list.append((win_r.ap()[gi].rearrange("p (kc n) -> p kc n", n=512), win_v[:, :, c0:c0 + 512], B_win[gi]))
        wout_v = w_out.ap().rearrange("(kc p) n -> p kc n", p=128)
        for cg in range(4):
            conv_list.append((wout_r.ap()[cg].rearrange("p (kc n) -> p kc n", n=512), wout_v[:, :, cg * 512:(cg + 1) * 512], B_wout[cg]))
        wgu_v = w_gate_up.ap().rearrange("(kc p) n -> p kc n", p=128)
        for f in range(22):
            dst = wgu_r.ap()[f].rearrange("p (s kc n) -> p s kc n", s=2, n=256)
            conv_list.append((dst[:, 0], wgu_v[:, :, f * 256:(f + 1) * 256], B_wgu[f]))
            conv_list.append((dst[:, 1], wgu_v[:, :, DFF + f * 256:DFF + (f + 1) * 256], B_wgu[f]))
        for half in range(2):
            for cg in range(4):
                for part in range(2):
                    idx = (half * 4 + cg) * 2 + part
                    r0 = half * 2816 + part * 1408
                    src = w_down.ap()[r0:r0 + 1408, cg * 512:(cg + 1) * 512].rearrange("(f p) n -> p f n", p=128)
                    conv_list.append((wdn_r.ap()[idx].rearrange("p (f n) -> p f n", n=512), src, B_wdn[idx]))
        wpg_v = w_ple_gate.ap().rearrange("(kc p) n -> p kc n", p=128)
        for cg in range(4):
            conv_list.append((wpg_r.ap()[cg].rearrange("p (kc n) -> p kc n", n=512), wpg_v[:, :, cg * 512:(cg + 1) * 512], B_wpg[cg]))
        conv_list.append((wpp_r.ap().rearrange("p (kc n) -> p kc n", n=2048), w_ple_proj.ap().rearrange("(kc p) n -> p kc n", p=128), B_wpp))
        conv_pos = [0]

        def do_conv(n):
            for _ in range(n):
                if conv_pos[0] >= len(conv_list):
                    return
                dst, src, bw = conv_list[conv_pos[0]]
                conv_pos[0] += 1
                add("pool", lambda e, dst=dst, src=src: e.dma_start(out=dst, in_=src), writes=[bw], dma="conv")

        do_conv(10)
        add("sp", lambda e: e.dma_start(out=wgates.ap, in_=wgates_r.ap().rearrange("p (kc n) -> p kc n", n=32)),
            reads=[B_wgates], writes=[wgates], dma="c2")

        add("dve", lambda e: e.tensor_reduce(out=small.ap[0:1, 0:1], in_=gq_bc.ap[0:1, :], axis=AX.X, op=ALU.max, apply_absolute_value=True),
            reads=[gq_bc], writes=[small])
        add("dve", lambda e: e.tensor_reduce(out=small.ap[0:1, 1:2], in_=gk_bc.ap[0:1, :], axis=AX.X, op=ALU.max, apply_absolute_value=True),
            reads=[gk_bc], writes=[small])
        add("dve", lambda e: e.tensor_tensor(out=small.ap[0:1, 2:3], in0=small.ap[0:1, 0:1], in1=small.ap[0:1, 1:2], op=ALU.mult),
            reads=[small], writes=[small])
        add("dve", lambda e: e.tensor_scalar(out=small.ap[0:1, 3:4], in0=small.ap[0:1, 2:3], scalar1=-float(np.sqrt(128.0)), scalar2=None, op0=ALU.mult),
            reads=[small], writes=[small])
        add("pe", lambda e: e.matmul(banks[0].ap[:, 0:1], lhsT=ones_f.ap[0:1, 0:128], rhs=small.ap[0:1, 3:4], start=True, stop=True),
            reads=[ones_f, small], writes=[banks[0]])
        add("dve", lambda e: e.tensor_copy(out=negshift.ap, in_=banks[0].ap[:, 0:1]), reads=[banks[0]], writes=[negshift])

        add("pool", lambda e: e.iota(fi.ap, pattern=[[1, 32]], base=0, channel_multiplier=0), writes=[fi])
        for q4 in range(4):
            add("dve", lambda e, q4=q4: e.tensor_copy(out=freq4.ap[:, q4 * 32:(q4 + 1) * 32], in_=fi.ap), reads=[fi, ident], writes=[freq4])
        add("act", lambda e: e.activation(out=freq4.ap, in_=freq4.ap, func=AF.Exp, scale=-float(np.log(10000.0)) / 32.0), reads=[freq4], writes=[freq4])
        add("pool", lambda e: e.memset(phase4.ap, 0.0), writes=[phase4])
        add("pool", lambda e: e.memset(phase4.ap[:, 32:64], PI / 2), writes=[phase4])
        add("pool", lambda e: e.memset(phase4.ap[:, 96:128], PI / 2), writes=[phase4])
        ang = R.view(0, [4, 128], F32)
        angi = R.view(2 * KB, [4, 128], I32)
        angm = R.view(4 * KB, [4, 128], F32)
        n_own_tiles = n_own_blocks * 4
        rope_groups = list(range(n_own_blocks)) + [8 + b for b in range(n_ctx_blocks)]
        for gi in rope_groups:
            for t in range(4):
                T = gi * 4 + t
                add("dve", lambda e, T=T, t=t: e.scalar_tensor_tensor(out=ang.ap[:, t, 0:64], in0=freq4.ap[:, 0:64], scalar=posT.ap[:, T, 0:1],
                                                                      in1=phase4.ap[:, 0:64], op0=ALU.mult, op1=ALU.add),
                    reads=[freq4, posT, phase4], writes=[ang])
                add("dve", lambda e, T=T, t=t: e.scalar_tensor_tensor(out=ang.ap[:, t, 64:128], in0=freq4.ap[:, 64:128], scalar=posT.ap[:, T, 1:2],
                                                                      in1=phase4.ap[:, 64:128], op0=ALU.mult, op1=ALU.add),
                    reads=[freq4, posT, phase4], writes=[ang])
            add("dve", lambda e: e.tensor_scalar(out=angi.ap, in0=ang.ap, scalar1=1.0 / (2 * PI), scalar2=None, op0=ALU.mult), reads=[ang], writes=[angi])
            add("dve", lambda e: e.scalar_tensor_tensor(out=ang.ap, in0=angi.ap, scalar=-2 * PI, in1=ang.ap, op0=ALU.mult, op1=ALU.add),
                reads=[angi, ang], writes=[ang])
            add("dve", lambda e: e.tensor_scalar(out=angm.ap, in0=ang.ap, scalar1=PI, scalar2=2 * PI, op0=ALU.is_gt, op1=ALU.mult), reads=[ang], writes=[angm])
            add("dve", lambda e: e.tensor_tensor(out=ang.ap, in0=ang.ap, in1=angm.ap, op=ALU.subtract), reads=[ang, angm], writes=[ang])
            add("dve", lambda e: e.tensor_scalar(out=angm.ap, in0=ang.ap, scalar1=-PI, scalar2=2 * PI, op0=ALU.is_lt, op1=ALU.mult), reads=[ang], writes=[angm])
            add("dve", lambda e: e.tensor_tensor(out=ang.ap, in0=ang.ap, in1=angm.ap, op=ALU.add), reads=[ang, angm], writes=[ang])
            add("dve", lambda e: e.tensor_scalar(out=ang.ap, in0=ang.ap, scalar1=-3.14159, scalar2=3.14159, op0=ALU.max, op1=ALU.min), reads=[ang], writes=[ang])
            add("act", lambda e: e.activation(out=ang.ap, in_=ang.ap, func=AF.Sin), reads=[ang], writes=[ang])
            add("sp", lambda e, gi=gi: e.dma_start(out=rope_d.ap()[gi * 512:(gi + 1) * 512, :].rearrange("(t p) c -> p t c", p=128), in_=ang.ap),
                reads=[ang], writes=[B_rope[gi]], dma="ropew")

        def rmsnorm_to_hT(xt, dst_hT, t, gidx, xn_slot, pbanks):
            s1, s2 = new_stat()
            add("act", lambda e: e.activation(out=xn_slot.ap, in_=xt.ap, func=AF.Square, accum_out=s1.ap), reads=[xt], writes=[xn_slot, s1])
            add("act", lambda e: e.activation(out=s2.ap, in_=s1.ap, func=AF.Ln, scale=1.0 / D, bias=EPS), reads=[s1], writes=[s2])
            add("act", lambda e: e.activation(out=s2.ap, in_=s2.ap, func=AF.Exp, scale=-0.5), reads=[s2], writes=[s2])
            add("dve", lambda e: e.tensor_scalar(out=xn_slot.ap, in0=xt.ap, scalar1=s2.ap, scalar2=None, op0=ALU.mult),
                reads=[xt, s2], writes=[xn_slot])
            for half in range(2):
                pb = pbanks[half]
                for k8 in range(8):
                    kc = half * 8 + k8
                    add("pe", lambda e, kc=kc, k8=k8, pb=pb: e.transpose(out=bank_bf(pb)[:, k8 * 128:(k8 + 1) * 128],
                                                                         in_=xn_slot.ap[:, kc * 128:(kc + 1) * 128], identity=ident.ap),
                        reads=[xn_slot, ident], writes=[banks[pb]])
                add("dve", lambda e, half=half, pb=pb: e.tensor_tensor(
                    out=dst_hT.ap[:, half * 8:(half + 1) * 8, t * 128:(t + 1) * 128],
                    in0=bank_bf(pb).rearrange("p (a b) -> p a b", a=8),
                    in1=g_fm.ap[:, gidx, half * 8:(half + 1) * 8].unsqueeze(2).to_broadcast([128, 8, 128]), op=ALU.mult),
                    reads=[banks[pb], g_fm], writes=[dst_hT])

        def qk_norm_rope(srcT, src_ap, H, g_bc, ropeT, rope_ap, outT, out_ap, tmpA, tmpB):
            src3 = src_ap.rearrange("p (h d) -> p h d", h=H)
            A3 = tmpA.ap.rearrange("p (h d) -> p h d", h=H)
            s1, s2 = new_stat(H)
            add("act", lambda e: e.activation(out=A3, in_=src3, func=AF.Square), reads=[srcT], writes=[tmpA])
            add("dve", lambda e: e.tensor_reduce(out=s2.ap, in_=A3, axis=AX.X, op=ALU.add), reads=[tmpA], writes=[s2])
            add("act", lambda e: e.activation(out=s2.ap, in_=s2.ap, func=AF.Ln, scale=1.0 / 128, bias=EPS), reads=[s2], writes=[s2])
            add("act", lambda e: e.activation(out=s2.ap, in_=s2.ap, func=AF.Exp, scale=-0.5), reads=[s2], writes=[s2])
            add("dve", lambda e: e.tensor_tensor(out=A3, in0=src3, in1=s2.ap.unsqueeze(2).to_broadcast([128, H, 128]), op=ALU.mult),
                reads=[srcT, s2], writes=[tmpA])
            add("dve", lambda e: e.tensor_tensor(out=A3, in0=A3, in1=g_bc.ap.unsqueeze(1).to_broadcast([128, H, 128]), op=ALU.mult),
                reads=[tmpA, g_bc], writes=[tmpA])
            x5 = tmpA.ap.rearrange("p (h a b f) -> p h a b f", h=H, a=2, b=2)
            t5 = tmpB.ap.rearrange("p (h a b f) -> p h a b f", h=H, a=2, b=2)
            o5 = out_ap.rearrange("p h (a b f) -> p h a b f", a=2, b=2)
            r4 = rope_ap.rearrange("p (a b f) -> p a b f", a=2, b=2)
            sin_b = r4[:, :, 0, :].unsqueeze(1).to_broadcast([128, H, 2, 32])
            cos_b = r4[:, :, 1, :].unsqueeze(1).to_broadcast([128, H, 2, 32])
            x1 = x5[:, :, :, 0, :]
            x2 = x5[:, :, :, 1, :]
            add("dve", lambda e: e.tensor_tensor(out=t5[:, :, :, 0, :], in0=x2, in1=sin_b, op=ALU.mult), reads=[tmpA, ropeT], writes=[tmpB])
            add("dve", lambda e: e.tensor_tensor(out=t5[:, :, :, 1, :], in0=x1, in1=sin_b, op=ALU.mult), reads=[tmpA, ropeT], writes=[tmpB])
            add("dve", lambda e: e.tensor_tensor(out=x1, in0=x1, in1=cos_b, op=ALU.mult), reads=[tmpA, ropeT], writes=[tmpA])
            add("dve", lambda e: e.tensor_tensor(out=x2, in0=x2, in1=cos_b, op=ALU.mult), reads=[tmpA, ropeT], writes=[tmpA])
            add("dve", lambda e: e.tensor_tensor(out=o5[:, :, :, 0, :], in0=x1, in1=t5[:, :, :, 0, :], op=ALU.subtract), reads=[tmpA, tmpB], writes=[outT])
            add("dve", lambda e: e.tensor_tensor(out=o5[:, :, :, 1, :], in0=x2, in1=t5[:, :, :, 1, :], op=ALU.add), reads=[tmpA, tmpB], writes=[outT])

        def softplus_neg(ps_tile, ps_ap, dirn, h, Ldst):
            add("act", lambda e: e.activation(out=Ldst.ap, in_=ps_ap, func=AF.Exp, scale=-1.0, bias=nbg.ap[:, dirn, h:h + 1]),
                reads=[ps_tile, nbg], writes=[Ldst])
            add("act", lambda e: e.activation(out=Ldst.ap, in_=Ldst.ap, func=AF.Ln, bias=1.0), reads=[Ldst], writes=[Ldst])

        def gates_low(hT_t, pb, glow):
            for dirn in range(2):
                for kc in range(16):
                    add("pe", lambda e, kc=kc, dirn=dirn: e.matmul(banks[pb].ap[0:16, :], lhsT=wgates.ap[:, kc, dirn * 16:(dirn + 1) * 16],
                                                                  rhs=hT_t.ap[:, kc, :], start=(kc == 0), stop=(kc == 15)),
                        reads=[wgates, hT_t], writes=[banks[pb]])
                add("act", lambda e, dirn=dirn: e.activation(out=glow.ap[:, dirn, :], in_=banks[pb].ap[0:16, :], func=AF.Copy),
                    reads=[banks[pb]], writes=[glow])

        def mm_group(pb_ap, pbT, lhs_fn, rhs_fn, n, reads):
            for kc in range(n):
                MM(pb_ap, lhs_fn(kc), rhs_fn(kc), kc == 0, kc == n - 1, reads, [pbT])

        wkv = R.view(0, [16, 512], BF16)
        xs_a = [R.view(16 * KB + i * 8 * KB, [2048], F32) for i in range(2)]
        o = 32 * KB
        ropeA = [R.view(o + i * 2 * KB, [4, 128], F32) for i in range(2)]; o += 4 * KB
        kvA = R.view(o, [256], F32); o += 1 * KB
        kvB = R.view(o, [256], F32); o += 1 * KB
        k_bf = R.view(o, [2, 128], BF16); o += 512
        v_bf = R.view(o, [2, 128], BF16); o += 512
        kT_blk = R.view(o, [2, 512], BF16); o += 2 * KB
        glowA = R.view(o, [2, 512], BF16, parts=16); o += 2 * KB
        LtD = [R.view(o + i * 2 * KB, [512], F32) for i in range(2)]; o += 4 * KB
        CtD = [R.view(o + i * 2 * KB, [512], F32) for i in range(2)]; o += 4 * KB
        kt_fm = [R.view(o + i * KB, [512], BF16) for i in range(2)]; o += 2 * KB
        kt_tm = [R.view(o + i * KB, [4, 128], BF16) for i in range(2)]; o += 2 * KB
        v_tmA = R.view(o, [4, 1024], BF16); o += 8 * KB
        Sf_st = R.view(o, [4, 256], F32); o += 4 * KB
        Sb_st = R.view(o, [4, 4, 256], F32); o += 16 * KB
        assert o <= 86 * KB, o
        smallD = [sbt("smallD%d" % i, [128, 8], F32) for i in range(2)]

        def load_w(slot_view, slotT, src_ap, srcbuf, key):
            add("sp", lambda e: e.dma_start(out=slot_view, in_=src_ap), reads=[srcbuf], writes=[slotT], dma=key)

        def win_src(gi):
            return win_r.ap()[gi].rearrange("p (kc n) -> p kc n", n=512)

        if n_ctx_blocks > 0:
            load_w(wkv.ap, wkv, win_src(2), B_win[2], "wA")
            load_w(wslot[0].ap, wslot[0], win_src(4), B_win[4], "wA")
            load_w(wslot[1].ap, wslot[1], win_src(5), B_win[5], "wA")
            load_w(wslot[2].ap, wslot[2], win_src(6), B_win[6], "wA")

        def init_states():
            add("pool", lambda e: e.memset(Sf_st.ap, 0.0), writes=[Sf_st])
            add("pool", lambda e: e.memset(Sb_st.ap, 0.0), writes=[Sb_st])
            add("pool", lambda e: e.memset(dacc.ap, 1.0), writes=[dacc])

        def spill_states(seq):
            add("sp", lambda e: e.dma_start(out=sfb_d.ap()[seq, 0].rearrange("p (h v) -> p h v", h=4), in_=Sf_st.ap),
                reads=[Sf_st], writes=[B_sfb[seq][0]], dma="spill")
            for j in range(4):
                add("sp", lambda e, j=j: e.dma_start(out=sfb_d.ap()[seq, 1 + j].rearrange("p (h v) -> p h v", h=4), in_=Sb_st.ap[:, :, j, :]),
                    reads=[Sb_st], writes=[B_sfb[seq][1 + j]], dma="spill")

        def run_interleaved(gens):
            gens = list(gens)
            while gens:
                for g_ in list(gens):
                    try:
                        next(g_)
                    except StopIteration:
                        gens.remove(g_)

        def gen_F(B):
            hT_t = hTa if B % 2 == 0 else hTb
            r0 = NOWN + B * 512
            rp = ropeA[B % 2]
            DMA("sp", rp.ap, rope_d.ap()[r0:r0 + 512, :].rearrange("(t p) c -> p t c", p=128), [B_rope[r0 // 512]], [rp], "ropeA%d" % (B % 2))
            for t in range(4):
                T = B * 4 + t
                xt = xs_a[T % 2]
                DMA("sp", xt.ap, x_ctx.ap()[T * 128:(T + 1) * 128, :], [], [xt], "xa%d" % (T % 2))
                yield
                rmsnorm_to_hT(xt, hT_t, t, 0, xn_bf[T % 2], (0, 0))
                yield
            if B == 0:
                dbg("hT0", hT_t, hT_t.ap[:, 0, :], [128, 512])

        def gen_KV(B):
            hT_t = hTa if B % 2 == 0 else hTb
            rp = ropeA[B % 2]
            for t in range(4):
                mm_group(banks[3].ap, banks[3], lambda kc: hT_t.ap[:, kc, t * 128:(t + 1) * 128], lambda kc: wkv.ap[:, kc, :], 16, [hT_t, wkv])
                yield
                ACT(v_bf.ap, banks[3].ap[:, 256:512].rearrange("p (g d) -> p g d", g=2), AF.Copy, [banks[3]], [v_bf])
                qk_norm_rope(banks[3], banks[3].ap[:, 0:256], 2, gk_bc, rp, rp.ap[:, t, :], k_bf, k_bf.ap, kvA, kvB)
                yield
                for g in range(2):
                    TR(bank_bf(0)[:, g * 128:(g + 1) * 128], k_bf.ap[:, g, :], [k_bf], [banks[0]])
                ACT(kT_blk.ap[:, :, t * 128:(t + 1) * 128], bank_bf(0)[:, 0:256].rearrange("p (g d) -> p g d", g=2), AF.Copy, [banks[0]], [kT_blk])
                chunk = B * 4 + t
                DMA("sp", v_r.ap()[:, :, chunk * 128:(chunk + 1) * 128].rearrange("g p d -> p g d"), v_bf.ap, [v_bf], [B_v[B]], "vst")
                yield
            DMA("sp", kT_ctx.ap()[:, :, B * 512:(B + 1) * 512].rearrange("g p n -> p g n"), kT_blk.ap, [kT_blk], [B_kT[B]], "kst")
            if B == 0:
                dbg("kT0", kT_blk, kT_blk.ap[:, 0, :], [128, 512])

        def gen_GKd(B, h, dirn, pk):
            pb = banks[5 + dirn]
            Lt, Ct, sd = LtD[dirn], CtD[dirn], smallD[dirn]
            mB = masks.ap[:, B, :]
            MM(pb.ap, wg_up.ap[:, dirn, h * 128:(h + 1) * 128], glowA.ap[:, dirn, :], True, True, [wg_up, glowA], [pb])
            softplus_neg(pb, pb.ap, dirn, h, Lt)
            yield
            add("dve", lambda e: e.tensor_tensor_scan(out=Ct.ap, data0=ones_f.ap, data1=Lt.ap, initial=0.0, op0=ALU.mult, op1=ALU.add),
                [ones_f, Lt], [Ct])
            TS(sd.ap[:, 0:1], Ct.ap[:, 511:512], -1.0 / 16, ALU.mult, [Ct], [sd])
            ACT(sd.ap[:, 1:2], sd.ap[:, 0:1], AF.Exp, [sd], [sd])
            yield
            if dirn == 0:
                ACT(Ct.ap, Ct.ap, AF.Exp, [Ct, sd], [Ct], scale=1.0 / 16, bias=sd.ap[:, 0:1])
            else:
                TT(Ct.ap, Ct.ap, Lt.ap, ALU.subtract, [Ct, Lt], [Ct])
                ACT(Ct.ap, Ct.ap, AF.Exp, [Ct], [Ct], scale=-1.0 / 16)
            TT(kt_fm[dirn].ap, banks[pk].ap, Ct.ap, ALU.mult, [banks[pk], Ct], [kt_fm[dirn]])
            yield
            for t in range(4):
                TR(bank_bf(5 + dirn)[:, t * 128:(t + 1) * 128], kt_fm[dirn].ap[:, t * 128:(t + 1) * 128], [kt_fm[dirn]], [pb])
            ACT(kt_tm[dirn].ap, bank_bf(5 + dirn)[:, 0:512].rearrange("p (t d) -> p t d", t=4), AF.Copy, [pb], [kt_tm[dirn]])
            yield
            kv = pb.ap[:, 256:512]
            for t in range(4):
                MM(kv, kt_tm[dirn].ap[:, t, :], v_tmA.ap[:, t, h * 256:(h + 1) * 256], t == 0, t == 3, [kt_tm[dirn], v_tmA], [pb])
            yield
            if dirn == 0:
                STT(sd.ap[:, 2:3], sd.ap[:, 1:2], mB[:, 0:1], mB[:, 1:2], ALU.mult, ALU.add, [sd, masks], [sd])
                TS(Sf_st.ap[:, h, :], Sf_st.ap[:, h, :], sd.ap[:, 2:3], ALU.mult, [Sf_st, sd], [Sf_st])
                STT(Sf_st.ap[:, h, :], kv, mB[:, 0:1], Sf_st.ap[:, h, :], ALU.mult, ALU.add, [pb, masks, Sf_st], [Sf_st])
            else:
                TT(dtmp.ap[:, h, :], dacc.ap[:, h, :], mB[:, 2:6], ALU.mult, [dacc, masks], [dtmp])
                for j in range(4):
                    STT(Sb_st.ap[:, h, j, :], kv, dtmp.ap[:, h, j:j + 1], Sb_st.ap[:, h, j, :], ALU.mult, ALU.add, [pb, dtmp, Sb_st], [Sb_st])
                    if j == 1:
                        yield
                STT(dtmp2.ap[:, h, :], mB[:, 2:6], sd.ap[:, 1:2], mB[:, 6:10], ALU.mult, ALU.add, [masks, sd], [dtmp2])
                TT(dacc.ap[:, h, :], dacc.ap[:, h, :], dtmp2.ap[:, h, :], ALU.mult, [dacc, dtmp2], [dacc])
            yield

        def gen_GV(B):
            hT_t = hTa if B % 2 == 0 else hTb
            gates_low(hT_t, 1, glowA)
            yield
            for t in range(4):
                for c2 in range(2):
                    pb = banks[1 + c2]
                    mm_group(pb.ap, pb, lambda kc: hT_t.ap[:, kc, t * 128:(t + 1) * 128], lambda kc: wslot[1 + c2].ap[:, kc, :], 16, [hT_t, wslot[1 + c2]])
                    ACT(v_tmA.ap[:, t, c2 * 512:(c2 + 1) * 512], pb.ap, AF.Copy, [pb], [v_tmA])
                    yield

        def gen_G(B):
            hT_t = hTa if B % 2 == 0 else hTb
            if B == 8:
                spill_states(0)
                init_states()
            yield
            yield
            for h in range(4):
                pk = 4 if h % 2 == 0 else 7
                mm_group(banks[pk].ap, banks[pk], lambda kc: wslot[0].ap[:, kc, h * 128:(h + 1) * 128], lambda kc: hT_t.ap[:, kc, :], 16, [hT_t, wslot[0]])
                yield
                g0 = gen_GKd(B, h, 0, pk)
                g1 = gen_GKd(B, h, 1, pk)
                live = [g0, g1]
                while live:
                    for g_ in list(live):
                        try:
                            next(g_)
                        except StopIteration:
                            live.remove(g_)
                    yield

        if n_ctx_blocks > 0:
            init_states()
            run_interleaved([gen_F(0)])
        for B in range(n_ctx_blocks):
            do_conv(3)
            gens = [gen_GV(B), gen_G(B), gen_KV(B)]
            if B + 1 < n_ctx_blocks:
                gens.append(gen_F(B + 1))
            run_interleaved(gens)
        if n_ctx_blocks > 8:
            spill_states(1)
        elif n_ctx_blocks > 0:
            spill_states(0)
        if n_ctx_blocks > 0:
            dbg("Sf_last", Sf_st, Sf_st.ap[:, 0, :], [128, 256])
            dbg("Sb_last", Sb_st, Sb_st.ap[:, 0, 0, :], [128, 256])
        do_conv(1000)

        o = 0
        qT = R.view(o, [8, 512], BF16); o += 8 * KB
        ropeO = R.view(o, [4, 128], F32); o += 2 * KB
        tA = R.view(o, [512], F32); o += 2 * KB
        tB = R.view(o, [512], F32); o += 2 * KB
        q_bf = R.view(o, [4, 128], BF16); o += 1 * KB
        glowO = R.view(o, [2, 512], BF16, parts=16); o += 2 * KB
        att_base = o
        Lg = [R.view(o + i * 2 * KB, [512], F32) for i in range(2)]; o += 4 * KB
        Cg = [R.view(o + i * 2 * KB, [512], F32) for i in range(2)]; o += 4 * KB
        E1 = R.view(o, [512], F32); o += 2 * KB
        E2 = R.view(o, [512], F32); o += 2 * KB
        qd = [R.view(o + i * 4 * KB, [4, 512], BF16) for i in range(2)]; o += 8 * KB
        kd = [R.view(o + i * 4 * KB, [4, 512], BF16) for i in range(2)]; o += 8 * KB
        v_tm = R.view(o, [4, 1024], BF16); o += 8 * KB
        gsil = R.view(o, [4, 1024], BF16); o += 8 * KB
        ktm = [R.view(o + i * KB, [4, 128], BF16) for i in range(2)]; o += 2 * KB
        Tb_bf = R.view(o, [4, 4, 256], BF16); o += 8 * KB
        Sf_bf = R.view(o, [4, 256], BF16); o += 2 * KB
        Sb_cur = R.view(o, [4, 256], F32); o += 4 * KB
        Am = [R.view(o + i * KB, [4, 128], BF16) for i in range(2)]; o += 2 * KB
        mix_bf = R.view(o, [1024], BF16); o += 2 * KB
        assert o <= 86 * KB, o
        xs_o = [T16b.view(i * 8 * KB, [2048], F32) for i in range(2)]
        o = att_base
        kts = [R.view(o + i * 2 * KB, [1024], BF16) for i in range(3)]; o += 6 * KB
        vts = [R.view(o + i * 2 * KB, [8, 128], BF16) for i in range(3)]; o += 6 * KB
        PTP = [R.view(o + i * 2 * KB, [1024], BF16) for i in range(4)]; o += 8 * KB
        accs = [R.view(o + i * 2 * KB, [512], F32) for i in range(2)]; o += 4 * KB
        rsum = R.view(o, [512], F32, parts=1); o += 2 * KB
        bcs = R.view(o, [512], F32); o += 2 * KB
        xs2 = [R.view(i * 8 * KB, [2048], F32) for i in range(4)]
        Y = [R.view(32 * KB + i * 8 * KB, [2048], F32) for i in range(4)]
        Hh = R.view(64 * KB, [22, 512], BF16)
        junk2 = R.view(64 * KB, [2048], BF16)
        p32 = R.view(64 * KB, [4, 256], F32)
        p_bf = R.view(68 * KB, [4, 256], BF16)
        pT = R.view(70 * KB, [2, 512], BF16)
        sgm = [R.view(72 * KB + i * 2 * KB, [512], F32) for i in range(2)]
        wpp_sb = R.view(76 * KB, [2, 2048], BF16)
        sgt = [XN.view(i * 2 * KB, [512], F32) for i in range(2)]
        wslot_gu = [W.view(i * 16 * KB, [2, 16, 256], BF16) for i in range(3)]
        wslot_dn = [W.view(i * 16 * KB, [11, 512], BF16) for i in range(3)]

        blk_uses = []
        for gi in (3, 4, 5, 6, 7, 8, 0, 1):
            blk_uses.append(("in", gi))
        for cg in range(4):
            blk_uses.append(("out", cg))
        for half in range(2):
            for f in range(11):
                blk_uses.append(("gu", half * 11 + f))
            for cg in range(4):
                for part in range(2):
                    blk_uses.append(("dn", (half * 4 + cg) * 2 + part))
        for cg in range(4):
            blk_uses.append(("pg", cg))
        NU = len(blk_uses)
        ws_issued = [0]
        total_uses = NU * n_own_blocks

        def ws_issue(i):
            kind, idx = blk_uses[i % NU]
            sl = i % 3
            key = "ws%d" % sl
            if kind == "in":
                DMA("sp", wslot[sl].ap, win_src(idx), [B_win[idx]], [wslot[sl]], key)
            elif kind == "out":
                DMA("sp", wslot[sl].ap, wout_r.ap()[idx].rearrange("p (kc n) -> p kc n", n=512), [B_wout[idx]], [wslot[sl]], key)
            elif kind == "gu":
                DMA("sp", wslot_gu[sl].ap, wgu_r.ap()[idx].rearrange("p (s kc n) -> p s kc n", s=2, n=256), [B_wgu[idx]], [wslot_gu[sl]], key)
            elif kind == "dn":
                DMA("sp", wslot_dn[sl].ap, wdn_r.ap()[idx].rearrange("p (f n) -> p f n", n=512), [B_wdn[idx]], [wslot_dn[sl]], key)
            elif kind == "pg":
                DMA("sp", wslot[sl].ap, wpg_r.ap()[idx].rearrange("p (kc n) -> p kc n", n=512), [B_wpg[idx]], [wslot[sl]], key)

        def ws_get(i):
            while ws_issued[0] <= i:
                ws_issue(ws_issued[0])
                ws_issued[0] += 1
            return i % 3

        QS = float(128.0 ** -0.5)

        for j in range(n_own_blocks):
            seq = j // 4
            jj = j % 4
            r0 = j * 512
            cb = 0 if seq == 0 else 32
            nkc = 32 if seq == 0 else 128
            ub = j * NU
            DMA("sp", Sb_cur.ap, sfb_d.ap()[seq, 1 + jj].rearrange("p (h v) -> p h v", h=4), [B_sfb[seq][1 + jj]], [Sb_cur], "sbl")
            if jj == 0:
                DMA("sp", Sf_own.ap, sfb_d.ap()[seq, 0].rearrange("p (h v) -> p h v", h=4), [B_sfb[seq][0]], [Sf_own], "sbl")
            DMA("sp", ropeO.ap, rope_d.ap()[r0:r0 + 512, :].rearrange("(t p) c -> p t c", p=128), [B_rope[j]], [ropeO], "ropeO")
            for t in range(4):
                xt = xs_o[t % 2]
                DMA("sp", xt.ap, x_own.ap()[r0 + t * 128:r0 + (t + 1) * 128, :], [], [xt], "xo%d" % (t % 2))
                rmsnorm_to_hT(xt, hTa, t, 0, xn_bf[t % 2], (0, 1))
            if j == 0:
                dbg("hTo", hTa, hTa.ap[:, 0, :], [128, 512])
            gates_low(hTa, 5, glowO)
            sq = ws_get(ub + 0)
            sk = ws_get(ub + 1)
            for h in range(4):
                for dirn in range(2):
                    MM(banks[4].ap, wg_up.ap[:, dirn, h * 128:(h + 1) * 128], glowO.ap[:, dirn, :], True, True, [wg_up, glowO], [banks[4]])
                    softplus_neg(banks[4], banks[4].ap, dirn, h, Lg[dirn])
                    add("dve", lambda e, dirn=dirn: e.tensor_tensor_scan(out=Cg[dirn].ap, data0=rmask.ap, data1=Lg[dirn].ap, initial=0.0,
                                                                         op0=ALU.mult, op1=ALU.add), [rmask, Lg[dirn]], [Cg[dirn]])
                mm_group(banks[2].ap, banks[2], lambda kc: wslot[sq].ap[:, kc, h * 128:(h + 1) * 128], lambda kc: hTa.ap[:, kc, :], 16, [hTa, wslot[sq]])
                mm_group(banks[3].ap, banks[3], lambda kc: wslot[sk].ap[:, kc, h * 128:(h + 1) * 128], lambda kc: hTa.ap[:, kc, :], 16, [hTa, wslot[sk]])
                ACT(E1.ap, Cg[0].ap, AF.Exp, [Cg[0]], [E1], scale=-1.0 / 16)
                ACT(E2.ap, Cg[0].ap, AF.Exp, [Cg[0]], [E2], scale=1.0 / 16)
                STT(qd[0].ap[:, h, :], banks[2].ap, QS, E1.ap, ALU.mult, ALU.mult, [banks[2], E1], [qd[0]])
                TT(kd[0].ap[:, h, :], banks[3].ap, E2.ap, ALU.mult, [banks[3], E2], [kd[0]])
                add("dve", lambda e, h=h: e.tensor_copy(out=dtl.ap[:, 0, h, :], in_=E1.ap.rearrange("p (t c) -> p t c", c=128)[:, :, 127]), [E1], [dtl])
                ACT(dtl.ap[:, 1, h, :], Cg[1].ap.rearrange("p (t c) -> p t c", c=128)[:, :, 127], AF.Exp, [Cg[1]], [dtl], scale=-1.0 / 16)
                TT(E1.ap, Cg[1].ap, Lg[1].ap, ALU.subtract, [Cg[1], Lg[1]], [E1])
                ACT(E2.ap, E1.ap, AF.Exp, [E1], [E2], scale=1.0 / 16)
                ACT(E1.ap, E1.ap, AF.Exp, [E1], [E1], scale=-1.0 / 16)
                STT(qd[1].ap[:, h, :], banks[2].ap, QS, E2.ap, ALU.mult, ALU.mult, [banks[2], E2], [qd[1]])
                TT(kd[1].ap[:, h, :], banks[3].ap, E1.ap, ALU.mult, [banks[3], E1], [kd[1]])
            sv = [ws_get(ub + 2), ws_get(ub + 3)]
            for t in range(4):
                for c2 in range(2):
                    pb = 6 + c2
                    mm_group(banks[pb].ap, banks[pb], lambda kc: hTa.ap[:, kc, t * 128:(t + 1) * 128], lambda kc: wslot[sv[c2]].ap[:, kc, :], 16,
                             [hTa, wslot[sv[c2]]])
                    ACT(v_tm.ap[:, t, c2 * 512:(c2 + 1) * 512], banks[pb].ap, AF.Copy, [banks[pb]], [v_tm])
            sg_ = [ws_get(ub + 4), ws_get(ub + 5)]
            for t in range(4):
                for c2 in range(2):
                    pb = 6 + c2
                    mm_group(banks[pb].ap, banks[pb], lambda kc: hTa.ap[:, kc, t * 128:(t + 1) * 128], lambda kc: wslot[sg_[c2]].ap[:, kc, :], 16,
                             [hTa, wslot[sg_[c2]]])
                    ACT(tA.ap, banks[pb].ap, AF.Silu, [banks[pb]], [tA])
                    TT(gsil.ap[:, t, c2 * 512:(c2 + 1) * 512].rearrange("p (a b) -> p a b", a=2), tA.ap.rearrange("p (a b) -> p a b", a=2),
                       ggla_bc.ap.unsqueeze(1).to_broadcast([128, 2, 256]), ALU.mult, [tA, ggla_bc], [gsil])
            for t in (3, 2, 1, 0):
                kt = ktm[t % 2]
                for h in range(4):
                    TR(bank_bf(5)[:, h * 128:(h + 1) * 128], kd[1].ap[:, h, t * 128:(t + 1) * 128], [kd[1]], [banks[5]])
                ACT(kt.ap, bank_bf(5)[:, 0:512].rearrange("p (a b) -> p a b", a=4), AF.Copy, [banks[5]], [kt])
                for h in range(4):
                    pbk = banks[6 + h // 2]
                    MM(pbk.ap[:, (h % 2) * 256:(h % 2 + 1) * 256], kt.ap[:, h, :], v_tm.ap[:, t, h * 256:(h + 1) * 256], True, True, [kt, v_tm], [pbk])
                for h in range(4):
                    pbk = banks[6 + h // 2]
                    ACT(Tb_bf.ap[:, t, h, :], Sb_cur.ap[:, h, :], AF.Copy, [Sb_cur, dtl], [Tb_bf], scale=dtl.ap[:, 1, h, t:t + 1])
                    STT(Sb_cur.ap[:, h, :], Sb_cur.ap[:, h, :], dtl.ap[:, 1, h, t:t + 1], pbk.ap[:, (h % 2) * 256:(h % 2 + 1) * 256], ALU.mult, ALU.add,
                        [Sb_cur, dtl, pbk], [Sb_cur])
            for t in range(4):
                kt = ktm[t % 2]
                am = Am[t % 2]
                for h in range(4):
                    TR(bank_bf(5)[:, h * 128:(h + 1) * 128], kd[0].ap[:, h, t * 128:(t + 1) * 128], [kd[0]], [banks[5]])
                ACT(kt.ap, bank_bf(5)[:, 0:512].rearrange("p (a b) -> p a b", a=4), AF.Copy, [banks[5]], [kt])
                for h in range(4):
                    pbk = banks[6 + h // 2]
                    MM(pbk.ap[:, (h % 2) * 256:(h % 2 + 1) * 256], kt.ap[:, h, :], v_tm.ap[:, t, h * 256:(h + 1) * 256], True, True, [kt, v_tm], [pbk])
                ACT(Sf_bf.ap, Sf_own.ap, AF.Copy, [Sf_own], [Sf_bf])
                for h in range(4):
                    MM(banks[2].ap[:, h * 128:(h + 1) * 128], kd[0].ap[:, h, t * 128:(t + 1) * 128], qd[0].ap[:, h, t * 128:(t + 1) * 128], True, True,
                       [kd[0], qd[0]], [banks[2]])
                    MM(banks[3].ap[:, h * 128:(h + 1) * 128], kd[1].ap[:, h, t * 128:(t + 1) * 128], qd[1].ap[:, h, t * 128:(t + 1) * 128], True, True,
                       [kd[1], qd[1]], [banks[3]])
                TT(tA.ap.rearrange("p (a b) -> p a b", a=4), banks[2].ap.rearrange("p (a b) -> p a b", a=4),
                   maskf.ap.unsqueeze(1).to_broadcast([128, 4, 128]), ALU.mult, [banks[2], maskf], [tA])
                TT(tB.ap.rearrange("p (a b) -> p a b", a=4), banks[3].ap.rearrange("p (a b) -> p a b", a=4),
                   maskb.ap.unsqueeze(1).to_broadcast([128, 4, 128]), ALU.mult, [banks[3], maskb], [tB])
                TT(am.ap.rearrange("p a b -> p (a b)"), tA.ap, tB.ap, ALU.add, [tA, tB], [am])
                for h in range(4):
                    pbo = banks[h // 2]
                    oap = pbo.ap[:, (h % 2) * 256:(h % 2 + 1) * 256]
                    MM(oap, am.ap[:, h, :], v_tm.ap[:, t, h * 256:(h + 1) * 256], True, False, [am, v_tm], [pbo])
                    MM(oap, qd[0].ap[:, h, t * 128:(t + 1) * 128], Sf_bf.ap[:, h, :], False, False, [qd[0], Sf_bf], [pbo])
                    MM(oap, qd[1].ap[:, h, t * 128:(t + 1) * 128], Tb_bf.ap[:, t, h, :], False, True, [qd[1], Tb_bf], [pbo])
                for h in range(4):
                    pbk = banks[6 + h // 2]
                    TT(Sf_own.ap[:, h, :], Sf_own.ap[:, h, :], pbk.ap[:, (h % 2) * 256:(h % 2 + 1) * 256], ALU.add, [Sf_own, pbk], [Sf_own])
                    TS(Sf_own.ap[:, h, :], Sf_own.ap[:, h, :], dtl.ap[:, 0, h, t:t + 1], ALU.mult, [Sf_own, dtl], [Sf_own])
                s1, s2 = new_stat(4)
                for h in range(4):
                    pbo = banks[h // 2]
                    oap = pbo.ap[:, (h % 2) * 256:(h % 2 + 1) * 256]
                    ACT(tB.ap[:, 0:256], oap, AF.Square, [pbo], [tB, s1], accum=s1.ap[:, h:h + 1])
                ACT(s2.ap, s1.ap, AF.Ln, [s1], [s2], scale=1.0 / 256, bias=EPS)
                ACT(s2.ap, s2.ap, AF.Exp, [s2], [s2], scale=-0.5)
                for h in range(4):
                    pbo = banks[h // 2]
                    oap = pbo.ap[:, (h % 2) * 256:(h % 2 + 1) * 256]
                    STT(mix_bf.ap[:, h * 256:(h + 1) * 256], oap, s2.ap[:, h:h + 1], gsil.ap[:, t, h * 256:(h + 1) * 256], ALU.mult, ALU.mult,
                        [pbo, s2, gsil], [mix_bf])
                if j == 0 and t == 0:
                    dbg("gla0", mix_bf, mix_bf.ap[:, 0:512], [128, 512])
                for c in range(8):
                    TR(bank_bf(4)[:, c * 128:(c + 1) * 128], mix_bf.ap[:, c * 128:(c + 1) * 128], [mix_bf], [banks[4]])
                ACT(hTb.ap[:, 8:16, t * 128:(t + 1) * 128], bank_bf(4).rearrange("p (a b) -> p a b", a=8), AF.Copy, [banks[4]], [hTb])
            sa = [ws_get(ub + 6), ws_get(ub + 7)]
            for t in range(4):
                for cg in range(2):
                    pb = 2 + cg
                    mm_group(banks[pb].ap, banks[pb], lambda kc: hTa.ap[:, kc, t * 128:(t + 1) * 128], lambda kc: wslot[sa[cg]].ap[:, kc, :], 16,
                             [hTa, wslot[sa[cg]]])
                    qk_norm_rope(banks[pb], banks[pb].ap, 4, gq_bc, ropeO, ropeO.ap[:, t, :], q_bf, q_bf.ap, tA, tB)
                    for h4 in range(4):
                        TR(bank_bf(5)[:, h4 * 128:(h4 + 1) * 128], q_bf.ap[:, h4, :], [q_bf], [banks[5]])
                    ACT(qT.ap[:, cg * 4:(cg + 1) * 4, t * 128:(t + 1) * 128], bank_bf(5)[:, 0:512].rearrange("p (a b) -> p a b", a=4), AF.Copy,
                        [banks[5]], [qT])
            if j == 0:
                dbg("qT0", qT, qT.ap[:, 0, :], [128, 512])
            ngrp = nkc // 8
            kv_n = [0]
            for g in range(2):
                for pair in range(2):
                    heads = (4 * g + 2 * pair, 4 * g + 2 * pair + 1)
                    niter = ngrp * 8
                    slot_of = {}

                    def stage_S(n):
                        G, c = n // 8, n % 8
                        if c == 0:
                            sl = kv_n[0] % 3
                            kv_n[0] += 1
                            slot_of[G] = sl
                            c0 = cb + G * 8
                            blks = [B_kT[c0 // 4], B_kT[c0 // 4 + 1], B_v[c0 // 4], B_v[c0 // 4 + 1]]
                            DMA("sp", kts[sl].ap, kT_ctx.ap()[g, :, c0 * 128:(c0 + 8) * 128], blks, [kts[sl]], "kv%d" % sl)
                            DMA("sp", vts[sl].ap, v_r.ap()[g, :, c0 * 128:(c0 + 8) * 128].rearrange("p (c d) -> p c d", d=128), blks, [vts[sl]],
                                "kv%d" % sl)
                        sl = slot_of[G]
                        k3 = 1 + n % 3
                        db = dbanks[k3]
                        for i in range(2):
                            MM(db.ap[:, i * 512:(i + 1) * 512], kts[sl].ap[:, c * 128:(c + 1) * 128], qT.ap[:, heads[i], :], True, True,
                               [kts[sl], qT], [banks[2 * k3 + i]])

                    def stage_EV(n):
                        G, c = n // 8, n % 8
                        sl = slot_of[G]
                        db = dbanks[1 + n % 3]
                        pt = PTP[n % 4]
                        ACT(pt.ap, db.ap, AF.Exp, [db, negshift], [pt], scale=QS, bias=negshift.ap)
                        first = (n == 0)
                        last = (n == niter - 1)
                        for i in range(2):
                            MM(banks[i].ap, vts[sl].ap[:, c, :], pt.ap[:, i * 512:(i + 1) * 512], first, last, [vts[sl], pt], [banks[i]])
                        for i in range(2):
                            eng = "dve" if i == 0 else "pool"
                            if first:
                                add(eng, lambda e, i=i, pt=pt: e.tensor_copy(out=accs[i].ap, in_=pt.ap[:, i * 512:(i + 1) * 512]), [pt], [accs[i]])
                            else:
                                TT(accs[i].ap, accs[i].ap, pt.ap[:, i * 512:(i + 1) * 512], ALU.add, [accs[i], pt], [accs[i]], eng=eng)

                    stage_S(0)
                    if niter > 1:
                        stage_S(1)
                    for n in range(niter):
                        if n + 2 < niter:
                            stage_S(n + 2)
                        stage_EV(n)
                    for i in range(2):
                        hq = heads[i]
                        MM(banks[2 + i].ap[0:1, :], ones_f.ap[:, 0:1], accs[i].ap, True, True, [ones_f, accs[i]], [banks[2 + i]])
                        ACT(rsum.ap, banks[2 + i].ap[0:1, :], AF.Ln, [banks[2 + i]], [rsum])
                        ACT(rsum.ap, rsum.ap, AF.Exp, [rsum], [rsum], scale=-1.0)
                        MM(banks[4 + i].ap, ones_f.ap[0:1, 0:128], rsum.ap, True, True, [ones_f, rsum], [banks[4 + i]])
                        ACT(bcs.ap, banks[4 + i].ap, AF.Copy, [banks[4 + i]], [bcs])
                        TT(hTb.ap[:, hq, :], banks[i].ap, bcs.ap, ALU.mult, [banks[i], bcs], [hTb])
            if j == 0:
                dbg("mixT_a", hTb, hTb.ap[:, 0, :], [128, 512])
                dbg("mixT_g", hTb, hTb.ap[:, 8, :], [128, 512])
            for t in range(4):
                DMA("sp", xs2[t].ap, x_own.ap()[r0 + t * 128:r0 + (t + 1) * 128, :], [], [xs2[t]], "xs2_%d" % t)
            n = 0
            for cg in range(4):
                sl = ws_get(ub + 8 + cg)
                for t in range(4):
                    pb = banks[n % 8]
                    n += 1
                    mm_group(pb.ap, pb, lambda kc: hTb.ap[:, kc, t * 128:(t + 1) * 128], lambda kc: wslot[sl].ap[:, kc, :], 16, [hTb, wslot[sl]])
                    ACT(Y[t].ap[:, cg * 512:(cg + 1) * 512], pb.ap, AF.Copy, [pb], [Y[t]])

            def post_norm(gsrc):
                DMA("sp", gtmp.ap, gsrc.ap().partition_broadcast(128)[:, 0, :], [], [gtmp], "gtmp")
                for t in range(4):
                    s1, s2 = new_stat()
                    ACT(junk2.ap, Y[t].ap, AF.Square, [Y[t]], [junk2, s1], accum=s1.ap)
                    ACT(s2.ap, s1.ap, AF.Ln, [s1], [s2], scale=1.0 / D, bias=EPS)
                    ACT(s2.ap, s2.ap, AF.Exp, [s2], [s2], scale=-0.5)
                    STT(Y[t].ap, Y[t].ap, s2.ap, gtmp.ap, ALU.mult, ALU.mult, [Y[t], s2, gtmp], [Y[t]])
                    TT(xs2[t].ap, xs2[t].ap, Y[t].ap, ALU.add, [xs2[t], Y[t]], [xs2[t]])

            post_norm(g_post_mix)
            if j == 0:
                dbg("x1", xs2[0], xs2[0].ap[:, 0:512], [128, 512])
            for t in range(4):
                rmsnorm_to_hT(xs2[t], hTa, t, 1, xn_bf[t % 2], (0, 1))
            ui = ub + 12
            nff = 0
            for half in range(2):
                for f in range(11):
                    sl = ws_get(ui)
                    ui += 1
                    for c in range(2):
                        pg_ = banks[(nff % 2) * 2]
                        pu_ = banks[(nff % 2) * 2 + 1]
                        st_ = sgt[nff % 2]
                        nff += 1
                        mm_group(pg_.ap, pg_, lambda kc: wslot_gu[sl].ap[:, 0, kc, c * 128:(c + 1) * 128], lambda kc: hTa.ap[:, kc, :], 16,
                                 [hTa, wslot_gu[sl]])
                        mm_group(pu_.ap, pu_, lambda kc: wslot_gu[sl].ap[:, 1, kc, c * 128:(c + 1) * 128], lambda kc: hTa.ap[:, kc, :], 16,
                                 [hTa, wslot_gu[sl]])
                        ACT(st_.ap, pg_.ap, AF.Silu, [pg_], [st_])
                        TT(Hh.ap[:, f * 2 + c, :], st_.ap, pu_.ap, ALU.mult, [st_, pu_], [Hh])
                for cg in range(4):
                    for part in range(2):
                        sl = ws_get(ui)
                        ui += 1
                        for t in range(4):
                            pd = banks[4 + t]
                            for ffc in range(11):
                                MM(pd.ap, Hh.ap[:, part * 11 + ffc, t * 128:(t + 1) * 128], wslot_dn[sl].ap[:, ffc, :],
                                   part == 0 and ffc == 0, part == 1 and ffc == 10, [Hh, wslot_dn[sl]], [pd])
                    for t in range(4):
                        pd = banks[4 + t]
                        if half == 0:
                            ACT(Y[t].ap[:, cg * 512:(cg + 1) * 512], pd.ap, AF.Copy, [pd], [Y[t]])
                        else:
                            TT(Y[t].ap[:, cg * 512:(cg + 1) * 512], pd.ap, Y[t].ap[:, cg * 512:(cg + 1) * 512], ALU.add, [pd, Y[t]], [Y[t]])
            post_norm(g_post_ffn)
            if j == 0:
                dbg("x2", xs2[0], xs2[0].ap[:, 0:512], [128, 512])
            for t in range(4):
                rmsnorm_to_hT(xs2[t], hTa, t, 2, xn_bf[t % 2], (0, 1))
            DMA("sp", p32.ap, p_own.ap()[r0:r0 + 512, :].rearrange("(t p) c -> p t c", p=128), [], [p32], "p32")
            DMA("sp", wpp_sb.ap, wpp_r.ap().rearrange("p (kc n) -> p kc n", n=2048), [B_wpp], [wpp_sb], "wpp")
            ACT(p_bf.ap, p32.ap, AF.Copy, [p32], [p_bf])
            for t in range(4):
                for c in range(2):
                    TR(bank_bf(0)[:, c * 128:(c + 1) * 128], p_bf.ap[:, t, c * 128:(c + 1) * 128], [p_bf], [banks[0]])
                ACT(pT.ap[:, :, t * 128:(t + 1) * 128], bank_bf(0)[:, 0:256].rearrange("p (a b) -> p a b", a=2), AF.Copy, [banks[0]], [pT])
            n = 0
            for cg in range(4):
                sl = ws_get(ub + 50 + cg)
                for t in range(4):
                    pgb = banks[n % 4]
                    ppb = banks[4 + n % 4]
                    sg2 = sgm[n % 2]
                    n += 1
                    mm_group(pgb.ap, pgb, lambda kc: hTa.ap[:, kc, t * 128:(t + 1) * 128], lambda kc: wslot[sl].ap[:, kc, :], 16, [hTa, wslot[sl]])
                    mm_group(ppb.ap, ppb, lambda kc: pT.ap[:, kc, t * 128:(t + 1) * 128], lambda kc: wpp_sb.ap[:, kc, cg * 512:(cg + 1) * 512], 2,
                             [pT, wpp_sb])
                    ACT(sg2.ap, pgb.ap, AF.Sigmoid, [pgb], [sg2])
                    TT(Y[t].ap[:, cg * 512:(cg + 1) * 512], sg2.ap, ppb.ap, ALU.mult, [sg2, ppb], [Y[t]])
            post_norm(g_ple_post)
            for t in range(4):
                DMA("sp", y_own.ap()[r0 + t * 128:r0 + (t + 1) * 128, :], xs2[t].ap, [xs2[t]], [B_yout], "yout")

        S.final.append("yout")
        S.emit(nc, es)
    return nc, dbg_outs


_PROG = {}


def _core_inputs(c, inp):
    p, hf = c // 2, c % 2
    xp = inp["x_prompt"]
    xsm = inp["x_sample"]
    pp = inp["p_prompt"][0]
    psm = inp["p_sample"][0]
    d = {}
    d["x_own"] = np.ascontiguousarray(np.concatenate([xp[p, hf * 2048:(hf + 1) * 2048], xsm[0, c * 2048:(c + 1) * 2048]], 0), dtype=np.float32)
    d["p_own"] = np.ascontiguousarray(np.concatenate([pp[p, hf * 2048:(hf + 1) * 2048], psm[0, c * 2048:(c + 1) * 2048]], 0), dtype=np.float32)
    d["x_ctx"] = np.ascontiguousarray(np.concatenate([xp[p], xsm[0]], 0), dtype=np.float32)
    tok = np.concatenate([hf * 2048 + np.arange(2048), c * 2048 + np.arange(2048), np.arange(4096), np.arange(16384)])
    d["pos_all"] = np.stack([tok // 64, tok % 64], 1).astype(np.float32)
    m = np.zeros((40, 10), np.float32)
    for B in range(40):
        if B < 8:
            b, start = B, hf * 4
        else:
            b, start = B - 8, c * 4
        mf = 1.0 if b < start else 0.0
        m[B, 0] = mf
        m[B, 1] = 1.0 - mf
        for j in range(4):
            mb = 1.0 if b > start + j else 0.0
            m[B, 2 + j] = mb
            m[B, 6 + j] = 1.0 - mb
    d["masks"] = m.reshape(1, 400)
    for k in ("g_pre_mix", "w_in", "g_q", "g_k", "w_gf_up", "b_gf", "w_gb_up", "b_gb", "g_gla_norm", "w_out", "g_post_mix", "g_pre_ffn",
              "w_gate_up", "w_down", "g_post_ffn", "g_ple_pre", "w_ple_gate", "w_ple_proj", "g_ple_post"):
        a = np.asarray(inp[k], dtype=np.float32)[0]
        if a.ndim == 1:
            a = a[None, :]
        d[k] = np.ascontiguousarray(a)
    return d


def kernel(**inputs):
    inp = {k: np.asarray(v) for k, v in inputs.items()}
    if "nc" not in _PROG:
        _PROG["nc"] = build_program()[0]
    nc = _PROG["nc"]
    in_maps = [_core_inputs(c, inp) for c in range(8)]
    res = run_bass_kernel_spmd(nc, in_maps, core_ids=list(range(8)))
    y_prompt = np.zeros((4, 4096, D), np.float32)
    y_sample = np.zeros((1, 16384, D), np.float32)
    for c in range(8):
        y = np.asarray(res.results[c]["y_own"], dtype=np.float32)
        y_prompt[c // 2, (c % 2) * 2048:(c % 2 + 1) * 2048] = y[:2048]
        y_sample[0, c * 2048:(c + 1) * 2048] = y[2048:]
    return (y_prompt, y_sample)
```

```python
import os
import numpy as np
from contextlib import ExitStack
import concourse.bass as bass
import concourse.mybir as mybir
from concourse.bass_utils import run_bass_kernel_spmd

F32 = mybir.dt.float32
BF16 = mybir.dt.bfloat16
I32 = mybir.dt.int32
AF = mybir.ActivationFunctionType
ALU = mybir.AluOpType
AX = mybir.AxisListType

D = 2048
DIN = 4640
DFF = 5632
DPLE = 256
NOWN = 4096
NCTX = 20480
EPS = 1e-6
PI = float(np.pi)


class Buf:
    __slots__ = ("name", "lw", "rd")

    def __init__(self, name):
        self.name = name
        self.lw = None
        self.rd = {}


class Tile:
    __slots__ = ("ap", "bufs")

    def __init__(self, ap, bufs):
        self.ap = ap
        self.bufs = bufs


class Op:
    __slots__ = ("eng", "fn", "waits", "signal", "semkey", "pos", "semval", "isdma")


def _flat(lst):
    out = []
    for x in lst:
        if x is None:
            continue
        if isinstance(x, Buf):
            out.append(x)
        elif isinstance(x, Tile):
            out.extend(x.bufs)
        else:
            out.extend(_flat(x))
    return out


class Sched:
    ENGS = ("pe", "dve", "act", "pool", "sp")

    def __init__(self):
        self.streams = {e: [] for e in self.ENGS}
        self.dma_count = {}
        self.seen = {e: {} for e in self.ENGS}
        self.final = []

    def add(self, eng, fn, reads=(), writes=(), dma=None):
        reads = _flat(reads)
        writes = _flat(writes)
        op = Op()
        op.eng = eng
        op.fn = fn
        op.signal = False
        op.isdma = dma is not None
        op.waits = []
        op.semval = None
        stream = self.streams[eng]
        deps = {}
        seen = self.seen[eng]
        pe_compute = (eng == "pe") and not op.isdma

        def need(d):
            if d is None:
                return
            if pe_compute and (not d.isdma) and d.eng == "pe":
                return
            k = d.semkey
            p = self.dma_count[k[1]] if d.isdma else d.pos
            if p <= seen.get(k, -1):
                return
            cur = deps.get(k)
            if cur is None or p > cur[0]:
                deps[k] = (p, d)
            elif (not d.isdma) and d.pos > cur[1].pos:
                deps[k] = (p, d)

        for b in reads:
            need(b.lw)
        for b in writes:
            need(b.lw)
            for r in b.rd.values():
                need(r)
        for k, (p, d) in deps.items():
            if d.isdma:
                op.waits.append((k[1], p * 16))
            else:
                d.signal = True
                op.waits.append(d)
            seen[k] = p
        if op.isdma:
            n = self.dma_count.get(dma, 0) + 1
            self.dma_count[dma] = n
            op.semkey = ("dma", dma)
            op.pos = n
        else:
            op.semkey = ("eng", eng)
            op.pos = len(stream)
        stream.append(op)
        for b in writes:
            b.lw = op
            b.rd = {}
        for b in reads:
            b.rd[op.semkey] = op
        return op

    def emit(self, nc, es):
        engsem = {e: es.enter_context(nc.semaphore("s_" + e)) for e in ("pe", "dve", "act", "pool")}
        dmasem = {k: es.enter_context(nc.semaphore("d_" + k)) for k in self.dma_count}
        for e, stream in self.streams.items():
            c = 0
            for op in stream:
                if (not op.isdma) and op.signal:
                    c += 1
                    op.semval = c
        block = es.enter_context(nc.Block())
        streams = self.streams
        final_waits = [(k, self.dma_count[k] * 16) for k in self.final if k in self.dma_count]

        def run(engname, eng):
            for op in streams[engname]:
                for w in op.waits:
                    if isinstance(w, Op):
                        eng.wait_ge(engsem[w.eng], w.semval)
                    else:
                        eng.wait_ge(dmasem[w[0]], w[1])
                inst = op.fn(eng)
                if op.isdma:
                    inst.then_inc(dmasem[op.semkey[1]], 16)
                elif op.signal:
                    inst.then_inc(engsem[engname], 1)

        @block.tensor
        def _(e):
            run("pe", e)

        @block.vector
        def _(e):
            run("dve", e)

        @block.scalar
        def _(e):
            run("act", e)

        @block.gpsimd
        def _(e):
            run("pool", e)

        @block.sync
        def _(e):
            run("sp", e)
            for k, v in final_waits:
                e.wait_ge(dmasem[k], v)


GRAN = 2048


class Arena:
    def __init__(self, nc, es, name, nbytes):
        assert nbytes % 4 == 0
        self.nbytes = nbytes
        self.t = es.enter_context(nc.sbuf_tensor(name, [128, nbytes // 4], F32))
        self.gr = [Buf("%s_%d" % (name, i)) for i in range((nbytes + GRAN - 1) // GRAN)]

    def view(self, off, shape, dt, parts=128):
        esz = 2 if dt == BF16 else 4
        n = 1
        for s in shape:
            n *= s
        nb = n * esz
        assert off % 4 == 0 and nb % 4 == 0 and off + nb <= self.nbytes, (off, nb, self.nbytes)
        ap = self.t[0:parts, off // 4:(off + nb) // 4]
        if dt != F32:
            ap = ap.bitcast(dt)
        if len(shape) == 2:
            ap = ap.rearrange("p (a b) -> p a b", a=shape[0])
        elif len(shape) == 3:
            ap = ap.rearrange("p (a b c) -> p a b c", a=shape[0], b=shape[1])
        elif len(shape) == 4:
            ap = ap.rearrange("p (a b c d) -> p a b c d", a=shape[0], b=shape[1], c=shape[2])
        return Tile(ap, self.gr[off // GRAN:(off + nb - 1) // GRAN + 1])


KB = 1024
IN_GROUP_COLS = [0, 512, 1024, 1536, 2048, 2560, 3072, 3584, 4096]


def build_program(debug=(), n_ctx_blocks=40, n_own_blocks=8):
    nc = bass.Bass("TRN2", target_bir_lowering=False)
    S = Sched()
    dbg_outs = {}

    def din(name, shape):
        return nc.dram_tensor(name, list(shape), F32, kind="ExternalInput")

    x_own = din("x_own", [NOWN, D])
    p_own = din("p_own", [NOWN, DPLE])
    x_ctx = din("x_ctx", [NCTX, D])
    pos_all = din("pos_all", [NOWN + NCTX, 2])
    masks_d = din("masks", [1, 400])
    g_pre_mix = din("g_pre_mix", [1, D])
    w_in = din("w_in", [D, DIN])
    g_q = din("g_q", [1, 128])
    g_k = din("g_k", [1, 128])
    w_gf_up = din("w_gf_up", [16, 512])
    b_gf = din("b_gf", [1, 512])
    w_gb_up = din("w_gb_up", [16, 512])
    b_gb = din("b_gb", [1, 512])
    g_gla_norm = din("g_gla_norm", [1, 256])
    w_out = din("w_out", [D, D])
    g_post_mix = din("g_post_mix", [1, D])
    g_pre_ffn = din("g_pre_ffn", [1, D])
    w_gate_up = din("w_gate_up", [D, 2 * DFF])
    w_down = din("w_down", [DFF, D])
    g_post_ffn = din("g_post_ffn", [1, D])
    g_ple_pre = din("g_ple_pre", [1, D])
    w_ple_gate = din("w_ple_gate", [D, D])
    w_ple_proj = din("w_ple_proj", [DPLE, D])
    g_ple_post = din("g_ple_post", [1, D])
    y_own = nc.dram_tensor("y_own", [NOWN, D], F32, kind="ExternalOutput")

    def dscr(name, shape, dt):
        return nc.dram_tensor(name, list(shape), dt, kind="Internal")

    win_r = dscr("win_r", [9, 128, 16 * 512], BF16)
    wgates_r = dscr("wgates_r", [128, 16 * 32], BF16)
    wout_r = dscr("wout_r", [4, 128, 16 * 512], BF16)
    wgu_r = dscr("wgu_r", [22, 128, 2 * 16 * 256], BF16)
    wdn_r = dscr("wdn_r", [16, 128, 11 * 512], BF16)
    wpg_r = dscr("wpg_r", [4, 128, 16 * 512], BF16)
    wpp_r = dscr("wpp_r", [128, 2 * 2048], BF16)
    kT_ctx = dscr("kT_ctx", [2, 128, NCTX], BF16)
    v_r = dscr("v_r", [2, 128, 160 * 128], BF16)
    rope_d = dscr("rope_d", [NOWN + NCTX, 128], F32)
    sfb_d = dscr("sfb_d", [2, 5, 128, 1024], F32)

    B_win = [Buf("win%d" % i) for i in range(9)]
    B_wgates = Buf("wgates_r")
    B_wout = [Buf("wout%d" % i) for i in range(4)]
    B_wgu = [Buf("wgu%d" % i) for i in range(22)]
    B_wdn = [Buf("wdn%d" % i) for i in range(16)]
    B_wpg = [Buf("wpg%d" % i) for i in range(4)]
    B_wpp = Buf("wpp")
    B_kT = [Buf("kTctx%d" % i) for i in range(40)]
    B_v = [Buf("vctx%d" % i) for i in range(40)]
    B_rope = [Buf("rope%d" % i) for i in range(48)]
    B_sfb = [[Buf("sfb%d_%d" % (s, j)) for j in range(5)] for s in range(2)]
    B_yout = Buf("yout")

    es = ExitStack()
    with es:
        def sbt(name, shape, dt):
            t = es.enter_context(nc.sbuf_tensor(name, list(shape), dt))
            return Tile(t[:], [Buf(name)])

        add = S.add

        def TT(out, in0, in1, op, r, w, eng="dve"):
            add(eng, lambda e: e.tensor_tensor(out=out, in0=in0, in1=in1, op=op), r, w)

        def TS(out, in0, s1, op0, r, w, s2=None, op1=None, eng="dve"):
            if op1 is None:
                add(eng, lambda e: e.tensor_scalar(out=out, in0=in0, scalar1=s1, scalar2=None, op0=op0), r, w)
            else:
                add(eng, lambda e: e.tensor_scalar(out=out, in0=in0, scalar1=s1, scalar2=s2, op0=op0, op1=op1), r, w)

        def STT(out, in0, scalar, in1, op0, op1, r, w):
            add("dve", lambda e: e.scalar_tensor_tensor(out=out, in0=in0, scalar=scalar, in1=in1, op0=op0, op1=op1), r, w)

        def ACT(out, in_, func, r, w, scale=None, bias=None, accum=None):
            kw = {}
            if scale is not None:
                kw["scale"] = scale
            if bias is not None:
                kw["bias"] = bias
            if accum is not None:
                kw["accum_out"] = accum
            add("act", lambda e: e.activation(out=out, in_=in_, func=func, **kw), r, w)

        def MM(out, lhsT, rhs, start, stop, r, w):
            add("pe", lambda e: e.matmul(out, lhsT=lhsT, rhs=rhs, start=start, stop=stop), r, w)

        def TR(out, in_, r, w):
            add("pe", lambda e: e.transpose(out=out, in_=in_, identity=ident.ap), list(r) + [ident], w)

        def DMA(q, out, in_, r, w, key, slow=False):
            if slow:
                add(q, lambda e: e.dma_start(out=out, in_=in_, allow_slow_non_contiguous=True), r, w, key)
            else:
                add(q, lambda e: e.dma_start(out=out, in_=in_), r, w, key)

        ident = sbt("ident", [128, 128], BF16)
        ones_f = sbt("ones_f", [128, 512], F32)
        ones_b = sbt("ones_b", [128, 128], BF16)
        maskf = sbt("maskf", [128, 128], F32)
        maskb = sbt("maskb", [128, 128], F32)
        rmask = sbt("rmask", [128, 512], F32)
        g_fm = sbt("g_fm", [128, 3, 16], F32)
        gq_bc = sbt("gq_bc", [128, 128], F32)
        gk_bc = sbt("gk_bc", [128, 128], F32)
        ggla_bc = sbt("ggla_bc", [128, 256], F32)
        nbg = sbt("nbg", [128, 2, 4], F32)
        wg_up = sbt("wg_up", [16, 2, 512], BF16)
        masks = sbt("masks_sb", [128, 40, 10], F32)
        negshift = sbt("negshift", [128, 1], F32)
        wgates = sbt("wgates", [128, 16, 32], BF16)
        stat_t = es.enter_context(nc.sbuf_tensor("stat", [128, 1104], F32))
        stat2_t = es.enter_context(nc.sbuf_tensor("stat2", [128, 1104], F32))
        small = sbt("small", [128, 64], F32)
        posT = sbt("posT", [128, 192, 2], F32)
        freq4 = sbt("freq4", [128, 128], F32)
        phase4 = sbt("phase4", [128, 128], F32)
        fi = sbt("fi", [128, 32], I32)
        dacc = sbt("dacc", [128, 4, 4], F32)
        dtmp = sbt("dtmp", [128, 4, 4], F32)
        dtmp2 = sbt("dtmp2", [128, 4, 4], F32)
        Sf_own = sbt("Sf_own", [128, 4, 256], F32)
        dtl = sbt("dtl", [128, 2, 4, 4], F32)

        T16a = Arena(nc, es, "T16a", 16 * KB)
        T16b = Arena(nc, es, "T16b", 16 * KB)
        W = Arena(nc, es, "W", 48 * KB)
        R = Arena(nc, es, "R", 86 * KB)
        XN = Arena(nc, es, "XN", 8 * KB)

        hTa = T16a.view(0, [16, 512], BF16)
        hTb = T16b.view(0, [16, 512], BF16)
        wslot = [W.view(i * 16 * KB, [16, 512], BF16) for i in range(3)]
        xn_bf = [XN.view(i * 4 * KB, [2048], BF16) for i in range(2)]
        gtmp = XN.view(0, [2048], F32)

        banks = []
        dbanks = []
        for k in range(4):
            t = es.enter_context(nc.psum_tensor("dbank%d" % k, [128, 1024], F32))
            b0, b1 = Buf("bank%d" % (2 * k)), Buf("bank%d" % (2 * k + 1))
            banks.append(Tile(t[:, 0:512], [b0]))
            banks.append(Tile(t[:, 512:1024], [b1]))
            dbanks.append(Tile(t[:], [b0, b1]))

        def bank_bf(i):
            return banks[i].ap.bitcast(BF16)

        stat_col = [0]
        stat_memset = [None]

        def new_stat(n=1):
            c = stat_col[0]
            stat_col[0] += n
            assert stat_col[0] <= 1104
            b1 = Buf("st%d" % c)
            b1.lw = stat_memset[0]
            return Tile(stat_t[:, c:c + n], [b1]), Tile(stat2_t[:, c:c + n], [Buf("st2_%d" % c)])

        def dbg(name, tile, ap, shape):
            if name not in debug:
                return
            t = nc.dram_tensor("dbg_" + name, list(shape), F32, kind="ExternalOutput")
            dbg_outs[name] = shape
            add("pool", lambda e: e.dma_start(out=t.ap(), in_=ap), reads=[tile], dma="dbg_" + name)
            S.final.append("dbg_" + name)

        add("pool", lambda e: e.memset(ones_f.ap, 1.0), writes=[ones_f])
        add("pool", lambda e: e.memset(ones_b.ap, 1.0), writes=[ones_b])
        stat_memset[0] = add("pool", lambda e: e.memset(stat_t[:], 0.0), writes=[])
        add("pool", lambda e: e.memset(rmask.ap, 1.0), writes=[rmask])
        add("pool", lambda e: e.memset(rmask.ap.rearrange("p (c t) -> p c t", t=128)[:, :, 0:1], 0.0), writes=[rmask])
        add("pool", lambda e: e.affine_select(out=maskf.ap, in_=ones_f.ap[:, 0:128], pattern=[[1, 128]], compare_op=ALU.is_ge,
                                              fill=0.0, base=0, channel_multiplier=-1), reads=[ones_f], writes=[maskf])
        add("pool", lambda e: e.affine_select(out=maskb.ap, in_=ones_f.ap[:, 0:128], pattern=[[-1, 128]], compare_op=ALU.is_ge,
                                              fill=0.0, base=-1, channel_multiplier=1), reads=[ones_f], writes=[maskb])
        add("pool", lambda e: e.affine_select(out=freq4.ap, in_=ones_f.ap[:, 0:128], pattern=[[-1, 128]], compare_op=ALU.is_equal,
                                              fill=0.0, base=0, channel_multiplier=1), reads=[ones_f], writes=[freq4])
        add("dve", lambda e: e.tensor_copy(out=ident.ap, in_=freq4.ap), reads=[freq4], writes=[ident])

        add("sp", lambda e: e.dma_start(out=gq_bc.ap, in_=g_q.ap().partition_broadcast(128)[:, 0, :]), writes=[gq_bc], dma="c0")
        add("sp", lambda e: e.dma_start(out=gk_bc.ap, in_=g_k.ap().partition_broadcast(128)[:, 0, :]), writes=[gk_bc], dma="c0")
        add("sp", lambda e: e.dma_start(out=ggla_bc.ap, in_=g_gla_norm.ap().partition_broadcast(128)[:, 0, :]), writes=[ggla_bc], dma="c0")
        add("sp", lambda e: e.dma_start(out=masks.ap.rearrange("p a b -> p (a b)"), in_=masks_d.ap().partition_broadcast(128)[:, 0, :]),
            writes=[masks], dma="c0")
        for i, gsrc in enumerate((g_pre_mix, g_pre_ffn, g_ple_pre)):
            add("sp", lambda e, i=i, gsrc=gsrc: e.dma_start(out=g_fm.ap[:, i, :], in_=gsrc.ap().rearrange("o (c p) -> p (o c)", p=128),
                                                            allow_slow_non_contiguous=True), writes=[g_fm], dma="c0")
        for i, bsrc in enumerate((b_gf, b_gb)):
            add("sp", lambda e, i=i, bsrc=bsrc: e.dma_start(out=nbg.ap[:, i, :], in_=bsrc.ap().rearrange("o (c p) -> p (o c)", p=128),
                                                            allow_slow_non_contiguous=True), writes=[nbg], dma="c0")
        add("dve", lambda e: e.tensor_scalar(out=nbg.ap, in0=nbg.ap, scalar1=-1.0, scalar2=None, op0=ALU.mult), reads=[nbg], writes=[nbg])
        add("sp", lambda e: e.dma_start(out=posT.ap, in_=pos_all.ap().rearrange("(t p) c -> p t c", p=128),
                                        allow_slow_non_contiguous=True), writes=[posT], dma="c0")
        add("pool", lambda e: e.dma_start(out=wg_up.ap[:, 0, :], in_=w_gf_up.ap()), writes=[wg_up], dma="c1")
        add("pool", lambda e: e.dma_start(out=wg_up.ap[:, 1, :], in_=w_gb_up.ap()), writes=[wg_up], dma="c1")

        conv_list = []
        win_v = w_in.ap().rearrange("(kc p) n -> p kc n", p=128)
        conv_list.append((wgates_r.ap().rearrange("p (kc n) -> p kc n", n=32), win_v[:, :, 4608:4640], B_wgates))
        for gi in (2, 4, 5, 6, 3, 7, 8, 0, 1):
            c0 = IN_GROUP_COLS[gi]
            conv_list.append((win_r.ap()[gi].rearrange("p (kc n) -> p kc n", n=512), win_v[:, :, c0:c0 + 512], B_win[gi]))
        wout_v = w_out.ap().rearrange("(kc p) n -> p kc n", p=128)
        for cg in range(4):
            conv_list.append((wout_r.ap()[cg].rearrange("p (kc n) -> p kc n", n=512), wout_v[:, :, cg * 512:(cg + 1) * 512], B_wout[cg]))
        wgu_v = w_gate_up.ap().rearrange("(kc p) n -> p kc n", p=128)
        for f in range(22):
            dst = wgu_r.ap()[f].rearrange("p (s kc n) -> p s kc n", s=2, n=256)
            conv_list.append((dst[:, 0], wgu_v[:, :, f * 256:(f + 1) * 256], B_wgu[f]))
            conv_list.append((dst[:, 1], wgu_v[:, :, DFF + f * 256:DFF + (f + 1) * 256], B_wgu[f]))
        for half in range(2):
            for cg in range(4):
                for part in range(2):
                    idx = (half * 4 + cg) * 2 + part
                    r0 = half * 2816 + part * 1408
                    src = w_down.ap()[r0:r0 + 1408, cg * 512:(cg + 1) * 512].rearrange("(f p) n -> p f n", p=128)
                    conv_list.append((wdn_r.ap()[idx].rearrange("p (f n) -> p f n", n=512), src, B_wdn[idx]))
        wpg_v = w_ple_gate.ap().rearrange("(kc p) n -> p kc n", p=128)
        for cg in range(4):
            conv_list.append((wpg_r.ap()[cg].rearrange("p (kc n) -> p kc n", n=512), wpg_v[:, :, cg * 512:(cg + 1) * 512], B_wpg[cg]))
        conv_list.append((wpp_r.ap().rearrange("p (kc n) -> p kc n", n=2048), w_ple_proj.ap().rearrange("(kc p) n -> p kc n", p=128), B_wpp))
        conv_pos = [0]

        def do_conv(n):
            for _ in range(n):
                if conv_pos[0] >= len(conv_list):
                    return
                dst, src, bw = conv_list[conv_pos[0]]
                conv_pos[0] += 1
                add("pool", lambda e, dst=dst, src=src: e.dma_start(out=dst, in_=src), writes=[bw], dma="conv")

        do_conv(10)
        add("sp", lambda e: e.dma_start(out=wgates.ap, in_=wgates_r.ap().rearrange("p (kc n) -> p kc n", n=32)),
            reads=[B_wgates], writes=[wgates], dma="c2")

        add("dve", lambda e: e.tensor_reduce(out=small.ap[0:1, 0:1], in_=gq_bc.ap[0:1, :], axis=AX.X, op=ALU.max, apply_absolute_value=True),
            reads=[gq_bc], writes=[small])
        add("dve", lambda e: e.tensor_reduce(out=small.ap[0:1, 1:2], in_=gk_bc.ap[0:1, :], axis=AX.X, op=ALU.max, apply_absolute_value=True),
            reads=[gk_bc], writes=[small])
        add("dve", lambda e: e.tensor_tensor(out=small.ap[0:1, 2:3], in0=small.ap[0:1, 0:1], in1=small.ap[0:1, 1:2], op=ALU.mult),
            reads=[small], writes=[small])
        add("dve", lambda e: e.tensor_scalar(out=small.ap[0:1, 3:4], in0=small.ap[0:1, 2:3], scalar1=-float(np.sqrt(128.0)), scalar2=None, op0=ALU.mult),
            reads=[small], writes=[small])
        add("pe", lambda e: e.matmul(banks[0].ap[:, 0:1], lhsT=ones_f.ap[0:1, 0:128], rhs=small.ap[0:1, 3:4], start=True, stop=True),
            reads=[ones_f, small], writes=[banks[0]])
        add("dve", lambda e: e.tensor_copy(out=negshift.ap, in_=banks[0].ap[:, 0:1]), reads=[banks[0]], writes=[negshift])

        add("pool", lambda e: e.iota(fi.ap, pattern=[[1, 32]], base=0, channel_multiplier=0), writes=[fi])
        for q4 in range(4):
            add("dve", lambda e, q4=q4: e.tensor_copy(out=freq4.ap[:, q4 * 32:(q4 + 1) * 32], in_=fi.ap), reads=[fi, ident], writes=[freq4])
        add("act", lambda e: e.activation(out=freq4.ap, in_=freq4.ap, func=AF.Exp, scale=-float(np.log(10000.0)) / 32.0), reads=[freq4], writes=[freq4])
        add("pool", lambda e: e.memset(phase4.ap, 0.0), writes=[phase4])
        add("pool", lambda e: e.memset(phase4.ap[:, 32:64], PI / 2), writes=[phase4])
        add("pool", lambda e: e.memset(phase4.ap[:, 96:128], PI / 2), writes=[phase4])
        ang = R.view(0, [4, 128], F32)
        angi = R.view(2 * KB, [4, 128], I32)
        angm = R.view(4 * KB, [4, 128], F32)
        n_own_tiles = n_own_blocks * 4
        rope_groups = list(range(n_own_blocks)) + [8 + b for b in range(n_ctx_blocks)]
        for gi in rope_groups:
            for t in range(4):
                T = gi * 4 + t
                add("dve", lambda e, T=T, t=t: e.scalar_tensor_tensor(out=ang.ap[:, t, 0:64], in0=freq4.ap[:, 0:64], scalar=posT.ap[:, T, 0:1],
                                                                      in1=phase4.ap[:, 0:64], op0=ALU.mult, op1=ALU.add),
                    reads=[freq4, posT, phase4], writes=[ang])
                add("dve", lambda e, T=T, t=t: e.scalar_tensor_tensor(out=ang.ap[:, t, 64:128], in0=freq4.ap[:, 64:128], scalar=posT.ap[:, T, 1:2],
                                                                      in1=phase4.ap[:, 64:128], op0=ALU.mult, op1=ALU.add),
                    reads=[freq4, posT, phase4], writes=[ang])
            add("dve", lambda e: e.tensor_scalar(out=angi.ap, in0=ang.ap, scalar1=1.0 / (2 * PI), scalar2=None, op0=ALU.mult), reads=[ang], writes=[angi])
            add("dve", lambda e: e.scalar_tensor_tensor(out=ang.ap, in0=angi.ap, scalar=-2 * PI, in1=ang.ap, op0=ALU.mult, op1=ALU.add),
                reads=[angi, ang], writes=[ang])
            add("dve", lambda e: e.tensor_scalar(out=angm.ap, in0=ang.ap, scalar1=PI, scalar2=2 * PI, op0=ALU.is_gt, op1=ALU.mult), reads=[ang], writes=[angm])
            add("dve", lambda e: e.tensor_tensor(out=ang.ap, in0=ang.ap, in1=angm.ap, op=ALU.subtract), reads=[ang, angm], writes=[ang])
            add("dve", lambda e: e.tensor_scalar(out=angm.ap, in0=ang.ap, scalar1=-PI, scalar2=2 * PI, op0=ALU.is_lt, op1=ALU.mult), reads=[ang], writes=[angm])
            add("dve", lambda e: e.tensor_tensor(out=ang.ap, in0=ang.ap, in1=angm.ap, op=ALU.add), reads=[ang, angm], writes=[ang])
            add("dve", lambda e: e.tensor_scalar(out=ang.ap, in0=ang.ap, scalar1=-3.14159, scalar2=3.14159, op0=ALU.max, op1=ALU.min), reads=[ang], writes=[ang])
            add("act", lambda e: e.activation(out=ang.ap, in_=ang.ap, func=AF.Sin), reads=[ang], writes=[ang])
            add("sp", lambda e, gi=gi: e.dma_start(out=rope_d.ap()[gi * 512:(gi + 1) * 512, :].rearrange("(t p) c -> p t c", p=128), in_=ang.ap),
                reads=[ang], writes=[B_rope[gi]], dma="ropew")

        def rmsnorm_to_hT(xt, dst_hT, t, gidx, xn_slot, pbanks):
            s1, s2 = new_stat()
            add("act", lambda e: e.activation(out=xn_slot.ap, in_=xt.ap, func=AF.Square, accum_out=s1.ap), reads=[xt], writes=[xn_slot, s1])
            add("act", lambda e: e.activation(out=s2.ap, in_=s1.ap, func=AF.Ln, scale=1.0 / D, bias=EPS), reads=[s1], writes=[s2])
            add("act", lambda e: e.activation(out=s2.ap, in_=s2.ap, func=AF.Exp, scale=-0.5), reads=[s2], writes=[s2])
            add("dve", lambda e: e.tensor_scalar(out=xn_slot.ap, in0=xt.ap, scalar1=s2.ap, scalar2=None, op0=ALU.mult),
                reads=[xt, s2], writes=[xn_slot])
            for half in range(2):
                pb = pbanks[half]
                for k8 in range(8):
                    kc = half * 8 + k8
                    add("pe", lambda e, kc=kc, k8=k8, pb=pb: e.transpose(out=bank_bf(pb)[:, k8 * 128:(k8 + 1) * 128],
                                                                         in_=xn_slot.ap[:, kc * 128:(kc + 1) * 128], identity=ident.ap),
                        reads=[xn_slot, ident], writes=[banks[pb]])
                add("dve", lambda e, half=half, pb=pb: e.tensor_tensor(
                    out=dst_hT.ap[:, half * 8:(half + 1) * 8, t * 128:(t + 1) * 128],
                    in0=bank_bf(pb).rearrange("p (a b) -> p a b", a=8),
                    in1=g_fm.ap[:, gidx, half * 8:(half + 1) * 8].unsqueeze(2).to_broadcast([128, 8, 128]), op=ALU.mult),
                    reads=[banks[pb], g_fm], writes=[dst_hT])

        def qk_norm_rope(srcT, src_ap, H, g_bc, ropeT, rope_ap, outT, out_ap, tmpA, tmpB):
            src3 = src_ap.rearrange("p (h d) -> p h d", h=H)
            A3 = tmpA.ap.rearrange("p (h d) -> p h d", h=H)
            s1, s2 = new_stat(H)
            add("act", lambda e: e.activation(out=A3, in_=src3, func=AF.Square), reads=[srcT], writes=[tmpA])
            add("dve", lambda e: e.tensor_reduce(out=s2.ap, in_=A3, axis=AX.X, op=ALU.add), reads=[tmpA], writes=[s2])
            add("act", lambda e: e.activation(out=s2.ap, in_=s2.ap, func=AF.Ln, scale=1.0 / 128, bias=EPS), reads=[s2], writes=[s2])
            add("act", lambda e: e.activation(out=s2.ap, in_=s2.ap, func=AF.Exp, scale=-0.5), reads=[s2], writes=[s2])
            add("dve", lambda e: e.tensor_tensor(out=A3, in0=src3, in1=s2.ap.unsqueeze(2).to_broadcast([128, H, 128]), op=ALU.mult),
                reads=[srcT, s2], writes=[tmpA])
            add("dve", lambda e: e.tensor_tensor(out=A3, in0=A3, in1=g_bc.ap.unsqueeze(1).to_broadcast([128, H, 128]), op=ALU.mult),
                reads=[tmpA, g_bc], writes=[tmpA])
            x5 = tmpA.ap.rearrange("p (h a b f) -> p h a b f", h=H, a=2, b=2)
            t5 = tmpB.ap.rearrange("p (h a b f) -> p h a b f", h=H, a=2, b=2)
            o5 = out_ap.rearrange("p h (a b f) -> p h a b f", a=2, b=2)
            r4 = rope_ap.rearrange("p (a b f) -> p a b f", a=2, b=2)
            sin_b = r4[:, :, 0, :].unsqueeze(1).to_broadcast([128, H, 2, 32])
            cos_b = r4[:, :, 1, :].unsqueeze(1).to_broadcast([128, H, 2, 32])
            x1 = x5[:, :, :, 0, :]
            x2 = x5[:, :, :, 1, :]
            add("dve", lambda e: e.tensor_tensor(out=t5[:, :, :, 0, :], in0=x2, in1=sin_b, op=ALU.mult), reads=[tmpA, ropeT], writes=[tmpB])
            add("dve", lambda e: e.tensor_tensor(out=t5[:, :, :, 1, :], in0=x1, in1=sin_b, op=ALU.mult), reads=[tmpA, ropeT], writes=[tmpB])
            add("dve", lambda e: e.tensor_tensor(out=x1, in0=x1, in1=cos_b, op=ALU.mult), reads=[tmpA, ropeT], writes=[tmpA])
            add("dve", lambda e: e.tensor_tensor(out=x2, in0=x2, in1=cos_b, op=ALU.mult), reads=[tmpA, ropeT], writes=[tmpA])
            add("dve", lambda e: e.tensor_tensor(out=o5[:, :, :, 0, :], in0=x1, in1=t5[:, :, :, 0, :], op=ALU.subtract), reads=[tmpA, tmpB], writes=[outT])
            add("dve", lambda e: e.tensor_tensor(out=o5[:, :, :, 1, :], in0=x2, in1=t5[:, :, :, 1, :], op=ALU.add), reads=[tmpA, tmpB], writes=[outT])

        def softplus_neg(ps_tile, ps_ap, dirn, h, Ldst):
            add("act", lambda e: e.activation(out=Ldst.ap, in_=ps_ap, func=AF.Exp, scale=-1.0, bias=nbg.ap[:, dirn, h:h + 1]),
                reads=[ps_tile, nbg], writes=[Ldst])
            add("act", lambda e: e.activation(out=Ldst.ap, in_=Ldst.ap, func=AF.Ln, bias=1.0), reads=[Ldst], writes=[Ldst])

        def gates_low(hT_t, pb, glow):
            for dirn in range(2):
                for kc in range(16):
                    add("pe", lambda e, kc=kc, dirn=dirn: e.matmul(banks[pb].ap[0:16, :], lhsT=wgates.ap[:, kc, dirn * 16:(dirn + 1) * 16],
                                                                  rhs=hT_t.ap[:, kc, :], start=(kc == 0), stop=(kc == 15)),
                        reads=[wgates, hT_t], writes=[banks[pb]])
                add("act", lambda e, dirn=dirn: e.activation(out=glow.ap[:, dirn, :], in_=banks[pb].ap[0:16, :], func=AF.Copy),
                    reads=[banks[pb]], writes=[glow])

        def mm_group(pb_ap, pbT, lhs_fn, rhs_fn, n, reads):
            for kc in range(n):
                MM(pb_ap, lhs_fn(kc), rhs_fn(kc), kc == 0, kc == n - 1, reads, [pbT])

        wkv = R.view(0, [16, 512], BF16)
        xs_a = [R.view(16 * KB + i * 8 * KB, [2048], F32) for i in range(2)]
        o = 32 * KB
        ropeA = [R.view(o + i * 2 * KB, [4, 128], F32) for i in range(2)]; o += 4 * KB
        kvA = R.view(o, [256], F32); o += 1 * KB
        kvB = R.view(o, [256], F32); o += 1 * KB
        k_bf = R.view(o, [2, 128], BF16); o += 512
        v_bf = R.view(o, [2, 128], BF16); o += 512
        kT_blk = R.view(o, [2, 512], BF16); o += 2 * KB
        glowA = R.view(o, [2, 512], BF16, parts=16); o += 2 * KB
        LtD = [R.view(o + i * 2 * KB, [512], F32) for i in range(2)]; o += 4 * KB
        CtD = [R.view(o + i * 2 * KB, [512], F32) for i in range(2)]; o += 4 * KB
        kt_fm = [R.view(o + i * KB, [512], BF16) for i in range(2)]; o += 2 * KB
        kt_tm = [R.view(o + i * KB, [4, 128], BF16) for i in range(2)]; o += 2 * KB
        v_tmA = R.view(o, [4, 1024], BF16); o += 8 * KB
        Sf_st = R.view(o, [4, 256], F32); o += 4 * KB
        Sb_st = R.view(o, [4, 4, 256], F32); o += 16 * KB
        assert o <= 86 * KB, o
        smallD = [sbt("smallD%d" % i, [128, 8], F32) for i in range(2)]

        def load_w(slot_view, slotT, src_ap, srcbuf, key):
            add("sp", lambda e: e.dma_start(out=slot_view, in_=src_ap), reads=[srcbuf], writes=[slotT], dma=key)

        def win_src(gi):
            return win_r.ap()[gi].rearrange("p (kc n) -> p kc n", n=512)

        if n_ctx_blocks > 0:
            load_w(wkv.ap, wkv, win_src(2), B_win[2], "wA")
            load_w(wslot[0].ap, wslot[0], win_src(4), B_win[4], "wA")
            load_w(wslot[1].ap, wslot[1], win_src(5), B_win[5], "wA")
            load_w(wslot[2].ap, wslot[2], win_src(6), B_win[6], "wA")

        def init_states():
            add("pool", lambda e: e.memset(Sf_st.ap, 0.0), writes=[Sf_st])
            add("pool", lambda e: e.memset(Sb_st.ap, 0.0), writes=[Sb_st])
            add("pool", lambda e: e.memset(dacc.ap, 1.0), writes=[dacc])

        def spill_states(seq):
            add("sp", lambda e: e.dma_start(out=sfb_d.ap()[seq, 0].rearrange("p (h v) -> p h v", h=4), in_=Sf_st.ap),
                reads=[Sf_st], writes=[B_sfb[seq][0]], dma="spill")
            for j in range(4):
                add("sp", lambda e, j=j: e.dma_start(out=sfb_d.ap()[seq, 1 + j].rearrange("p (h v) -> p h v", h=4), in_=Sb_st.ap[:, :, j, :]),
                    reads=[Sb_st], writes=[B_sfb[seq][1 + j]], dma="spill")

        def run_interleaved(gens):
            gens = list(gens)
            while gens:
                for g_ in list(gens):
                    try:
                        next(g_)
                    except StopIteration:
                        gens.remove(g_)

        def gen_F(B):
            hT_t = hTa if B % 2 == 0 else hTb
            r0 = NOWN + B * 512
            rp = ropeA[B % 2]
            DMA("sp", rp.ap, rope_d.ap()[r0:r0 + 512, :].rearrange("(t p) c -> p t c", p=128), [B_rope[r0 // 512]], [rp], "ropeA%d" % (B % 2))
            for t in range(4):
                T = B * 4 + t
                xt = xs_a[T % 2]
                DMA("sp", xt.ap, x_ctx.ap()[T * 128:(T + 1) * 128, :], [], [xt], "xa%d" % (T % 2))
                yield
                rmsnorm_to_hT(xt, hT_t, t, 0, xn_bf[T % 2], (0, 0))
                yield
            if B == 0:
                dbg("hT0", hT_t, hT_t.ap[:, 0, :], [128, 512])

        def gen_KV(B):
            hT_t = hTa if B % 2 == 0 else hTb
            rp = ropeA[B % 2]
            for t in range(4):
                mm_group(banks[3].ap, banks[3], lambda kc: hT_t.ap[:, kc, t * 128:(t + 1) * 128], lambda kc: wkv.ap[:, kc, :], 16, [hT_t, wkv])
                yield
                ACT(v_bf.ap, banks[3].ap[:, 256:512].rearrange("p (g d) -> p g d", g=2), AF.Copy, [banks[3]], [v_bf])
                qk_norm_rope(banks[3], banks[3].ap[:, 0:256], 2, gk_bc, rp, rp.ap[:, t, :], k_bf, k_bf.ap, kvA, kvB)
                yield
                for g in range(2):
                    TR(bank_bf(0)[:, g * 128:(g + 1) * 128], k_bf.ap[:, g, :], [k_bf], [banks[0]])
                ACT(kT_blk.ap[:, :, t * 128:(t + 1) * 128], bank_bf(0)[:, 0:256].rearrange("p (g d) -> p g d", g=2), AF.Copy, [banks[0]], [kT_blk])
                chunk = B * 4 + t
                DMA("sp", v_r.ap()[:, :, chunk * 128:(chunk + 1) * 128].rearrange("g p d -> p g d"), v_bf.ap, [v_bf], [B_v[B]], "vst")
                yield
            DMA("sp", kT_ctx.ap()[:, :, B * 512:(B + 1) * 512].rearrange("g p n -> p g n"), kT_blk.ap, [kT_blk], [B_kT[B]], "kst")
            if B == 0:
                dbg("kT0", kT_blk, kT_blk.ap[:, 0, :], [128, 512])

        def gen_GKd(B, h, dirn, pk):
            pb = banks[5 + dirn]
            Lt, Ct, sd = LtD[dirn], CtD[dirn], smallD[dirn]
            mB = masks.ap[:, B, :]
            MM(pb.ap, wg_up.ap[:, dirn, h * 128:(h + 1) * 128], glowA.ap[:, dirn, :], True, True, [wg_up, glowA], [pb])
            softplus_neg(pb, pb.ap, dirn, h, Lt)
            yield
            add("dve", lambda e: e.tensor_tensor_scan(out=Ct.ap, data0=ones_f.ap, data1=Lt.ap, initial=0.0, op0=ALU.mult, op1=ALU.add),
                [ones_f, Lt], [Ct])
            TS(sd.ap[:, 0:1], Ct.ap[:, 511:512], -1.0 / 16, ALU.mult, [Ct], [sd])
            ACT(sd.ap[:, 1:2], sd.ap[:, 0:1], AF.Exp, [sd], [sd])
            yield
            if dirn == 0:
                ACT(Ct.ap, Ct.ap, AF.Exp, [Ct, sd], [Ct], scale=1.0 / 16, bias=sd.ap[:, 0:1])
            else:
                TT(Ct.ap, Ct.ap, Lt.ap, ALU.subtract, [Ct, Lt], [Ct])
                ACT(Ct.ap, Ct.ap, AF.Exp, [Ct], [Ct], scale=-1.0 / 16)
            TT(kt_fm[dirn].ap, banks[pk].ap, Ct.ap, ALU.mult, [banks[pk], Ct], [kt_fm[dirn]])
            yield
            for t in range(4):
                TR(bank_bf(5 + dirn)[:, t * 128:(t + 1) * 128], kt_fm[dirn].ap[:, t * 128:(t + 1) * 128], [kt_fm[dirn]], [pb])
            ACT(kt_tm[dirn].ap, bank_bf(5 + dirn)[:, 0:512].rearrange("p (t d) -> p t d", t=4), AF.Copy, [pb], [kt_tm[dirn]])
            yield
            kv = pb.ap[:, 256:512]
            for t in range(4):
                MM(kv, kt_tm[dirn].ap[:, t, :], v_tmA.ap[:, t, h * 256:(h + 1) * 256], t == 0, t == 3, [kt_tm[dirn], v_tmA], [pb])
            yield
            if dirn == 0:
                STT(sd.ap[:, 2:3], sd.ap[:, 1:2], mB[:, 0:1], mB[:, 1:2], ALU.mult, ALU.add, [sd, masks], [sd])
                TS(Sf_st.ap[:, h, :], Sf_st.ap[:, h, :], sd.ap[:, 2:3], ALU.mult, [Sf_st, sd], [Sf_st])
                STT(Sf_st.ap[:, h, :], kv, mB[:, 0:1], Sf_st.ap[:, h, :], ALU.mult, ALU.add, [pb, masks, Sf_st], [Sf_st])
            else:
                TT(dtmp.ap[:, h, :], dacc.ap[:, h, :], mB[:, 2:6], ALU.mult, [dacc, masks], [dtmp])
                for j in range(4):
                    STT(Sb_st.ap[:, h, j, :], kv, dtmp.ap[:, h, j:j + 1], Sb_st.ap[:, h, j, :], ALU.mult, ALU.add, [pb, dtmp, Sb_st], [Sb_st])
                    if j == 1:
                        yield
                STT(dtmp2.ap[:, h, :], mB[:, 2:6], sd.ap[:, 1:2], mB[:, 6:10], ALU.mult, ALU.add, [masks, sd], [dtmp2])
                TT(dacc.ap[:, h, :], dacc.ap[:, h, :], dtmp2.ap[:, h, :], ALU.mult, [dacc, dtmp2], [dacc])
            yield

        def gen_GV(B):
            hT_t = hTa if B % 2 == 0 else hTb
            gates_low(hT_t, 1, glowA)
            yield
            for t in range(4):
                for c2 in range(2):
                    pb = banks[1 + c2]
                    mm_group(pb.ap, pb, lambda kc: hT_t.ap[:, kc, t * 128:(t + 1) * 128], lambda kc: wslot[1 + c2].ap[:, kc, :], 16, [hT_t, wslot[1 + c2]])
                    ACT(v_tmA.ap[:, t, c2 * 512:(c2 + 1) * 512], pb.ap, AF.Copy, [pb], [v_tmA])
                    yield

        def gen_G(B):
            hT_t = hTa if B % 2 == 0 else hTb
            if B == 8:
                spill_states(0)
                init_states()
            yield
            yield
            for h in range(4):
                pk = 4 if h % 2 == 0 else 7
                mm_group(banks[pk].ap, banks[pk], lambda kc: wslot[0].ap[:, kc, h * 128:(h + 1) * 128], lambda kc: hT_t.ap[:, kc, :], 16, [hT_t, wslot[0]])
                yield
                g0 = gen_GKd(B, h, 0, pk)
                g1 = gen_GKd(B, h, 1, pk)
                live = [g0, g1]
                while live:
                    for g_ in list(live):
                        try:
                            next(g_)
                        except StopIteration:
                            live.remove(g_)
                    yield

        if n_ctx_blocks > 0:
            init_states()
            run_interleaved([gen_F(0)])
        for B in range(n_ctx_blocks):
            do_conv(3)
            gens = [gen_GV(B), gen_G(B), gen_KV(B)]
            if B + 1 < n_ctx_blocks:
                gens.append(gen_F(B + 1))
            run_interleaved(gens)
        if n_ctx_blocks > 8:
            spill_states(1)
        elif n_ctx_blocks > 0:
            spill_states(0)
        if n_ctx_blocks > 0:
            dbg("Sf_last", Sf_st, Sf_st.ap[:, 0, :], [128, 256])
            dbg("Sb_last", Sb_st, Sb_st.ap[:, 0, 0, :], [128, 256])
        do_conv(1000)

        o = 0
        qT = R.view(o, [8, 512], BF16); o += 8 * KB
        ropeO = R.view(o, [4, 128], F32); o += 2 * KB
        tA = R.view(o, [512], F32); o += 2 * KB
        tB = R.view(o, [512], F32); o += 2 * KB
        q_bf = R.view(o, [4, 128], BF16); o += 1 * KB
        glowO = R.view(o, [2, 512], BF16, parts=16); o += 2 * KB
        att_base = o
        Lg = [R.view(o + i * 2 * KB, [512], F32) for i in range(2)]; o += 4 * KB
        Cg = [R.view(o + i * 2 * KB, [512], F32) for i in range(2)]; o += 4 * KB
        E1 = R.view(o, [512], F32); o += 2 * KB
        E2 = R.view(o, [512], F32); o += 2 * KB
        qd = [R.view(o + i * 4 * KB, [4, 512], BF16) for i in range(2)]; o += 8 * KB
        kd = [R.view(o + i * 4 * KB, [4, 512], BF16) for i in range(2)]; o += 8 * KB
        v_tm = R.view(o, [4, 1024], BF16); o += 8 * KB
        gsil = R.view(o, [4, 1024], BF16); o += 8 * KB
        ktm = [R.view(o + i * KB, [4, 128], BF16) for i in range(2)]; o += 2 * KB
        Tb_bf = R.view(o, [4, 4, 256], BF16); o += 8 * KB
        Sf_bf = R.view(o, [4, 256], BF16); o += 2 * KB
        Sb_cur = R.view(o, [4, 256], F32); o += 4 * KB
        Am = [R.view(o + i * KB, [4, 128], BF16) for i in range(2)]; o += 2 * KB
        mix_bf = R.view(o, [1024], BF16); o += 2 * KB
        assert o <= 86 * KB, o
        xs_o = [T16b.view(i * 8 * KB, [2048], F32) for i in range(2)]
        o = att_base
        kts = [R.view(o + i * 2 * KB, [1024], BF16) for i in range(3)]; o += 6 * KB
        vts = [R.view(o + i * 2 * KB, [8, 128], BF16) for i in range(3)]; o += 6 * KB
        PTP = [R.view(o + i * 2 * KB, [1024], BF16) for i in range(4)]; o += 8 * KB
        accs = R.view(o, [1024], F32); o += 4 * KB
        rsum = R.view(o, [512], F32, parts=1); o += 2 * KB
        bcs = R.view(o, [512], F32); o += 2 * KB
        xs2 = [R.view(i * 8 * KB, [2048], F32) for i in range(4)]
        Y = [R.view(32 * KB + i * 8 * KB, [2048], F32) for i in range(4)]
        Hh = R.view(64 * KB, [22, 512], BF16)
        junk2 = R.view(64 * KB, [2048], BF16)
        p32 = R.view(64 * KB, [4, 256], F32)
        p_bf = R.view(68 * KB, [4, 256], BF16)
        pT = R.view(70 * KB, [2, 512], BF16)
        sgm = [R.view(72 * KB + i * 2 * KB, [512], F32) for i in range(2)]
        wpp_sb = R.view(76 * KB, [2, 2048], BF16)
        sgt = [XN.view(i * 2 * KB, [512], F32) for i in range(2)]
        wslot_gu = [W.view(i * 16 * KB, [2, 16, 256], BF16) for i in range(3)]
        wslot_dn = [W.view(i * 16 * KB, [11, 512], BF16) for i in range(3)]

        blk_uses = []
        for gi in (3, 4, 5, 6, 7, 8, 0, 1):
            blk_uses.append(("in", gi))
        for cg in range(4):
            blk_uses.append(("out", cg))
        for half in range(2):
            for f in range(11):
                blk_uses.append(("gu", half * 11 + f))
            for cg in range(4):
                for part in range(2):
                    blk_uses.append(("dn", (half * 4 + cg) * 2 + part))
        for cg in range(4):
            blk_uses.append(("pg", cg))
        NU = len(blk_uses)
        ws_issued = [0]
        total_uses = NU * n_own_blocks

        def ws_issue(i):
            kind, idx = blk_uses[i % NU]
            sl = i % 3
            key = "ws%d" % sl
            if kind == "in":
                DMA("sp", wslot[sl].ap, win_src(idx), [B_win[idx]], [wslot[sl]], key)
            elif kind == "out":
                DMA("sp", wslot[sl].ap, wout_r.ap()[idx].rearrange("p (kc n) -> p kc n", n=512), [B_wout[idx]], [wslot[sl]], key)
            elif kind == "gu":
                DMA("sp", wslot_gu[sl].ap, wgu_r.ap()[idx].rearrange("p (s kc n) -> p s kc n", s=2, n=256), [B_wgu[idx]], [wslot_gu[sl]], key)
            elif kind == "dn":
                DMA("sp", wslot_dn[sl].ap, wdn_r.ap()[idx].rearrange("p (f n) -> p f n", n=512), [B_wdn[idx]], [wslot_dn[sl]], key)
            elif kind == "pg":
                DMA("sp", wslot[sl].ap, wpg_r.ap()[idx].rearrange("p (kc n) -> p kc n", n=512), [B_wpg[idx]], [wslot[sl]], key)

        def ws_get(i):
            while ws_issued[0] <= i:
                ws_issue(ws_issued[0])
                ws_issued[0] += 1
            return i % 3

        QS = float(128.0 ** -0.5)

        for j in range(n_own_blocks):
            seq = j // 4
            jj = j % 4
            r0 = j * 512
            cb = 0 if seq == 0 else 32
            nkc = 32 if seq == 0 else 128
            ub = j * NU
            DMA("sp", Sb_cur.ap, sfb_d.ap()[seq, 1 + jj].rearrange("p (h v) -> p h v", h=4), [B_sfb[seq][1 + jj]], [Sb_cur], "sbl")
            if jj == 0:
                DMA("sp", Sf_own.ap, sfb_d.ap()[seq, 0].rearrange("p (h v) -> p h v", h=4), [B_sfb[seq][0]], [Sf_own], "sbl")
            DMA("sp", ropeO.ap, rope_d.ap()[r0:r0 + 512, :].rearrange("(t p) c -> p t c", p=128), [B_rope[j]], [ropeO], "ropeO")
            for t in range(4):
                xt = xs_o[t % 2]
                DMA("sp", xt.ap, x_own.ap()[r0 + t * 128:r0 + (t + 1) * 128, :], [], [xt], "xo%d" % (t % 2))
                rmsnorm_to_hT(xt, hTa, t, 0, xn_bf[t % 2], (0, 1))
            if j == 0:
                dbg("hTo", hTa, hTa.ap[:, 0, :], [128, 512])
            gates_low(hTa, 5, glowO)
            sq = ws_get(ub + 0)
            sk = ws_get(ub + 1)
            for h in range(4):
                for dirn in range(2):
                    MM(banks[4].ap, wg_up.ap[:, dirn, h * 128:(h + 1) * 128], glowO.ap[:, dirn, :], True, True, [wg_up, glowO], [banks[4]])
                    softplus_neg(banks[4], banks[4].ap, dirn, h, Lg[dirn])
                    add("dve", lambda e, dirn=dirn: e.tensor_tensor_scan(out=Cg[dirn].ap, data0=rmask.ap, data1=Lg[dirn].ap, initial=0.0,
                                                                         op0=ALU.mult, op1=ALU.add), [rmask, Lg[dirn]], [Cg[dirn]])
                mm_group(banks[2].ap, banks[2], lambda kc: wslot[sq].ap[:, kc, h * 128:(h + 1) * 128], lambda kc: hTa.ap[:, kc, :], 16, [hTa, wslot[sq]])
                mm_group(banks[3].ap, banks[3], lambda kc: wslot[sk].ap[:, kc, h * 128:(h + 1) * 128], lambda kc: hTa.ap[:, kc, :], 16, [hTa, wslot[sk]])
                ACT(E1.ap, Cg[0].ap, AF.Exp, [Cg[0]], [E1], scale=-1.0 / 16)
                ACT(E2.ap, Cg[0].ap, AF.Exp, [Cg[0]], [E2], scale=1.0 / 16)
                STT(qd[0].ap[:, h, :], banks[2].ap, QS, E1.ap, ALU.mult, ALU.mult, [banks[2], E1], [qd[0]])
                TT(kd[0].ap[:, h, :], banks[3].ap, E2.ap, ALU.mult, [banks[3], E2], [kd[0]])
                add("dve", lambda e, h=h: e.tensor_copy(out=dtl.ap[:, 0, h, :], in_=E1.ap.rearrange("p (t c) -> p t c", c=128)[:, :, 127]), [E1], [dtl])
                ACT(dtl.ap[:, 1, h, :], Cg[1].ap.rearrange("p (t c) -> p t c", c=128)[:, :, 127], AF.Exp, [Cg[1]], [dtl], scale=-1.0 / 16)
                TT(E1.ap, Cg[1].ap, Lg[1].ap, ALU.subtract, [Cg[1], Lg[1]], [E1])
                ACT(E2.ap, E1.ap, AF.Exp, [E1], [E2], scale=1.0 / 16)
                ACT(E1.ap, E1.ap, AF.Exp, [E1], [E1], scale=-1.0 / 16)
                STT(qd[1].ap[:, h, :], banks[2].ap, QS, E2.ap, ALU.mult, ALU.mult, [banks[2], E2], [qd[1]])
                TT(kd[1].ap[:, h, :], banks[3].ap, E1.ap, ALU.mult, [banks[3], E1], [kd[1]])
            sv = [ws_get(ub + 2), ws_get(ub + 3)]
            for t in range(4):
                for c2 in range(2):
                    pb = 6 + c2
                    mm_group(banks[pb].ap, banks[pb], lambda kc: hTa.ap[:, kc, t * 128:(t + 1) * 128], lambda kc: wslot[sv[c2]].ap[:, kc, :], 16,
                             [hTa, wslot[sv[c2]]])
                    ACT(v_tm.ap[:, t, c2 * 512:(c2 + 1) * 512], banks[pb].ap, AF.Copy, [banks[pb]], [v_tm])
            sg_ = [ws_get(ub + 4), ws_get(ub + 5)]
            for t in range(4):
                for c2 in range(2):
                    pb = 6 + c2
                    mm_group(banks[pb].ap, banks[pb], lambda kc: hTa.ap[:, kc, t * 128:(t + 1) * 128], lambda kc: wslot[sg_[c2]].ap[:, kc, :], 16,
                             [hTa, wslot[sg_[c2]]])
                    ACT(tA.ap, banks[pb].ap, AF.Silu, [banks[pb]], [tA])
                    TT(gsil.ap[:, t, c2 * 512:(c2 + 1) * 512].rearrange("p (a b) -> p a b", a=2), tA.ap.rearrange("p (a b) -> p a b", a=2),
                       ggla_bc.ap.unsqueeze(1).to_broadcast([128, 2, 256]), ALU.mult, [tA, ggla_bc], [gsil])
            for t in (3, 2, 1, 0):
                kt = ktm[t % 2]
                for h in range(4):
                    TR(bank_bf(5)[:, h * 128:(h + 1) * 128], kd[1].ap[:, h, t * 128:(t + 1) * 128], [kd[1]], [banks[5]])
                ACT(kt.ap, bank_bf(5)[:, 0:512].rearrange("p (a b) -> p a b", a=4), AF.Copy, [banks[5]], [kt])
                for h in range(4):
                    pbk = banks[6 + h // 2]
                    MM(pbk.ap[:, (h % 2) * 256:(h % 2 + 1) * 256], kt.ap[:, h, :], v_tm.ap[:, t, h * 256:(h + 1) * 256], True, True, [kt, v_tm], [pbk])
                for h in range(4):
                    pbk = banks[6 + h // 2]
                    ACT(Tb_bf.ap[:, t, h, :], Sb_cur.ap[:, h, :], AF.Copy, [Sb_cur, dtl], [Tb_bf], scale=dtl.ap[:, 1, h, t:t + 1])
                    STT(Sb_cur.ap[:, h, :], Sb_cur.ap[:, h, :], dtl.ap[:, 1, h, t:t + 1], pbk.ap[:, (h % 2) * 256:(h % 2 + 1) * 256], ALU.mult, ALU.add,
                        [Sb_cur, dtl, pbk], [Sb_cur])
            for t in range(4):
                kt = ktm[t % 2]
                am = Am[t % 2]
                for h in range(4):
                    TR(bank_bf(5)[:, h * 128:(h + 1) * 128], kd[0].ap[:, h, t * 128:(t + 1) * 128], [kd[0]], [banks[5]])
                ACT(kt.ap, bank_bf(5)[:, 0:512].rearrange("p (a b) -> p a b", a=4), AF.Copy, [banks[5]], [kt])
                for h in range(4):
                    pbk = banks[6 + h // 2]
                    MM(pbk.ap[:, (h % 2) * 256:(h % 2 + 1) * 256], kt.ap[:, h, :], v_tm.ap[:, t, h * 256:(h + 1) * 256], True, True, [kt, v_tm], [pbk])
                ACT(Sf_bf.ap, Sf_own.ap, AF.Copy, [Sf_own], [Sf_bf])
                for h in range(4):
                    MM(banks[2].ap[:, h * 128:(h + 1) * 128], kd[0].ap[:, h, t * 128:(t + 1) * 128], qd[0].ap[:, h, t * 128:(t + 1) * 128], True, True,
                       [kd[0], qd[0]], [banks[2]])
                    MM(banks[3].ap[:, h * 128:(h + 1) * 128], kd[1].ap[:, h, t * 128:(t + 1) * 128], qd[1].ap[:, h, t * 128:(t + 1) * 128], True, True,
                       [kd[1], qd[1]], [banks[3]])
                TT(tA.ap.rearrange("p (a b) -> p a b", a=4), banks[2].ap.rearrange("p (a b) -> p a b", a=4),
                   maskf.ap.unsqueeze(1).to_broadcast([128, 4, 128]), ALU.mult, [banks[2], maskf], [tA])
                TT(tB.ap.rearrange("p (a b) -> p a b", a=4), banks[3].ap.rearrange("p (a b) -> p a b", a=4),
                   maskb.ap.unsqueeze(1).to_broadcast([128, 4, 128]), ALU.mult, [banks[3], maskb], [tB])
                TT(am.ap.rearrange("p a b -> p (a b)"), tA.ap, tB.ap, ALU.add, [tA, tB], [am])
                for h in range(4):
                    pbo = banks[h // 2]
                    oap = pbo.ap[:, (h % 2) * 256:(h % 2 + 1) * 256]
                    MM(oap, am.ap[:, h, :], v_tm.ap[:, t, h * 256:(h + 1) * 256], True, False, [am, v_tm], [pbo])
                    MM(oap, qd[0].ap[:, h, t * 128:(t + 1) * 128], Sf_bf.ap[:, h, :], False, False, [qd[0], Sf_bf], [pbo])
                    MM(oap, qd[1].ap[:, h, t * 128:(t + 1) * 128], Tb_bf.ap[:, t, h, :], False, True, [qd[1], Tb_bf], [pbo])
                for h in range(4):
                    pbk = banks[6 + h // 2]
                    TT(Sf_own.ap[:, h, :], Sf_own.ap[:, h, :], pbk.ap[:, (h % 2) * 256:(h % 2 + 1) * 256], ALU.add, [Sf_own, pbk], [Sf_own])
                    TS(Sf_own.ap[:, h, :], Sf_own.ap[:, h, :], dtl.ap[:, 0, h, t:t + 1], ALU.mult, [Sf_own, dtl], [Sf_own])
                s1, s2 = new_stat(4)
                for h in range(4):
                    pbo = banks[h // 2]
                    oap = pbo.ap[:, (h % 2) * 256:(h % 2 + 1) * 256]
                    ACT(tB.ap[:, 0:256], oap, AF.Square, [pbo], [tB, s1], accum=s1.ap[:, h:h + 1])
                ACT(s2.ap, s1.ap, AF.Ln, [s1], [s2], scale=1.0 / 256, bias=EPS)
                ACT(s2.ap, s2.ap, AF.Exp, [s2], [s2], scale=-0.5)
                for h in range(4):
                    pbo = banks[h // 2]
                    oap = pbo.ap[:, (h % 2) * 256:(h % 2 + 1) * 256]
                    STT(mix_bf.ap[:, h * 256:(h + 1) * 256], oap, s2.ap[:, h:h + 1], gsil.ap[:, t, h * 256:(h + 1) * 256], ALU.mult, ALU.mult,
                        [pbo, s2, gsil], [mix_bf])
                if j == 0 and t == 0:
                    dbg("gla0", mix_bf, mix_bf.ap[:, 0:512], [128, 512])
                for c in range(8):
                    TR(bank_bf(4)[:, c * 128:(c + 1) * 128], mix_bf.ap[:, c * 128:(c + 1) * 128], [mix_bf], [banks[4]])
                ACT(hTb.ap[:, 8:16, t * 128:(t + 1) * 128], bank_bf(4).rearrange("p (a b) -> p a b", a=8), AF.Copy, [banks[4]], [hTb])
            sa = [ws_get(ub + 6), ws_get(ub + 7)]
            for t in range(4):
                for cg in range(2):
                    pb = 2 + cg
                    mm_group(banks[pb].ap, banks[pb], lambda kc: hTa.ap[:, kc, t * 128:(t + 1) * 128], lambda kc: wslot[sa[cg]].ap[:, kc, :], 16,
                             [hTa, wslot[sa[cg]]])
                    qk_norm_rope(banks[pb], banks[pb].ap, 4, gq_bc, ropeO, ropeO.ap[:, t, :], q_bf, q_bf.ap, tA, tB)
                    for h4 in range(4):
                        TR(bank_bf(5)[:, h4 * 128:(h4 + 1) * 128], q_bf.ap[:, h4, :], [q_bf], [banks[5]])
                    ACT(qT.ap[:, cg * 4:(cg + 1) * 4, t * 128:(t + 1) * 128], bank_bf(5)[:, 0:512].rearrange("p (a b) -> p a b", a=4), AF.Copy,
                        [banks[5]], [qT])
            if j == 0:
                dbg("qT0", qT, qT.ap[:, 0, :], [128, 512])
            ngrp = nkc // 8
            kv_n = [0]
            for g in range(2):
                for pair in range(2):
                    heads = (4 * g + 2 * pair, 4 * g + 2 * pair + 1)
                    niter = ngrp * 8
                    slot_of = {}

                    def stage_S(n):
                        G, c = n // 8, n % 8
                        if c == 0:
                            sl = kv_n[0] % 3
                            kv_n[0] += 1
                            slot_of[G] = sl
                            c0 = cb + G * 8
                            blks = [B_kT[c0 // 4], B_kT[c0 // 4 + 1], B_v[c0 // 4], B_v[c0 // 4 + 1]]
                            DMA("sp", kts[sl].ap, kT_ctx.ap()[g, :, c0 * 128:(c0 + 8) * 128], blks, [kts[sl]], "kv%d" % sl)
                            DMA("sp", vts[sl].ap, v_r.ap()[g, :, c0 * 128:(c0 + 8) * 128].rearrange("p (c d) -> p c d", d=128), blks, [vts[sl]],
                                "kv%d" % sl)
                        sl = slot_of[G]
                        k3 = 1 + n % 3
                        db = dbanks[k3]
                        for i in range(2):
                            MM(db.ap[:, i * 512:(i + 1) * 512], kts[sl].ap[:, c * 128:(c + 1) * 128], qT.ap[:, heads[i], :], True, True,
                               [kts[sl], qT], [banks[2 * k3 + i]])

                    def stage_EV(n):
                        G, c = n // 8, n % 8
                        sl = slot_of[G]
                        db = dbanks[1 + n % 3]
                        pt = PTP[n % 4]
                        ACT(pt.ap, db.ap, AF.Exp, [db, negshift], [pt], scale=QS, bias=negshift.ap)
                        first = (n == 0)
                        last = (n == niter - 1)
                        for i in range(2):
                            MM(banks[i].ap, vts[sl].ap[:, c, :], pt.ap[:, i * 512:(i + 1) * 512], first, last, [vts[sl], pt], [banks[i]])
                        if first:
                            add("dve", lambda e, pt=pt: e.tensor_copy(out=accs.ap, in_=pt.ap), [pt], [accs])
                        else:
                            TT(accs.ap, accs.ap, pt.ap, ALU.add, [accs, pt], [accs])

                    stage_S(0)
                    if niter > 1:
                        stage_S(1)
                    for n in range(niter):
                        if n + 2 < niter:
                            stage_S(n + 2)
                        stage_EV(n)
                    for i in range(2):
                        hq = heads[i]
                        MM(banks[2 + i].ap[0:1, :], ones_f.ap[:, 0:1], accs.ap[:, i * 512:(i + 1) * 512], True, True, [ones_f, accs], [banks[2 + i]])
                        ACT(rsum.ap, banks[2 + i].ap[0:1, :], AF.Ln, [banks[2 + i]], [rsum])
                        ACT(rsum.ap, rsum.ap, AF.Exp, [rsum], [rsum], scale=-1.0)
                        MM(banks[4 + i].ap, ones_f.ap[0:1, 0:128], rsum.ap, True, True, [ones_f, rsum], [banks[4 + i]])
                        ACT(bcs.ap, banks[4 + i].ap, AF.Copy, [banks[4 + i]], [bcs])
                        TT(hTb.ap[:, hq, :], banks[i].ap, bcs.ap, ALU.mult, [banks[i], bcs], [hTb])
            if j == 0:
                dbg("mixT_a", hTb, hTb.ap[:, 0, :], [128, 512])
                dbg("mixT_g", hTb, hTb.ap[:, 8, :], [128, 512])
            for t in range(4):
                DMA("sp", xs2[t].ap, x_own.ap()[r0 + t * 128:r0 + (t + 1) * 128, :], [], [xs2[t]], "xs2_%d" % t)
            n = 0
            for cg in range(4):
                sl = ws_get(ub + 8 + cg)
                for t in range(4):
                    pb = banks[n % 8]
                    n += 1
                    mm_group(pb.ap, pb, lambda kc: hTb.ap[:, kc, t * 128:(t + 1) * 128], lambda kc: wslot[sl].ap[:, kc, :], 16, [hTb, wslot[sl]])
                    ACT(Y[t].ap[:, cg * 512:(cg + 1) * 512], pb.ap, AF.Copy, [pb], [Y[t]])

            def post_norm(gsrc):
                DMA("sp", gtmp.ap, gsrc.ap().partition_broadcast(128)[:, 0, :], [], [gtmp], "gtmp")
                for t in range(4):
                    s1, s2 = new_stat()
                    ACT(junk2.ap, Y[t].ap, AF.Square, [Y[t]], [junk2, s1], accum=s1.ap)
                    ACT(s2.ap, s1.ap, AF.Ln, [s1], [s2], scale=1.0 / D, bias=EPS)
                    ACT(s2.ap, s2.ap, AF.Exp, [s2], [s2], scale=-0.5)
                    STT(Y[t].ap, Y[t].ap, s2.ap, gtmp.ap, ALU.mult, ALU.mult, [Y[t], s2, gtmp], [Y[t]])
                    TT(xs2[t].ap, xs2[t].ap, Y[t].ap, ALU.add, [xs2[t], Y[t]], [xs2[t]])

            post_norm(g_post_mix)
            if j == 0:
                dbg("x1", xs2[0], xs2[0].ap[:, 0:512], [128, 512])
            for t in range(4):
                rmsnorm_to_hT(xs2[t], hTa, t, 1, xn_bf[t % 2], (0, 1))
            ui = ub + 12
            nff = 0
            for half in range(2):
                for f in range(11):
                    sl = ws_get(ui)
                    ui += 1
                    for c in range(2):
                        pg_ = banks[(nff % 2) * 2]
                        pu_ = banks[(nff % 2) * 2 + 1]
                        st_ = sgt[nff % 2]
                        nff += 1
                        mm_group(pg_.ap, pg_, lambda kc: wslot_gu[sl].ap[:, 0, kc, c * 128:(c + 1) * 128], lambda kc: hTa.ap[:, kc, :], 16,
                                 [hTa, wslot_gu[sl]])
                        mm_group(pu_.ap, pu_, lambda kc: wslot_gu[sl].ap[:, 1, kc, c * 128:(c + 1) * 128], lambda kc: hTa.ap[:, kc, :], 16,
                                 [hTa, wslot_gu[sl]])
                        ACT(st_.ap, pg_.ap, AF.Silu, [pg_], [st_])
                        TT(Hh.ap[:, f * 2 + c, :], st_.ap, pu_.ap, ALU.mult, [st_, pu_], [Hh])
                for cg in range(4):
                    for part in range(2):
                        sl = ws_get(ui)
                        ui += 1
                        for t in range(4):
                            pd = banks[4 + t]
                            for ffc in range(11):
                                MM(pd.ap, Hh.ap[:, part * 11 + ffc, t * 128:(t + 1) * 128], wslot_dn[sl].ap[:, ffc, :],
                                   part == 0 and ffc == 0, part == 1 and ffc == 10, [Hh, wslot_dn[sl]], [pd])
                    for t in range(4):
                        pd = banks[4 + t]
                        if half == 0:
                            ACT(Y[t].ap[:, cg * 512:(cg + 1) * 512], pd.ap, AF.Copy, [pd], [Y[t]])
                        else:
                            TT(Y[t].ap[:, cg * 512:(cg + 1) * 512], pd.ap, Y[t].ap[:, cg * 512:(cg + 1) * 512], ALU.add, [pd, Y[t]], [Y[t]])
            post_norm(g_post_ffn)
            if j == 0:
                dbg("x2", xs2[0], xs2[0].ap[:, 0:512], [128, 512])
            for t in range(4):
                rmsnorm_to_hT(xs2[t], hTa, t, 2, xn_bf[t % 2], (0, 1))
            DMA("sp", p32.ap, p_own.ap()[r0:r0 + 512, :].rearrange("(t p) c -> p t c", p=128), [], [p32], "p32")
            DMA("sp", wpp_sb.ap, wpp_r.ap().rearrange("p (kc n) -> p kc n", n=2048), [B_wpp], [wpp_sb], "wpp")
            ACT(p_bf.ap, p32.ap, AF.Copy, [p32], [p_bf])
            for t in range(4):
                for c in range(2):
                    TR(bank_bf(0)[:, c * 128:(c + 1) * 128], p_bf.ap[:, t, c * 128:(c + 1) * 128], [p_bf], [banks[0]])
                ACT(pT.ap[:, :, t * 128:(t + 1) * 128], bank_bf(0)[:, 0:256].rearrange("p (a b) -> p a b", a=2), AF.Copy, [banks[0]], [pT])
            n = 0
            for cg in range(4):
                sl = ws_get(ub + 50 + cg)
                for t in range(4):
                    pgb = banks[n % 4]
                    ppb = banks[4 + n % 4]
                    sg2 = sgm[n % 2]
                    n += 1
                    mm_group(pgb.ap, pgb, lambda kc: hTa.ap[:, kc, t * 128:(t + 1) * 128], lambda kc: wslot[sl].ap[:, kc, :], 16, [hTa, wslot[sl]])
                    mm_group(ppb.ap, ppb, lambda kc: pT.ap[:, kc, t * 128:(t + 1) * 128], lambda kc: wpp_sb.ap[:, kc, cg * 512:(cg + 1) * 512], 2,
                             [pT, wpp_sb])
                    ACT(sg2.ap, pgb.ap, AF.Sigmoid, [pgb], [sg2])
                    TT(Y[t].ap[:, cg * 512:(cg + 1) * 512], sg2.ap, ppb.ap, ALU.mult, [sg2, ppb], [Y[t]])
            post_norm(g_ple_post)
            for t in range(4):
                DMA("sp", y_own.ap()[r0 + t * 128:r0 + (t + 1) * 128, :], xs2[t].ap, [xs2[t]], [B_yout], "yout")

        S.final.append("yout")
        S.emit(nc, es)
    return nc, dbg_outs


_PROG = {}


def _core_inputs(c, inp):
    p, hf = c // 2, c % 2
    xp = inp["x_prompt"]
    xsm = inp["x_sample"]
    pp = inp["p_prompt"][0]
    psm = inp["p_sample"][0]
    d = {}
    d["x_own"] = np.ascontiguousarray(np.concatenate([xp[p, hf * 2048:(hf + 1) * 2048], xsm[0, c * 2048:(c + 1) * 2048]], 0), dtype=np.float32)
    d["p_own"] = np.ascontiguousarray(np.concatenate([pp[p, hf * 2048:(hf + 1) * 2048], psm[0, c * 2048:(c + 1) * 2048]], 0), dtype=np.float32)
    d["x_ctx"] = np.ascontiguousarray(np.concatenate([xp[p], xsm[0]], 0), dtype=np.float32)
    tok = np.concatenate([hf * 2048 + np.arange(2048), c * 2048 + np.arange(2048), np.arange(4096), np.arange(16384)])
    d["pos_all"] = np.stack([tok // 64, tok % 64], 1).astype(np.float32)
    m = np.zeros((40, 10), np.float32)
    for B in range(40):
        if B < 8:
            b, start = B, hf * 4
        else:
            b, start = B - 8, c * 4
        mf = 1.0 if b < start else 0.0
        m[B, 0] = mf
        m[B, 1] = 1.0 - mf
        for j in range(4):
            mb = 1.0 if b > start + j else 0.0
            m[B, 2 + j] = mb
            m[B, 6 + j] = 1.0 - mb
    d["masks"] = m.reshape(1, 400)
    for k in ("g_pre_mix", "w_in", "g_q", "g_k", "w_gf_up", "b_gf", "w_gb_up", "b_gb", "g_gla_norm", "w_out", "g_post_mix", "g_pre_ffn",
              "w_gate_up", "w_down", "g_post_ffn", "g_ple_pre", "w_ple_gate", "w_ple_proj", "g_ple_post"):
        a = np.asarray(inp[k], dtype=np.float32)[0]
        if a.ndim == 1:
            a = a[None, :]
        d[k] = np.ascontiguousarray(a)
    return d


def kernel(**inputs):
    inp = {k: np.asarray(v) for k, v in inputs.items()}
    if "nc" not in _PROG:
        _PROG["nc"] = build_program()[0]
    nc = _PROG["nc"]
    in_maps = [_core_inputs(c, inp) for c in range(8)]
    res = run_bass_kernel_spmd(nc, in_maps, core_ids=list(range(8)))
    y_prompt = np.zeros((4, 4096, D), np.float32)
    y_sample = np.zeros((1, 16384, D), np.float32)
    for c in range(8):
        y = np.asarray(res.results[c]["y_own"], dtype=np.float32)
        y_prompt[c // 2, (c % 2) * 2048:(c % 2 + 1) * 2048] = y[:2048]
        y_sample[0, c * 2048:(c + 1) * 2048] = y[2048:]
    return (y_prompt, y_sample)
```

```python
import os
import numpy as np
from contextlib import ExitStack
import concourse.bass as bass
import concourse.mybir as mybir
from concourse.bass_utils import run_bass_kernel_spmd

F32 = mybir.dt.float32
BF16 = mybir.dt.bfloat16
I32 = mybir.dt.int32
AF = mybir.ActivationFunctionType
ALU = mybir.AluOpType
AX = mybir.AxisListType

D = 2048
DIN = 4640
DFF = 5632
DPLE = 256
NOWN = 4096
NCTX = 20480
EPS = 1e-6
PI = float(np.pi)


class Buf:
    __slots__ = ("name", "lw", "rd")

    def __init__(self, name):
        self.name = name
        self.lw = None
        self.rd = {}


class Tile:
    __slots__ = ("ap", "bufs")

    def __init__(self, ap, bufs):
        self.ap = ap
        self.bufs = bufs


class Op:
    __slots__ = ("eng", "fn", "waits", "signal", "semkey", "pos", "semval", "isdma")


def _flat(lst):
    out = []
    for x in lst:
        if x is None:
            continue
        if isinstance(x, Buf):
            out.append(x)
        elif isinstance(x, Tile):
            out.extend(x.bufs)
        else:
            out.extend(_flat(x))
    return out


class Sched:
    ENGS = ("pe", "dve", "act", "pool", "sp")

    def __init__(self):
        self.streams = {e: [] for e in self.ENGS}
        self.dma_count = {}
        self.seen = {e: {} for e in self.ENGS}
        self.final = []

    def add(self, eng, fn, reads=(), writes=(), dma=None):
        reads = _flat(reads)
        writes = _flat(writes)
        op = Op()
        op.eng = eng
        op.fn = fn
        op.signal = False
        op.isdma = dma is not None
        op.waits = []
        op.semval = None
        stream = self.streams[eng]
        deps = {}
        seen = self.seen[eng]
        pe_compute = (eng == "pe") and not op.isdma

        def need(d):
            if d is None:
                return
            if pe_compute and (not d.isdma) and d.eng == "pe":
                return
            k = d.semkey
            p = self.dma_count[k[1]] if d.isdma else d.pos
            if p <= seen.get(k, -1):
                return
            cur = deps.get(k)
            if cur is None or p > cur[0]:
                deps[k] = (p, d)
            elif (not d.isdma) and d.pos > cur[1].pos:
                deps[k] = (p, d)

        for b in reads:
            need(b.lw)
        for b in writes:
            need(b.lw)
            for r in b.rd.values():
                need(r)
        for k, (p, d) in deps.items():
            if d.isdma:
                op.waits.append((k[1], p * 16))
            else:
                d.signal = True
                op.waits.append(d)
            seen[k] = p
        if op.isdma:
            n = self.dma_count.get(dma, 0) + 1
            self.dma_count[dma] = n
            op.semkey = ("dma", dma)
            op.pos = n
        else:
            op.semkey = ("eng", eng)
            op.pos = len(stream)
        stream.append(op)
        for b in writes:
            b.lw = op
            b.rd = {}
        for b in reads:
            b.rd[op.semkey] = op
        return op

    def emit(self, nc, es):
        engsem = {e: es.enter_context(nc.semaphore("s_" + e)) for e in ("pe", "dve", "act", "pool")}
        dmasem = {k: es.enter_context(nc.semaphore("d_" + k)) for k in self.dma_count}
        for e, stream in self.streams.items():
            c = 0
            for op in stream:
                if (not op.isdma) and op.signal:
                    c += 1
                    op.semval = c
        block = es.enter_context(nc.Block())
        streams = self.streams
        final_waits = [(k, self.dma_count[k] * 16) for k in self.final if k in self.dma_count]

        def run(engname, eng):
            for op in streams[engname]:
                for w in op.waits:
                    if isinstance(w, Op):
                        eng.wait_ge(engsem[w.eng], w.semval)
                    else:
                        eng.wait_ge(dmasem[w[0]], w[1])
                inst = op.fn(eng)
                if op.isdma:
                    inst.then_inc(dmasem[op.semkey[1]], 16)
                elif op.signal:
                    inst.then_inc(engsem[engname], 1)

        @block.tensor
        def _(e):
            run("pe", e)

        @block.vector
        def _(e):
            run("dve", e)

        @block.scalar
        def _(e):
            run("act", e)

        @block.gpsimd
        def _(e):
            run("pool", e)

        @block.sync
        def _(e):
            run("sp", e)
            for k, v in final_waits:
                e.wait_ge(dmasem[k], v)


GRAN = 1024


class Arena:
    def __init__(self, nc, es, name, nbytes):
        assert nbytes % 4 == 0
        self.nbytes = nbytes
        self.t = es.enter_context(nc.sbuf_tensor(name, [128, nbytes // 4], F32))
        self.gr = [Buf("%s_%d" % (name, i)) for i in range((nbytes + GRAN - 1) // GRAN)]

    def view(self, off, shape, dt, parts=128):
        esz = 2 if dt == BF16 else 4
        n = 1
        for s in shape:
            n *= s
        nb = n * esz
        assert off % 4 == 0 and nb % 4 == 0 and off + nb <= self.nbytes, (off, nb, self.nbytes)
        ap = self.t[0:parts, off // 4:(off + nb) // 4]
        if dt != F32:
            ap = ap.bitcast(dt)
        if len(shape) == 2:
            ap = ap.rearrange("p (a b) -> p a b", a=shape[0])
        elif len(shape) == 3:
            ap = ap.rearrange("p (a b c) -> p a b c", a=shape[0], b=shape[1])
        elif len(shape) == 4:
            ap = ap.rearrange("p (a b c d) -> p a b c d", a=shape[0], b=shape[1], c=shape[2])
        return Tile(ap, self.gr[off // GRAN:(off + nb - 1) // GRAN + 1])


KB = 1024
IN_GROUP_COLS = [0, 512, 1024, 1536, 2048, 2560, 3072, 3584, 4096]


def build_program(debug=(), n_ctx_blocks=40, n_own_blocks=8):
    nc = bass.Bass("TRN2", target_bir_lowering=False)
    S = Sched()
    dbg_outs = {}

    def din(name, shape):
        return nc.dram_tensor(name, list(shape), F32, kind="ExternalInput")

    x_own = din("x_own", [NOWN, D])
    p_own = din("p_own", [NOWN, DPLE])
    x_ctx = din("x_ctx", [NCTX, D])
    pos_all = din("pos_all", [NOWN + NCTX, 2])
    masks_d = din("masks", [1, 400])
    g_pre_mix = din("g_pre_mix", [1, D])
    w_in = din("w_in", [D, DIN])
    g_q = din("g_q", [1, 128])
    g_k = din("g_k", [1, 128])
    w_gf_up = din("w_gf_up", [16, 512])
    b_gf = din("b_gf", [1, 512])
    w_gb_up = din("w_gb_up", [16, 512])
    b_gb = din("b_gb", [1, 512])
    g_gla_norm = din("g_gla_norm", [1, 256])
    w_out = din("w_out", [D, D])
    g_post_mix = din("g_post_mix", [1, D])
    g_pre_ffn = din("g_pre_ffn", [1, D])
    w_gate_up = din("w_gate_up", [D, 2 * DFF])
    w_down = din("w_down", [DFF, D])
    g_post_ffn = din("g_post_ffn", [1, D])
    g_ple_pre = din("g_ple_pre", [1, D])
    w_ple_gate = din("w_ple_gate", [D, D])
    w_ple_proj = din("w_ple_proj", [DPLE, D])
    g_ple_post = din("g_ple_post", [1, D])
    y_own = nc.dram_tensor("y_own", [NOWN, D], F32, kind="ExternalOutput")

    def dscr(name, shape, dt):
        return nc.dram_tensor(name, list(shape), dt, kind="Internal")

    win_r = dscr("win_r", [9, 128, 16 * 512], BF16)
    wgates_r = dscr("wgates_r", [128, 16 * 32], BF16)
    wout_r = dscr("wout_r", [4, 128, 16 * 512], BF16)
    wgu_r = dscr("wgu_r", [22, 128, 2 * 16 * 256], BF16)
    wdn_r = dscr("wdn_r", [16, 128, 11 * 512], BF16)
    wpg_r = dscr("wpg_r", [4, 128, 16 * 512], BF16)
    wpp_r = dscr("wpp_r", [128, 2 * 2048], BF16)
    kT_ctx = dscr("kT_ctx", [2, 128, NCTX], BF16)
    v_r = dscr("v_r", [2, 128, 160 * 128], BF16)
    rope_d = dscr("rope_d", [NOWN + NCTX, 128], F32)
    sfb_d = dscr("sfb_d", [2, 5, 128, 1024], F32)

    B_win = [Buf("win%d" % i) for i in range(9)]
    B_wgates = Buf("wgates_r")
    B_wout = [Buf("wout%d" % i) for i in range(4)]
    B_wgu = [Buf("wgu%d" % i) for i in range(22)]
    B_wdn = [Buf("wdn%d" % i) for i in range(16)]
    B_wpg = [Buf("wpg%d" % i) for i in range(4)]
    B_wpp = Buf("wpp")
    B_kT = [Buf("kTctx%d" % i) for i in range(40)]
    B_v = [Buf("vctx%d" % i) for i in range(40)]
    B_rope = [Buf("rope%d" % i) for i in range(48)]
    B_sfb = [[Buf("sfb%d_%d" % (s, j)) for j in range(5)] for s in range(2)]
    B_yout = Buf("yout")

    es = ExitStack()
    with es:
        def sbt(name, shape, dt):
            t = es.enter_context(nc.sbuf_tensor(name, list(shape), dt))
            return Tile(t[:], [Buf(name)])

        add = S.add

        def TT(out, in0, in1, op, r, w, eng="dve"):
            add(eng, lambda e: e.tensor_tensor(out=out, in0=in0, in1=in1, op=op), r, w)

        def TS(out, in0, s1, op0, r, w, s2=None, op1=None, eng="dve"):
            if op1 is None:
                add(eng, lambda e: e.tensor_scalar(out=out, in0=in0, scalar1=s1, scalar2=None, op0=op0), r, w)
            else:
                add(eng, lambda e: e.tensor_scalar(out=out, in0=in0, scalar1=s1, scalar2=s2, op0=op0, op1=op1), r, w)

        def STT(out, in0, scalar, in1, op0, op1, r, w):
            add("dve", lambda e: e.scalar_tensor_tensor(out=out, in0=in0, scalar=scalar, in1=in1, op0=op0, op1=op1), r, w)

        def ACT(out, in_, func, r, w, scale=None, bias=None, accum=None):
            kw = {}
            if scale is not None:
                kw["scale"] = scale
            if bias is not None:
                kw["bias"] = bias
            if accum is not None:
                kw["accum_out"] = accum
            add("act", lambda e: e.activation(out=out, in_=in_, func=func, **kw), r, w)

        def MM(out, lhsT, rhs, start, stop, r, w):
            add("pe", lambda e: e.matmul(out, lhsT=lhsT, rhs=rhs, start=start, stop=stop), r, w)

        def TR(out, in_, r, w):
            add("pe", lambda e: e.transpose(out=out, in_=in_, identity=ident.ap), list(r) + [ident], w)

        def DMA(q, out, in_, r, w, key, slow=False):
            if slow:
                add(q, lambda e: e.dma_start(out=out, in_=in_, allow_slow_non_contiguous=True), r, w, key)
            else:
                add(q, lambda e: e.dma_start(out=out, in_=in_), r, w, key)

        ident = sbt("ident", [128, 128], BF16)
        ones_f = sbt("ones_f", [128, 512], F32)
        ones_b = sbt("ones_b", [128, 128], BF16)
        maskf = sbt("maskf", [128, 128], F32)
        maskb = sbt("maskb", [128, 128], F32)
        rmask = sbt("rmask", [128, 512], F32)
        g_fm = sbt("g_fm", [128, 3, 16], F32)
        gq_bc = sbt("gq_bc", [128, 128], F32)
        gk_bc = sbt("gk_bc", [128, 128], F32)
        ggla_bc = sbt("ggla_bc", [128, 256], F32)
        nbg = sbt("nbg", [128, 2, 4], F32)
        wg_up = sbt("wg_up", [16, 2, 512], BF16)
        masks = sbt("masks_sb", [128, 40, 10], F32)
        negshift = sbt("negshift", [128, 1], F32)
        wgates = sbt("wgates", [128, 16, 32], BF16)
        stat_t = es.enter_context(nc.sbuf_tensor("stat", [128, 1104], F32))
        stat2_t = es.enter_context(nc.sbuf_tensor("stat2", [128, 1104], F32))
        small = sbt("small", [128, 64], F32)
        posT = sbt("posT", [128, 192, 2], F32)
        freq4 = sbt("freq4", [128, 128], F32)
        phase4 = sbt("phase4", [128, 128], F32)
        fi = sbt("fi", [128, 32], I32)
        dacc = sbt("dacc", [128, 4, 4], F32)
        dtmp = sbt("dtmp", [128, 4, 4], F32)
        dtmp2 = sbt("dtmp2", [128, 4, 4], F32)
        Sf_own = sbt("Sf_own", [128, 4, 256], F32)
        dtl = sbt("dtl", [128, 2, 4, 4], F32)

        T16a = Arena(nc, es, "T16a", 16 * KB)
        T16b = Arena(nc, es, "T16b", 16 * KB)
        W = Arena(nc, es, "W", 48 * KB)
        R = Arena(nc, es, "R", 86 * KB)
        XN = Arena(nc, es, "XN", 8 * KB)

        hTa = T16a.view(0, [16, 512], BF16)
        hTb = T16b.view(0, [16, 512], BF16)
        wslot = [W.view(i * 16 * KB, [16, 512], BF16) for i in range(3)]
        xn_bf = [XN.view(i * 4 * KB, [2048], BF16) for i in range(2)]
        gtmp = XN.view(0, [2048], F32)

        banks = []
        dbanks = []
        for k in range(4):
            t = es.enter_context(nc.psum_tensor("dbank%d" % k, [128, 1024], F32))
            b0, b1 = Buf("bank%d" % (2 * k)), Buf("bank%d" % (2 * k + 1))
            banks.append(Tile(t[:, 0:512], [b0]))
            banks.append(Tile(t[:, 512:1024], [b1]))
            dbanks.append(Tile(t[:], [b0, b1]))

        def bank_bf(i):
            return banks[i].ap.bitcast(BF16)

        stat_col = [0]
        stat_memset = [None]

        def new_stat(n=1):
            c = stat_col[0]
            stat_col[0] += n
            assert stat_col[0] <= 1104
            b1 = Buf("st%d" % c)
            b1.lw = stat_memset[0]
            return Tile(stat_t[:, c:c + n], [b1]), Tile(stat2_t[:, c:c + n], [Buf("st2_%d" % c)])

        def dbg(name, tile, ap, shape):
            if name not in debug:
                return
            t = nc.dram_tensor("dbg_" + name, list(shape), F32, kind="ExternalOutput")
            dbg_outs[name] = shape
            add("pool", lambda e: e.dma_start(out=t.ap(), in_=ap), reads=[tile], dma="dbg_" + name)
            S.final.append("dbg_" + name)

        add("pool", lambda e: e.memset(ones_f.ap, 1.0), writes=[ones_f])
        add("pool", lambda e: e.memset(ones_b.ap, 1.0), writes=[ones_b])
        stat_memset[0] = add("pool", lambda e: e.memset(stat_t[:], 0.0), writes=[])
        add("pool", lambda e: e.memset(rmask.ap, 1.0), writes=[rmask])
        add("pool", lambda e: e.memset(rmask.ap.rearrange("p (c t) -> p c t", t=128)[:, :, 0:1], 0.0), writes=[rmask])
        add("pool", lambda e: e.affine_select(out=maskf.ap, in_=ones_f.ap[:, 0:128], pattern=[[1, 128]], compare_op=ALU.is_ge,
                                              fill=0.0, base=0, channel_multiplier=-1), reads=[ones_f], writes=[maskf])
        add("pool", lambda e: e.affine_select(out=maskb.ap, in_=ones_f.ap[:, 0:128], pattern=[[-1, 128]], compare_op=ALU.is_ge,
                                              fill=0.0, base=-1, channel_multiplier=1), reads=[ones_f], writes=[maskb])
        add("pool", lambda e: e.affine_select(out=freq4.ap, in_=ones_f.ap[:, 0:128], pattern=[[-1, 128]], compare_op=ALU.is_equal,
                                              fill=0.0, base=0, channel_multiplier=1), reads=[ones_f], writes=[freq4])
        add("dve", lambda e: e.tensor_copy(out=ident.ap, in_=freq4.ap), reads=[freq4], writes=[ident])

        add("sp", lambda e: e.dma_start(out=gq_bc.ap, in_=g_q.ap().partition_broadcast(128)[:, 0, :]), writes=[gq_bc], dma="c0")
        add("sp", lambda e: e.dma_start(out=gk_bc.ap, in_=g_k.ap().partition_broadcast(128)[:, 0, :]), writes=[gk_bc], dma="c0")
        add("sp", lambda e: e.dma_start(out=ggla_bc.ap, in_=g_gla_norm.ap().partition_broadcast(128)[:, 0, :]), writes=[ggla_bc], dma="c0")
        add("sp", lambda e: e.dma_start(out=masks.ap.rearrange("p a b -> p (a b)"), in_=masks_d.ap().partition_broadcast(128)[:, 0, :]),
            writes=[masks], dma="c0")
        for i, gsrc in enumerate((g_pre_mix, g_pre_ffn, g_ple_pre)):
            add("sp", lambda e, i=i, gsrc=gsrc: e.dma_start(out=g_fm.ap[:, i, :], in_=gsrc.ap().rearrange("o (c p) -> p (o c)", p=128),
                                                            allow_slow_non_contiguous=True), writes=[g_fm], dma="c0")
        for i, bsrc in enumerate((b_gf, b_gb)):
            add("sp", lambda e, i=i, bsrc=bsrc: e.dma_start(out=nbg.ap[:, i, :], in_=bsrc.ap().rearrange("o (c p) -> p (o c)", p=128),
                                                            allow_slow_non_contiguous=True), writes=[nbg], dma="c0")
        add("dve", lambda e: e.tensor_scalar(out=nbg.ap, in0=nbg.ap, scalar1=-1.0, scalar2=None, op0=ALU.mult), reads=[nbg], writes=[nbg])
        add("sp", lambda e: e.dma_start(out=posT.ap, in_=pos_all.ap().rearrange("(t p) c -> p t c", p=128),
                                        allow_slow_non_contiguous=True), writes=[posT], dma="c0")
        add("pool", lambda e: e.dma_start(out=wg_up.ap[:, 0, :], in_=w_gf_up.ap()), writes=[wg_up], dma="c1")
        add("pool", lambda e: e.dma_start(out=wg_up.ap[:, 1, :], in_=w_gb_up.ap()), writes=[wg_up], dma="c1")

        conv_list = []
        win_v = w_in.ap().rearrange("(kc p) n -> p kc n", p=128)
        conv_list.append((wgates_r.ap().rearrange("p (kc n) -> p kc n", n=32), win_v[:, :, 4608:4640], B_wgates))
        for gi in (2, 4, 5, 6, 3, 7, 8, 0, 1):
            c0 = IN_GROUP_COLS[gi]
            conv_list.append((win_r.ap()[gi].rearrange("p (kc n) -> p kc n", n=512), win_v[:, :, c0:c0 + 512], B_win[gi]))
        wout_v = w_out.ap().rearrange("(kc p) n -> p kc n", p=128)
        for cg in range(4):
            conv_list.append((wout_r.ap()[cg].rearrange("p (kc n) -> p kc n", n=512), wout_v[:, :, cg * 512:(cg + 1) * 512], B_wout[cg]))
        wgu_v = w_gate_up.ap().rearrange("(kc p) n -> p kc n", p=128)
        for f in range(22):
            dst = wgu_r.ap()[f].rearrange("p (s kc n) -> p s kc n", s=2, n=256)
            conv_list.append((dst[:, 0], wgu_v[:, :, f * 256:(f + 1) * 256], B_wgu[f]))
            conv_list.append((dst[:, 1], wgu_v[:, :, DFF + f * 256:DFF + (f + 1) * 256], B_wgu[f]))
        for half in range(2):
            for cg in range(4):
                for part in range(2):
                    idx = (half * 4 + cg) * 2 + part
                    r0 = half * 2816 + part * 1408
                    src = w_down.ap()[r0:r0 + 1408, cg * 512:(cg + 1) * 512].rearrange("(f p) n -> p f n", p=128)
                    conv_list.append((wdn_r.ap()[idx].rearrange("p (f n) -> p f n", n=512), src, B_wdn[idx]))
        wpg_v = w_ple_gate.ap().rearrange("(kc p) n -> p kc n", p=128)
        for cg in range(4):
            conv_list.append((wpg_r.ap()[cg].rearrange("p (kc n) -> p kc n", n=512), wpg_v[:, :, cg * 512:(cg + 1) * 512], B_wpg[cg]))
        conv_list.append((wpp_r.ap().rearrange("p (kc n) -> p kc n", n=2048), w_ple_proj.ap().rearrange("(kc p) n -> p kc n", p=128), B_wpp))
        conv_pos = [0]

        def do_conv(n):
            for _ in range(n):
                if conv_pos[0] >= len(conv_list):
                    return
                dst, src, bw = conv_list[conv_pos[0]]
                conv_pos[0] += 1
                add("pool", lambda e, dst=dst, src=src: e.dma_start(out=dst, in_=src), writes=[bw], dma="conv")

        do_conv(10)
        add("sp", lambda e: e.dma_start(out=wgates.ap, in_=wgates_r.ap().rearrange("p (kc n) -> p kc n", n=32)),
            reads=[B_wgates], writes=[wgates], dma="c2")

        add("dve", lambda e: e.tensor_reduce(out=small.ap[0:1, 0:1], in_=gq_bc.ap[0:1, :], axis=AX.X, op=ALU.max, apply_absolute_value=True),
            reads=[gq_bc], writes=[small])
        add("dve", lambda e: e.tensor_reduce(out=small.ap[0:1, 1:2], in_=gk_bc.ap[0:1, :], axis=AX.X, op=ALU.max, apply_absolute_value=True),
            reads=[gk_bc], writes=[small])
        add("dve", lambda e: e.tensor_tensor(out=small.ap[0:1, 2:3], in0=small.ap[0:1, 0:1], in1=small.ap[0:1, 1:2], op=ALU.mult),
            reads=[small], writes=[small])
        add("dve", lambda e: e.tensor_scalar(out=small.ap[0:1, 3:4], in0=small.ap[0:1, 2:3], scalar1=-float(np.sqrt(128.0)), scalar2=None, op0=ALU.mult),
            reads=[small], writes=[small])
        add("pe", lambda e: e.matmul(banks[0].ap[:, 0:1], lhsT=ones_f.ap[0:1, 0:128], rhs=small.ap[0:1, 3:4], start=True, stop=True),
            reads=[ones_f, small], writes=[banks[0]])
        add("dve", lambda e: e.tensor_copy(out=negshift.ap, in_=banks[0].ap[:, 0:1]), reads=[banks[0]], writes=[negshift])

        add("pool", lambda e: e.iota(fi.ap, pattern=[[1, 32]], base=0, channel_multiplier=0), writes=[fi])
        for q4 in range(4):
            add("dve", lambda e, q4=q4: e.tensor_copy(out=freq4.ap[:, q4 * 32:(q4 + 1) * 32], in_=fi.ap), reads=[fi, ident], writes=[freq4])
        add("act", lambda e: e.activation(out=freq4.ap, in_=freq4.ap, func=AF.Exp, scale=-float(np.log(10000.0)) / 32.0), reads=[freq4], writes=[freq4])
        add("pool", lambda e: e.memset(phase4.ap, 0.0), writes=[phase4])
        add("pool", lambda e: e.memset(phase4.ap[:, 32:64], PI / 2), writes=[phase4])
        add("pool", lambda e: e.memset(phase4.ap[:, 96:128], PI / 2), writes=[phase4])
        ang = R.view(0, [4, 128], F32)
        angi = R.view(2 * KB, [4, 128], I32)
        angm = R.view(4 * KB, [4, 128], F32)
        n_own_tiles = n_own_blocks * 4
        rope_groups = list(range(n_own_blocks)) + [8 + b for b in range(n_ctx_blocks)]
        for gi in rope_groups:
            for t in range(4):
                T = gi * 4 + t
                add("dve", lambda e, T=T, t=t: e.scalar_tensor_tensor(out=ang.ap[:, t, 0:64], in0=freq4.ap[:, 0:64], scalar=posT.ap[:, T, 0:1],
                                                                      in1=phase4.ap[:, 0:64], op0=ALU.mult, op1=ALU.add),
                    reads=[freq4, posT, phase4], writes=[ang])
                add("dve", lambda e, T=T, t=t: e.scalar_tensor_tensor(out=ang.ap[:, t, 64:128], in0=freq4.ap[:, 64:128], scalar=posT.ap[:, T, 1:2],
                                                                      in1=phase4.ap[:, 64:128], op0=ALU.mult, op1=ALU.add),
                    reads=[freq4, posT, phase4], writes=[ang])
            add("dve", lambda e: e.tensor_scalar(out=angi.ap, in0=ang.ap, scalar1=1.0 / (2 * PI), scalar2=None, op0=ALU.mult), reads=[ang], writes=[angi])
            add("dve", lambda e: e.scalar_tensor_tensor(out=ang.ap, in0=angi.ap, scalar=-2 * PI, in1=ang.ap, op0=ALU.mult, op1=ALU.add),
                reads=[angi, ang], writes=[ang])
            add("dve", lambda e: e.tensor_scalar(out=angm.ap, in0=ang.ap, scalar1=PI, scalar2=2 * PI, op0=ALU.is_gt, op1=ALU.mult), reads=[ang], writes=[angm])
            add("dve", lambda e: e.tensor_tensor(out=ang.ap, in0=ang.ap, in1=angm.ap, op=ALU.subtract), reads=[ang, angm], writes=[ang])
            add("dve", lambda e: e.tensor_scalar(out=angm.ap, in0=ang.ap, scalar1=-PI, scalar2=2 * PI, op0=ALU.is_lt, op1=ALU.mult), reads=[ang], writes=[angm])
            add("dve", lambda e: e.tensor_tensor(out=ang.ap, in0=ang.ap, in1=angm.ap, op=ALU.add), reads=[ang, angm], writes=[ang])
            add("dve", lambda e: e.tensor_scalar(out=ang.ap, in0=ang.ap, scalar1=-3.14159, scalar2=3.14159, op0=ALU.max, op1=ALU.min), reads=[ang], writes=[ang])
            add("act", lambda e: e.activation(out=ang.ap, in_=ang.ap, func=AF.Sin), reads=[ang], writes=[ang])
            add("sp", lambda e, gi=gi: e.dma_start(out=rope_d.ap()[gi * 512:(gi + 1) * 512, :].rearrange("(t p) c -> p t c", p=128), in_=ang.ap),
                reads=[ang], writes=[B_rope[gi]], dma="ropew")

        def rmsnorm_to_hT(xt, dst_hT, t, gidx, xn_slot, pbanks):
            s1, s2 = new_stat()
            add("act", lambda e: e.activation(out=xn_slot.ap, in_=xt.ap, func=AF.Square, accum_out=s1.ap), reads=[xt], writes=[xn_slot, s1])
            add("act", lambda e: e.activation(out=s2.ap, in_=s1.ap, func=AF.Ln, scale=1.0 / D, bias=EPS), reads=[s1], writes=[s2])
            add("act", lambda e: e.activation(out=s2.ap, in_=s2.ap, func=AF.Exp, scale=-0.5), reads=[s2], writes=[s2])
            add("dve", lambda e: e.tensor_scalar(out=xn_slot.ap, in0=xt.ap, scalar1=s2.ap, scalar2=None, op0=ALU.mult),
                reads=[xt, s2], writes=[xn_slot])
            for half in range(2):
                pb = pbanks[half]
                for k8 in range(8):
                    kc = half * 8 + k8
                    add("pe", lambda e, kc=kc, k8=k8, pb=pb: e.transpose(out=bank_bf(pb)[:, k8 * 128:(k8 + 1) * 128],
                                                                         in_=xn_slot.ap[:, kc * 128:(kc + 1) * 128], identity=ident.ap),
                        reads=[xn_slot, ident], writes=[banks[pb]])
                add("dve", lambda e, half=half, pb=pb: e.tensor_tensor(
                    out=dst_hT.ap[:, half * 8:(half + 1) * 8, t * 128:(t + 1) * 128],
                    in0=bank_bf(pb).rearrange("p (a b) -> p a b", a=8),
                    in1=g_fm.ap[:, gidx, half * 8:(half + 1) * 8].unsqueeze(2).to_broadcast([128, 8, 128]), op=ALU.mult),
                    reads=[banks[pb], g_fm], writes=[dst_hT])

        def qk_norm_rope(srcT, src_ap, H, g_bc, ropeT, rope_ap, outT, out_ap, tmpA, tmpB):
            src3 = src_ap.rearrange("p (h d) -> p h d", h=H)
            A3 = tmpA.ap.rearrange("p (h d) -> p h d", h=H)
            s1, s2 = new_stat(H)
            add("act", lambda e: e.activation(out=A3, in_=src3, func=AF.Square), reads=[srcT], writes=[tmpA])
            add("dve", lambda e: e.tensor_reduce(out=s2.ap, in_=A3, axis=AX.X, op=ALU.add), reads=[tmpA], writes=[s2])
            add("act", lambda e: e.activation(out=s2.ap, in_=s2.ap, func=AF.Ln, scale=1.0 / 128, bias=EPS), reads=[s2], writes=[s2])
            add("act", lambda e: e.activation(out=s2.ap, in_=s2.ap, func=AF.Exp, scale=-0.5), reads=[s2], writes=[s2])
            add("dve", lambda e: e.tensor_tensor(out=A3, in0=src3, in1=s2.ap.unsqueeze(2).to_broadcast([128, H, 128]), op=ALU.mult),
                reads=[srcT, s2], writes=[tmpA])
            add("dve", lambda e: e.tensor_tensor(out=A3, in0=A3, in1=g_bc.ap.unsqueeze(1).to_broadcast([128, H, 128]), op=ALU.mult),
                reads=[tmpA, g_bc], writes=[tmpA])
            x5 = tmpA.ap.rearrange("p (h a b f) -> p h a b f", h=H, a=2, b=2)
            t5 = tmpB.ap.rearrange("p (h a b f) -> p h a b f", h=H, a=2, b=2)
            o5 = out_ap.rearrange("p h (a b f) -> p h a b f", a=2, b=2)
            r4 = rope_ap.rearrange("p (a b f) -> p a b f", a=2, b=2)
            sin_b = r4[:, :, 0, :].unsqueeze(1).to_broadcast([128, H, 2, 32])
            cos_b = r4[:, :, 1, :].unsqueeze(1).to_broadcast([128, H, 2, 32])
            x1 = x5[:, :, :, 0, :]
            x2 = x5[:, :, :, 1, :]
            add("dve", lambda e: e.tensor_tensor(out=t5[:, :, :, 0, :], in0=x2, in1=sin_b, op=ALU.mult), reads=[tmpA, ropeT], writes=[tmpB])
            add("dve", lambda e: e.tensor_tensor(out=t5[:, :, :, 1, :], in0=x1, in1=sin_b, op=ALU.mult), reads=[tmpA, ropeT], writes=[tmpB])
            add("dve", lambda e: e.tensor_tensor(out=x1, in0=x1, in1=cos_b, op=ALU.mult), reads=[tmpA, ropeT], writes=[tmpA])
            add("dve", lambda e: e.tensor_tensor(out=x2, in0=x2, in1=cos_b, op=ALU.mult), reads=[tmpA, ropeT], writes=[tmpA])
            add("dve", lambda e: e.tensor_tensor(out=o5[:, :, :, 0, :], in0=x1, in1=t5[:, :, :, 0, :], op=ALU.subtract), reads=[tmpA, tmpB], writes=[outT])
            add("dve", lambda e: e.tensor_tensor(out=o5[:, :, :, 1, :], in0=x2, in1=t5[:, :, :, 1, :], op=ALU.add), reads=[tmpA, tmpB], writes=[outT])

        def softplus_neg(ps_tile, ps_ap, dirn, h, Ldst):
            add("act", lambda e: e.activation(out=Ldst.ap, in_=ps_ap, func=AF.Exp, scale=-1.0, bias=nbg.ap[:, dirn, h:h + 1]),
                reads=[ps_tile, nbg], writes=[Ldst])
            add("act", lambda e: e.activation(out=Ldst.ap, in_=Ldst.ap, func=AF.Ln, bias=1.0), reads=[Ldst], writes=[Ldst])

        def gates_low(hT_t, pb, glow):
            for dirn in range(2):
                for kc in range(16):
                    add("pe", lambda e, kc=kc, dirn=dirn: e.matmul(banks[pb].ap[0:16, :], lhsT=wgates.ap[:, kc, dirn * 16:(dirn + 1) * 16],
                                                                  rhs=hT_t.ap[:, kc, :], start=(kc == 0), stop=(kc == 15)),
                        reads=[wgates, hT_t], writes=[banks[pb]])
                add("act", lambda e, dirn=dirn: e.activation(out=glow.ap[:, dirn, :], in_=banks[pb].ap[0:16, :], func=AF.Copy),
                    reads=[banks[pb]], writes=[glow])

        def mm_group(pb_ap, pbT, lhs_fn, rhs_fn, n, reads):
            for kc in range(n):
                MM(pb_ap, lhs_fn(kc), rhs_fn(kc), kc == 0, kc == n - 1, reads, [pbT])

        wkv = R.view(0, [16, 512], BF16)
        xs_a = [R.view(16 * KB + i * 8 * KB, [2048], F32) for i in range(2)]
        o = 32 * KB
        ropeA = [R.view(o + i * 2 * KB, [4, 128], F32) for i in range(2)]; o += 4 * KB
        kvA = R.view(o, [256], F32); o += 1 * KB
        kvB = R.view(o, [256], F32); o += 1 * KB
        k_bf = R.view(o, [2, 128], BF16); o += 512
        v_bf = R.view(o, [2, 128], BF16); o += 512
        kT_blk = R.view(o, [2, 512], BF16); o += 2 * KB
        glowA = R.view(o, [2, 512], BF16, parts=16); o += 2 * KB
        LtD = [R.view(o + i * 2 * KB, [512], F32) for i in range(2)]; o += 4 * KB
        CtD = [R.view(o + i * 2 * KB, [512], F32) for i in range(2)]; o += 4 * KB
        kt_fm = [R.view(o + i * KB, [512], BF16) for i in range(2)]; o += 2 * KB
        kt_tm = [R.view(o + i * KB, [4, 128], BF16) for i in range(2)]; o += 2 * KB
        v_tmA = R.view(o, [4, 1024], BF16); o += 8 * KB
        Sf_st = R.view(o, [4, 256], F32); o += 4 * KB
        Sb_st = R.view(o, [4, 4, 256], F32); o += 16 * KB
        assert o <= 86 * KB, o
        smallD = [sbt("smallD%d" % i, [128, 8], F32) for i in range(2)]

        def load_w(slot_view, slotT, src_ap, srcbuf, key):
            add("sp", lambda e: e.dma_start(out=slot_view, in_=src_ap), reads=[srcbuf], writes=[slotT], dma=key)

        def win_src(gi):
            return win_r.ap()[gi].rearrange("p (kc n) -> p kc n", n=512)

        if n_ctx_blocks > 0:
            load_w(wkv.ap, wkv, win_src(2), B_win[2], "wA")
            load_w(wslot[0].ap, wslot[0], win_src(4), B_win[4], "wA")
            load_w(wslot[1].ap, wslot[1], win_src(5), B_win[5], "wA")
            load_w(wslot[2].ap, wslot[2], win_src(6), B_win[6], "wA")

        def init_states():
            add("pool", lambda e: e.memset(Sf_st.ap, 0.0), writes=[Sf_st])
            add("pool", lambda e: e.memset(Sb_st.ap, 0.0), writes=[Sb_st])
            add("pool", lambda e: e.memset(dacc.ap, 1.0), writes=[dacc])

        def spill_states(seq):
            add("sp", lambda e: e.dma_start(out=sfb_d.ap()[seq, 0].rearrange("p (h v) -> p h v", h=4), in_=Sf_st.ap),
                reads=[Sf_st], writes=[B_sfb[seq][0]], dma="spill")
            for j in range(4):
                add("sp", lambda e, j=j: e.dma_start(out=sfb_d.ap()[seq, 1 + j].rearrange("p (h v) -> p h v", h=4), in_=Sb_st.ap[:, :, j, :]),
                    reads=[Sb_st], writes=[B_sfb[seq][1 + j]], dma="spill")

        def run_interleaved(gens):
            gens = list(gens)
            while gens:
                for g_ in list(gens):
                    try:
                        next(g_)
                    except StopIteration:
                        gens.remove(g_)

        def gen_F(B):
            hT_t = hTa if B % 2 == 0 else hTb
            r0 = NOWN + B * 512
            rp = ropeA[B % 2]
            DMA("sp", rp.ap, rope_d.ap()[r0:r0 + 512, :].rearrange("(t p) c -> p t c", p=128), [B_rope[r0 // 512]], [rp], "ropeA%d" % (B % 2))
            for t in range(4):
                T = B * 4 + t
                xt = xs_a[T % 2]
                DMA("sp", xt.ap, x_ctx.ap()[T * 128:(T + 1) * 128, :], [], [xt], "xa%d" % (T % 2))
                yield
                rmsnorm_to_hT(xt, hT_t, t, 0, xn_bf[T % 2], (0, 0))
                yield
            if B == 0:
                dbg("hT0", hT_t, hT_t.ap[:, 0, :], [128, 512])

        def gen_KV(B):
            hT_t = hTa if B % 2 == 0 else hTb
            rp = ropeA[B % 2]
            for t in range(4):
                mm_group(banks[3].ap, banks[3], lambda kc: hT_t.ap[:, kc, t * 128:(t + 1) * 128], lambda kc: wkv.ap[:, kc, :], 16, [hT_t, wkv])
                yield
                ACT(v_bf.ap, banks[3].ap[:, 256:512].rearrange("p (g d) -> p g d", g=2), AF.Copy, [banks[3]], [v_bf])
                qk_norm_rope(banks[3], banks[3].ap[:, 0:256], 2, gk_bc, rp, rp.ap[:, t, :], k_bf, k_bf.ap, kvA, kvB)
                yield
                for g in range(2):
                    TR(bank_bf(0)[:, g * 128:(g + 1) * 128], k_bf.ap[:, g, :], [k_bf], [banks[0]])
                ACT(kT_blk.ap[:, :, t * 128:(t + 1) * 128], bank_bf(0)[:, 0:256].rearrange("p (g d) -> p g d", g=2), AF.Copy, [banks[0]], [kT_blk])
                chunk = B * 4 + t
                DMA("sp", v_r.ap()[:, :, chunk * 128:(chunk + 1) * 128].rearrange("g p d -> p g d"), v_bf.ap, [v_bf], [B_v[B]], "vst")
                yield
            DMA("sp", kT_ctx.ap()[:, :, B * 512:(B + 1) * 512].rearrange("g p n -> p g n"), kT_blk.ap, [kT_blk], [B_kT[B]], "kst")
            if B == 0:
                dbg("kT0", kT_blk, kT_blk.ap[:, 0, :], [128, 512])

        def gen_GKd(B, h, dirn, pk):
            pb = banks[5 + dirn]
            Lt, Ct, sd = LtD[dirn], CtD[dirn], smallD[dirn]
            mB = masks.ap[:, B, :]
            MM(pb.ap, wg_up.ap[:, dirn, h * 128:(h + 1) * 128], glowA.ap[:, dirn, :], True, True, [wg_up, glowA], [pb])
            softplus_neg(pb, pb.ap, dirn, h, Lt)
            yield
            add("dve", lambda e: e.tensor_tensor_scan(out=Ct.ap, data0=ones_f.ap, data1=Lt.ap, initial=0.0, op0=ALU.mult, op1=ALU.add),
                [ones_f, Lt], [Ct])
            TS(sd.ap[:, 0:1], Ct.ap[:, 511:512], -1.0 / 16, ALU.mult, [Ct], [sd])
            ACT(sd.ap[:, 1:2], sd.ap[:, 0:1], AF.Exp, [sd], [sd])
            yield
            if dirn == 0:
                ACT(Ct.ap, Ct.ap, AF.Exp, [Ct, sd], [Ct], scale=1.0 / 16, bias=sd.ap[:, 0:1])
            else:
                TT(Ct.ap, Ct.ap, Lt.ap, ALU.subtract, [Ct, Lt], [Ct])
                ACT(Ct.ap, Ct.ap, AF.Exp, [Ct], [Ct], scale=-1.0 / 16)
            TT(kt_fm[dirn].ap, banks[pk].ap, Ct.ap, ALU.mult, [banks[pk], Ct], [kt_fm[dirn]])
            yield
            for t in range(4):
                TR(bank_bf(5 + dirn)[:, t * 128:(t + 1) * 128], kt_fm[dirn].ap[:, t * 128:(t + 1) * 128], [kt_fm[dirn]], [pb])
            ACT(kt_tm[dirn].ap, bank_bf(5 + dirn)[:, 0:512].rearrange("p (t d) -> p t d", t=4), AF.Copy, [pb], [kt_tm[dirn]])
            yield
            kv = pb.ap[:, 256:512]
            for t in range(4):
                MM(kv, kt_tm[dirn].ap[:, t, :], v_tmA.ap[:, t, h * 256:(h + 1) * 256], t == 0, t == 3, [kt_tm[dirn], v_tmA], [pb])
            yield
            if dirn == 0:
                STT(sd.ap[:, 2:3], sd.ap[:, 1:2], mB[:, 0:1], mB[:, 1:2], ALU.mult, ALU.add, [sd, masks], [sd])
                TS(Sf_st.ap[:, h, :], Sf_st.ap[:, h, :], sd.ap[:, 2:3], ALU.mult, [Sf_st, sd], [Sf_st])
                STT(Sf_st.ap[:, h, :], kv, mB[:, 0:1], Sf_st.ap[:, h, :], ALU.mult, ALU.add, [pb, masks, Sf_st], [Sf_st])
            else:
                TT(dtmp.ap[:, h, :], dacc.ap[:, h, :], mB[:, 2:6], ALU.mult, [dacc, masks], [dtmp])
                for j in range(4):
                    STT(Sb_st.ap[:, h, j, :], kv, dtmp.ap[:, h, j:j + 1], Sb_st.ap[:, h, j, :], ALU.mult, ALU.add, [pb, dtmp, Sb_st], [Sb_st])
                    if j == 1:
                        yield
                STT(dtmp2.ap[:, h, :], mB[:, 2:6], sd.ap[:, 1:2], mB[:, 6:10], ALU.mult, ALU.add, [masks, sd], [dtmp2])
                TT(dacc.ap[:, h, :], dacc.ap[:, h, :], dtmp2.ap[:, h, :], ALU.mult, [dacc, dtmp2], [dacc])
            yield

        def gen_GV(B):
            hT_t = hTa if B % 2 == 0 else hTb
            gates_low(hT_t, 1, glowA)
            yield
            for t in range(4):
                for c2 in range(2):
                    pb = banks[1 + c2]
                    mm_group(pb.ap, pb, lambda kc: hT_t.ap[:, kc, t * 128:(t + 1) * 128], lambda kc: wslot[1 + c2].ap[:, kc, :], 16, [hT_t, wslot[1 + c2]])
                    ACT(v_tmA.ap[:, t, c2 * 512:(c2 + 1) * 512], pb.ap, AF.Copy, [pb], [v_tmA])
                    yield

        def gen_G(B):
            hT_t = hTa if B % 2 == 0 else hTb
            if B == 8:
                spill_states(0)
                init_states()
            yield
            yield
            for h in range(4):
                pk = 4 if h % 2 == 0 else 7
                mm_group(banks[pk].ap, banks[pk], lambda kc: wslot[0].ap[:, kc, h * 128:(h + 1) * 128], lambda kc: hT_t.ap[:, kc, :], 16, [hT_t, wslot[0]])
                yield
                g0 = gen_GKd(B, h, 0, pk)
                g1 = gen_GKd(B, h, 1, pk)
                live = [g0, g1]
                while live:
                    for g_ in list(live):
                        try:
                            next(g_)
                        except StopIteration:
                            live.remove(g_)
                    yield

        if n_ctx_blocks > 0:
            init_states()
            run_interleaved([gen_F(0)])
        for B in range(n_ctx_blocks):
            do_conv(3)
            gens = [gen_GV(B), gen_G(B), gen_KV(B)]
            if B + 1 < n_ctx_blocks:
                gens.append(gen_F(B + 1))
            run_interleaved(gens)
        if n_ctx_blocks > 8:
            spill_states(1)
        elif n_ctx_blocks > 0:
            spill_states(0)
        if n_ctx_blocks > 0:
            dbg("Sf_last", Sf_st, Sf_st.ap[:, 0, :], [128, 256])
            dbg("Sb_last", Sb_st, Sb_st.ap[:, 0, 0, :], [128, 256])
        do_conv(1000)

        o = 0
        qT = R.view(o, [8, 512], BF16); o += 8 * KB
        ropeO = R.view(o, [4, 128], F32); o += 2 * KB
        tA = R.view(o, [512], F32); o += 2 * KB
        tB = R.view(o, [512], F32); o += 2 * KB
        q_bf = R.view(o, [4, 128], BF16); o += 1 * KB
        glowO = R.view(o, [2, 512], BF16, parts=16); o += 2 * KB
        att_base = o
        Lg = [R.view(o + i * 2 * KB, [512], F32) for i in range(2)]; o += 4 * KB
        Cg = [R.view(o + i * 2 * KB, [512], F32) for i in range(2)]; o += 4 * KB
        E1 = R.view(o, [512], F32); o += 2 * KB
        E2 = R.view(o, [512], F32); o += 2 * KB
        qd = [R.view(o + i * 4 * KB, [4, 512], BF16) for i in range(2)]; o += 8 * KB
        kd = [R.view(o + i * 4 * KB, [4, 512], BF16) for i in range(2)]; o += 8 * KB
        v_tm = R.view(o, [4, 1024], BF16); o += 8 * KB
        gsil = R.view(o, [4, 1024], BF16); o += 8 * KB
        ktm = [R.view(o + i * KB, [4, 128], BF16) for i in range(2)]; o += 2 * KB
        Tb_bf = R.view(o, [4, 4, 256], BF16); o += 8 * KB
        Sf_bf = R.view(o, [4, 256], BF16); o += 2 * KB
        Sb_cur = R.view(o, [4, 256], F32); o += 4 * KB
        Am = [R.view(o + i * KB, [4, 128], BF16) for i in range(2)]; o += 2 * KB
        mix_bf = R.view(o, [1024], BF16); o += 2 * KB
        assert o <= 86 * KB, o
        xs_o = [T16b.view(i * 8 * KB, [2048], F32) for i in range(2)]
        o = att_base
        kts = [R.view(o + i * 2 * KB, [1024], BF16) for i in range(3)]; o += 6 * KB
        vts = [R.view(o + i * 2 * KB, [8, 128], BF16) for i in range(3)]; o += 6 * KB
        PTP = [R.view(o + i * 2 * KB, [1024], BF16) for i in range(4)]; o += 8 * KB
        accs = R.view(o, [1024], F32); o += 4 * KB
        rsum = R.view(o, [512], F32, parts=1); o += 2 * KB
        bcs = R.view(o, [512], F32); o += 2 * KB
        xs2 = [R.view(i * 8 * KB, [2048], F32) for i in range(4)]
        Y = [R.view(32 * KB + i * 8 * KB, [2048], F32) for i in range(4)]
        Hh = R.view(64 * KB, [22, 512], BF16)
        junk2 = R.view(64 * KB, [2048], BF16)
        p32 = R.view(64 * KB, [4, 256], F32)
        p_bf = R.view(68 * KB, [4, 256], BF16)
        pT = R.view(70 * KB, [2, 512], BF16)
        sgm = [R.view(72 * KB + i * 2 * KB, [512], F32) for i in range(2)]
        wpp_sb = R.view(76 * KB, [2, 2048], BF16)
        sgt = [XN.view(i * 2 * KB, [512], F32) for i in range(2)]
        wslot_gu = [W.view(i * 16 * KB, [2, 16, 256], BF16) for i in range(3)]
        wslot_dn = [W.view(i * 16 * KB, [11, 512], BF16) for i in range(3)]

        blk_uses = []
        for gi in (3, 4, 5, 6, 7, 8, 0, 1):
            blk_uses.append(("in", gi))
        for cg in range(4):
            blk_uses.append(("out", cg))
        for half in range(2):
            for f in range(11):
                blk_uses.append(("gu", half * 11 + f))
            for cg in range(4):
                for part in range(2):
                    blk_uses.append(("dn", (half * 4 + cg) * 2 + part))
        for cg in range(4):
            blk_uses.append(("pg", cg))
        NU = len(blk_uses)
        ws_issued = [0]
        total_uses = NU * n_own_blocks

        def ws_issue(i):
            kind, idx = blk_uses[i % NU]
            sl = i % 3
            key = "ws%d" % sl
            if kind == "in":
                DMA("sp", wslot[sl].ap, win_src(idx), [B_win[idx]], [wslot[sl]], key)
            elif kind == "out":
                DMA("sp", wslot[sl].ap, wout_r.ap()[idx].rearrange("p (kc n) -> p kc n", n=512), [B_wout[idx]], [wslot[sl]], key)
            elif kind == "gu":
                DMA("sp", wslot_gu[sl].ap, wgu_r.ap()[idx].rearrange("p (s kc n) -> p s kc n", s=2, n=256), [B_wgu[idx]], [wslot_gu[sl]], key)
            elif kind == "dn":
                DMA("sp", wslot_dn[sl].ap, wdn_r.ap()[idx].rearrange("p (f n) -> p f n", n=512), [B_wdn[idx]], [wslot_dn[sl]], key)
            elif kind == "pg":
                DMA("sp", wslot[sl].ap, wpg_r.ap()[idx].rearrange("p (kc n) -> p kc n", n=512), [B_wpg[idx]], [wslot[sl]], key)

        def ws_get(i):
            while ws_issued[0] <= i:
                ws_issue(ws_issued[0])
                ws_issued[0] += 1
            return i % 3

        QS = float(128.0 ** -0.5)

        for j in range(n_own_blocks):
            seq = j // 4
            jj = j % 4
            r0 = j * 512
            cb = 0 if seq == 0 else 32
            nkc = 32 if seq == 0 else 128
            ub = j * NU
            DMA("sp", Sb_cur.ap, sfb_d.ap()[seq, 1 + jj].rearrange("p (h v) -> p h v", h=4), [B_sfb[seq][1 + jj]], [Sb_cur], "sbl")
            if jj == 0:
                DMA("sp", Sf_own.ap, sfb_d.ap()[seq, 0].rearrange("p (h v) -> p h v", h=4), [B_sfb[seq][0]], [Sf_own], "sbl")
            DMA("sp", ropeO.ap, rope_d.ap()[r0:r0 + 512, :].rearrange("(t p) c -> p t c", p=128), [B_rope[j]], [ropeO], "ropeO")
            for t in range(4):
                xt = xs_o[t % 2]
                DMA("sp", xt.ap, x_own.ap()[r0 + t * 128:r0 + (t + 1) * 128, :], [], [xt], "xo%d" % (t % 2))
                rmsnorm_to_hT(xt, hTa, t, 0, xn_bf[t % 2], (0, 1))
            if j == 0:
                dbg("hTo", hTa, hTa.ap[:, 0, :], [128, 512])
            gates_low(hTa, 5, glowO)
            sq = ws_get(ub + 0)
            sk = ws_get(ub + 1)
            for h in range(4):
                for dirn in range(2):
                    MM(banks[4].ap, wg_up.ap[:, dirn, h * 128:(h + 1) * 128], glowO.ap[:, dirn, :], True, True, [wg_up, glowO], [banks[4]])
                    softplus_neg(banks[4], banks[4].ap, dirn, h, Lg[dirn])
                    add("dve", lambda e, dirn=dirn: e.tensor_tensor_scan(out=Cg[dirn].ap, data0=rmask.ap, data1=Lg[dirn].ap, initial=0.0,
                                                                         op0=ALU.mult, op1=ALU.add), [rmask, Lg[dirn]], [Cg[dirn]])
                mm_group(banks[2].ap, banks[2], lambda kc: wslot[sq].ap[:, kc, h * 128:(h + 1) * 128], lambda kc: hTa.ap[:, kc, :], 16, [hTa, wslot[sq]])
                mm_group(banks[3].ap, banks[3], lambda kc: wslot[sk].ap[:, kc, h * 128:(h + 1) * 128], lambda kc: hTa.ap[:, kc, :], 16, [hTa, wslot[sk]])
                ACT(E1.ap, Cg[0].ap, AF.Exp, [Cg[0]], [E1], scale=-1.0 / 16)
                ACT(E2.ap, Cg[0].ap, AF.Exp, [Cg[0]], [E2], scale=1.0 / 16)
                STT(qd[0].ap[:, h, :], banks[2].ap, QS, E1.ap, ALU.mult, ALU.mult, [banks[2], E1], [qd[0]])
                TT(kd[0].ap[:, h, :], banks[3].ap, E2.ap, ALU.mult, [banks[3], E2], [kd[0]])
                add("dve", lambda e, h=h: e.tensor_copy(out=dtl.ap[:, 0, h, :], in_=E1.ap.rearrange("p (t c) -> p t c", c=128)[:, :, 127]), [E1], [dtl])
                ACT(dtl.ap[:, 1, h, :], Cg[1].ap.rearrange("p (t c) -> p t c", c=128)[:, :, 127], AF.Exp, [Cg[1]], [dtl], scale=-1.0 / 16)
                TT(E1.ap, Cg[1].ap, Lg[1].ap, ALU.subtract, [Cg[1], Lg[1]], [E1])
                ACT(E2.ap, E1.ap, AF.Exp, [E1], [E2], scale=1.0 / 16)
                ACT(E1.ap, E1.ap, AF.Exp, [E1], [E1], scale=-1.0 / 16)
                STT(qd[1].ap[:, h, :], banks[2].ap, QS, E2.ap, ALU.mult, ALU.mult, [banks[2], E2], [qd[1]])
                TT(kd[1].ap[:, h, :], banks[3].ap, E1.ap, ALU.mult, [banks[3], E1], [kd[1]])
            sv = [ws_get(ub + 2), ws_get(ub + 3)]
            for t in range(4):
                for c2 in range(2):
                    pb = 6 + c2
                    mm_group(banks[pb].ap, banks[pb], lambda kc: hTa.ap[:, kc, t * 128:(t + 1) * 128], lambda kc: wslot[sv[c2]].ap[:, kc, :], 16,
                             [hTa, wslot[sv[c2]]])
                    ACT(v_tm.ap[:, t, c2 * 512:(c2 + 1) * 512], banks[pb].ap, AF.Copy, [banks[pb]], [v_tm])
            sg_ = [ws_get(ub + 4), ws_get(ub + 5)]
            for t in range(4):
                for c2 in range(2):
                    pb = 6 + c2
                    mm_group(banks[pb].ap, banks[pb], lambda kc: hTa.ap[:, kc, t * 128:(t + 1) * 128], lambda kc: wslot[sg_[c2]].ap[:, kc, :], 16,
                             [hTa, wslot[sg_[c2]]])
                    ACT(tA.ap, banks[pb].ap, AF.Silu, [banks[pb]], [tA])
                    TT(gsil.ap[:, t, c2 * 512:(c2 + 1) * 512].rearrange("p (a b) -> p a b", a=2), tA.ap.rearrange("p (a b) -> p a b", a=2),
                       ggla_bc.ap.unsqueeze(1).to_broadcast([128, 2, 256]), ALU.mult, [tA, ggla_bc], [gsil])
            for t in (3, 2, 1, 0):
                kt = ktm[t % 2]
                for h in range(4):
                    TR(bank_bf(5)[:, h * 128:(h + 1) * 128], kd[1].ap[:, h, t * 128:(t + 1) * 128], [kd[1]], [banks[5]])
                ACT(kt.ap, bank_bf(5)[:, 0:512].rearrange("p (a b) -> p a b", a=4), AF.Copy, [banks[5]], [kt])
                for h in range(4):
                    pbk = banks[6 + h // 2]
                    MM(pbk.ap[:, (h % 2) * 256:(h % 2 + 1) * 256], kt.ap[:, h, :], v_tm.ap[:, t, h * 256:(h + 1) * 256], True, True, [kt, v_tm], [pbk])
                for h in range(4):
                    pbk = banks[6 + h // 2]
                    ACT(Tb_bf.ap[:, t, h, :], Sb_cur.ap[:, h, :], AF.Copy, [Sb_cur, dtl], [Tb_bf], scale=dtl.ap[:, 1, h, t:t + 1])
                    STT(Sb_cur.ap[:, h, :], Sb_cur.ap[:, h, :], dtl.ap[:, 1, h, t:t + 1], pbk.ap[:, (h % 2) * 256:(h % 2 + 1) * 256], ALU.mult, ALU.add,
                        [Sb_cur, dtl, pbk], [Sb_cur])
            for t in range(4):
                kt = ktm[t % 2]
                am = Am[t % 2]
                for h in range(4):
                    TR(bank_bf(5)[:, h * 128:(h + 1) * 128], kd[0].ap[:, h, t * 128:(t + 1) * 128], [kd[0]], [banks[5]])
                ACT(kt.ap, bank_bf(5)[:, 0:512].rearrange("p (a b) -> p a b", a=4), AF.Copy, [banks[5]], [kt])
                for h in range(4):
                    pbk = banks[6 + h // 2]
                    MM(pbk.ap[:, (h % 2) * 256:(h % 2 + 1) * 256], kt.ap[:, h, :], v_tm.ap[:, t, h * 256:(h + 1) * 256], True, True, [kt, v_tm], [pbk])
                ACT(Sf_bf.ap, Sf_own.ap, AF.Copy, [Sf_own], [Sf_bf])
                for h in range(4):
                    MM(banks[2].ap[:, h * 128:(h + 1) * 128], kd[0].ap[:, h, t * 128:(t + 1) * 128], qd[0].ap[:, h, t * 128:(t + 1) * 128], True, True,
                       [kd[0], qd[0]], [banks[2]])
                    MM(banks[3].ap[:, h * 128:(h + 1) * 128], kd[1].ap[:, h, t * 128:(t + 1) * 128], qd[1].ap[:, h, t * 128:(t + 1) * 128], True, True,
                       [kd[1], qd[1]], [banks[3]])
                TT(tA.ap.rearrange("p (a b) -> p a b", a=4), banks[2].ap.rearrange("p (a b) -> p a b", a=4),
                   maskf.ap.unsqueeze(1).to_broadcast([128, 4, 128]), ALU.mult, [banks[2], maskf], [tA])
                TT(tB.ap.rearrange("p (a b) -> p a b", a=4), banks[3].ap.rearrange("p (a b) -> p a b", a=4),
                   maskb.ap.unsqueeze(1).to_broadcast([128, 4, 128]), ALU.mult, [banks[3], maskb], [tB])
                TT(am.ap.rearrange("p a b -> p (a b)"), tA.ap, tB.ap, ALU.add, [tA, tB], [am])
                for h in range(4):
                    pbo = banks[h // 2]
                    oap = pbo.ap[:, (h % 2) * 256:(h % 2 + 1) * 256]
                    MM(oap, am.ap[:, h, :], v_tm.ap[:, t, h * 256:(h + 1) * 256], True, False, [am, v_tm], [pbo])
                    MM(oap, qd[0].ap[:, h, t * 128:(t + 1) * 128], Sf_bf.ap[:, h, :], False, False, [qd[0], Sf_bf], [pbo])
                    MM(oap, qd[1].ap[:, h, t * 128:(t + 1) * 128], Tb_bf.ap[:, t, h, :], False, True, [qd[1], Tb_bf], [pbo])
                for h in range(4):
                    pbk = banks[6 + h // 2]
                    TT(Sf_own.ap[:, h, :], Sf_own.ap[:, h, :], pbk.ap[:, (h % 2) * 256:(h % 2 + 1) * 256], ALU.add, [Sf_own, pbk], [Sf_own])
                    TS(Sf_own.ap[:, h, :], Sf_own.ap[:, h, :], dtl.ap[:, 0, h, t:t + 1], ALU.mult, [Sf_own, dtl], [Sf_own])
                s1, s2 = new_stat(4)
                for h in range(4):
                    pbo = banks[h // 2]
                    oap = pbo.ap[:, (h % 2) * 256:(h % 2 + 1) * 256]
                    ACT(tB.ap[:, 0:256], oap, AF.Square, [pbo], [tB, s1], accum=s1.ap[:, h:h + 1])
                ACT(s2.ap, s1.ap, AF.Ln, [s1], [s2], scale=1.0 / 256, bias=EPS)
                ACT(s2.ap, s2.ap, AF.Exp, [s2], [s2], scale=-0.5)
                for h in range(4):
                    pbo = banks[h // 2]
                    oap = pbo.ap[:, (h % 2) * 256:(h % 2 + 1) * 256]
                    STT(mix_bf.ap[:, h * 256:(h + 1) * 256], oap, s2.ap[:, h:h + 1], gsil.ap[:, t, h * 256:(h + 1) * 256], ALU.mult, ALU.mult,
                        [pbo, s2, gsil], [mix_bf])
                if j == 0 and t == 0:
                    dbg("gla0", mix_bf, mix_bf.ap[:, 0:512], [128, 512])
                for c in range(8):
                    TR(bank_bf(4)[:, c * 128:(c + 1) * 128], mix_bf.ap[:, c * 128:(c + 1) * 128], [mix_bf], [banks[4]])
                ACT(hTb.ap[:, 8:16, t * 128:(t + 1) * 128], bank_bf(4).rearrange("p (a b) -> p a b", a=8), AF.Copy, [banks[4]], [hTb])
            sa = [ws_get(ub + 6), ws_get(ub + 7)]
            for t in range(4):
                for cg in range(2):
                    pb = 2 + cg
                    mm_group(banks[pb].ap, banks[pb], lambda kc: hTa.ap[:, kc, t * 128:(t + 1) * 128], lambda kc: wslot[sa[cg]].ap[:, kc, :], 16,
                             [hTa, wslot[sa[cg]]])
                    qk_norm_rope(banks[pb], banks[pb].ap, 4, gq_bc, ropeO, ropeO.ap[:, t, :], q_bf, q_bf.ap, tA, tB)
                    for h4 in range(4):
                        TR(bank_bf(5)[:, h4 * 128:(h4 + 1) * 128], q_bf.ap[:, h4, :], [q_bf], [banks[5]])
                    ACT(qT.ap[:, cg * 4:(cg + 1) * 4, t * 128:(t + 1) * 128], bank_bf(5)[:, 0:512].rearrange("p (a b) -> p a b", a=4), AF.Copy,
                        [banks[5]], [qT])
            if j == 0:
                dbg("qT0", qT, qT.ap[:, 0, :], [128, 512])
            ngrp = nkc // 8
            kv_n = [0]
            for g in range(2):
                for pair in range(2):
                    heads = (4 * g + 2 * pair, 4 * g + 2 * pair + 1)
                    niter = ngrp * 8
                    slot_of = {}

                    def stage_S(n):
                        G, c = n // 8, n % 8
                        if c == 0:
                            sl = kv_n[0] % 3
                            kv_n[0] += 1
                            slot_of[G] = sl
                            c0 = cb + G * 8
                            blks = [B_kT[c0 // 4], B_kT[c0 // 4 + 1], B_v[c0 // 4], B_v[c0 // 4 + 1]]
                            DMA("sp", kts[sl].ap, kT_ctx.ap()[g, :, c0 * 128:(c0 + 8) * 128], blks, [kts[sl]], "kv%d" % sl)
                            DMA("sp", vts[sl].ap, v_r.ap()[g, :, c0 * 128:(c0 + 8) * 128].rearrange("p (c d) -> p c d", d=128), blks, [vts[sl]],
                                "kv%d" % sl)
                        sl = slot_of[G]
                        k3 = 1 + n % 3
                        db = dbanks[k3]
                        for i in range(2):
                            MM(db.ap[:, i * 512:(i + 1) * 512], kts[sl].ap[:, c * 128:(c + 1) * 128], qT.ap[:, heads[i], :], True, True,
                               [kts[sl], qT], [banks[2 * k3 + i]])

                    def stage_EV(n):
                        G, c = n // 8, n % 8
                        sl = slot_of[G]
                        db = dbanks[1 + n % 3]
                        pt = PTP[n % 4]
                        ACT(pt.ap, db.ap, AF.Exp, [db, negshift], [pt], scale=QS, bias=negshift.ap)
                        first = (n == 0)
                        last = (n == niter - 1)
                        for i in range(2):
                            MM(banks[i].ap, vts[sl].ap[:, c, :], pt.ap[:, i * 512:(i + 1) * 512], first, last, [vts[sl], pt], [banks[i]])
                        if first:
                            add("dve", lambda e, pt=pt: e.tensor_copy(out=accs.ap, in_=pt.ap), [pt], [accs])
                        else:
                            TT(accs.ap, accs.ap, pt.ap, ALU.add, [accs, pt], [accs])

                    stage_S(0)
                    if niter > 1:
                        stage_S(1)
                    for n in range(niter):
                        if n + 2 < niter:
                            stage_S(n + 2)
                        stage_EV(n)
                    for i in range(2):
                        hq = heads[i]
                        MM(banks[2 + i].ap[0:1, :], ones_f.ap[:, 0:1], accs.ap[:, i * 512:(i + 1) * 512], True, True, [ones_f, accs], [banks[2 + i]])
                        ACT(rsum.ap, banks[2 + i].ap[0:1, :], AF.Ln, [banks[2 + i]], [rsum])
                        ACT(rsum.ap, rsum.ap, AF.Exp, [rsum], [rsum], scale=-1.0)
                        MM(banks[4 + i].ap, ones_f.ap[0:1, 0:128], rsum.ap, True, True, [ones_f, rsum], [banks[4 + i]])
                        ACT(bcs.ap, banks[4 + i].ap, AF.Copy, [banks[4 + i]], [bcs])
                        TT(hTb.ap[:, hq, :], banks[i].ap, bcs.ap, ALU.mult, [banks[i], bcs], [hTb])
            if j == 0:
                dbg("mixT_a", hTb, hTb.ap[:, 0, :], [128, 512])
                dbg("mixT_g", hTb, hTb.ap[:, 8, :], [128, 512])
            for t in range(4):
                DMA("sp", xs2[t].ap, x_own.ap()[r0 + t * 128:r0 + (t + 1) * 128, :], [], [xs2[t]], "xs2_%d" % t)
            n = 0
            for cg in range(4):
                sl = ws_get(ub + 8 + cg)
                for t in range(4):
                    pb = banks[n % 8]
                    n += 1
                    mm_group(pb.ap, pb, lambda kc: hTb.ap[:, kc, t * 128:(t + 1) * 128], lambda kc: wslot[sl].ap[:, kc, :], 16, [hTb, wslot[sl]])
                    ACT(Y[t].ap[:, cg * 512:(cg + 1) * 512], pb.ap, AF.Copy, [pb], [Y[t]])

            def post_norm(gsrc):
                DMA("sp", gtmp.ap, gsrc.ap().partition_broadcast(128)[:, 0, :], [], [gtmp], "gtmp")
                for t in range(4):
                    s1, s2 = new_stat()
                    ACT(junk2.ap, Y[t].ap, AF.Square, [Y[t]], [junk2, s1], accum=s1.ap)
                    ACT(s2.ap, s1.ap, AF.Ln, [s1], [s2], scale=1.0 / D, bias=EPS)
                    ACT(s2.ap, s2.ap, AF.Exp, [s2], [s2], scale=-0.5)
                    STT(Y[t].ap, Y[t].ap, s2.ap, gtmp.ap, ALU.mult, ALU.mult, [Y[t], s2, gtmp], [Y[t]])
                    TT(xs2[t].ap, xs2[t].ap, Y[t].ap, ALU.add, [xs2[t], Y[t]], [xs2[t]])

            post_norm(g_post_mix)
            if j == 0:
                dbg("x1", xs2[0], xs2[0].ap[:, 0:512], [128, 512])
            for t in range(4):
                rmsnorm_to_hT(xs2[t], hTa, t, 1, xn_bf[t % 2], (0, 1))
            ui = ub + 12
            nff = 0
            for half in range(2):
                for f in range(11):
                    sl = ws_get(ui)
                    ui += 1
                    for c in range(2):
                        pg_ = banks[(nff % 2) * 2]
                        pu_ = banks[(nff % 2) * 2 + 1]
                        st_ = sgt[nff % 2]
                        nff += 1
                        mm_group(pg_.ap, pg_, lambda kc: wslot_gu[sl].ap[:, 0, kc, c * 128:(c + 1) * 128], lambda kc: hTa.ap[:, kc, :], 16,
                                 [hTa, wslot_gu[sl]])
                        mm_group(pu_.ap, pu_, lambda kc: wslot_gu[sl].ap[:, 1, kc, c * 128:(c + 1) * 128], lambda kc: hTa.ap[:, kc, :], 16,
                                 [hTa, wslot_gu[sl]])
                        ACT(st_.ap, pg_.ap, AF.Silu, [pg_], [st_])
                        TT(Hh.ap[:, f * 2 + c, :], st_.ap, pu_.ap, ALU.mult, [st_, pu_], [Hh])
                for cg in range(4):
                    for part in range(2):
                        sl = ws_get(ui)
                        ui += 1
                        for t in range(4):
                            pd = banks[4 + t]
                            for ffc in range(11):
                                MM(pd.ap, Hh.ap[:, part * 11 + ffc, t * 128:(t + 1) * 128], wslot_dn[sl].ap[:, ffc, :],
                                   part == 0 and ffc == 0, part == 1 and ffc == 10, [Hh, wslot_dn[sl]], [pd])
                    for t in range(4):
                        pd = banks[4 + t]
                        if half == 0:
                            ACT(Y[t].ap[:, cg * 512:(cg + 1) * 512], pd.ap, AF.Copy, [pd], [Y[t]])
                        else:
                            TT(Y[t].ap[:, cg * 512:(cg + 1) * 512], pd.ap, Y[t].ap[:, cg * 512:(cg + 1) * 512], ALU.add, [pd, Y[t]], [Y[t]])
            post_norm(g_post_ffn)
            if j == 0:
                dbg("x2", xs2[0], xs2[0].ap[:, 0:512], [128, 512])
            for t in range(4):
                rmsnorm_to_hT(xs2[t], hTa, t, 2, xn_bf[t % 2], (0, 1))
            DMA("sp", p32.ap, p_own.ap()[r0:r0 + 512, :].rearrange("(t p) c -> p t c", p=128), [], [p32], "p32")
            DMA("sp", wpp_sb.ap, wpp_r.ap().rearrange("p (kc n) -> p kc n", n=2048), [B_wpp], [wpp_sb], "wpp")
            ACT(p_bf.ap, p32.ap, AF.Copy, [p32], [p_bf])
            for t in range(4):
                for c in range(2):
                    TR(bank_bf(0)[:, c * 128:(c + 1) * 128], p_bf.ap[:, t, c * 128:(c + 1) * 128], [p_bf], [banks[0]])
                ACT(pT.ap[:, :, t * 128:(t + 1) * 128], bank_bf(0)[:, 0:256].rearrange("p (a b) -> p a b", a=2), AF.Copy, [banks[0]], [pT])
            n = 0
            for cg in range(4):
                sl = ws_get(ub + 50 + cg)
                for t in range(4):
                    pgb = banks[n % 4]
                    ppb = banks[4 + n % 4]
                    sg2 = sgm[n % 2]
                    n += 1
                    mm_group(pgb.ap, pgb, lambda kc: hTa.ap[:, kc, t * 128:(t + 1) * 128], lambda kc: wslot[sl].ap[:, kc, :], 16, [hTa, wslot[sl]])
                    mm_group(ppb.ap, ppb, lambda kc: pT.ap[:, kc, t * 128:(t + 1) * 128], lambda kc: wpp_sb.ap[:, kc, cg * 512:(cg + 1) * 512], 2,
                             [pT, wpp_sb])
                    ACT(sg2.ap, pgb.ap, AF.Sigmoid, [pgb], [sg2])
                    TT(Y[t].ap[:, cg * 512:(cg + 1) * 512], sg2.ap, ppb.ap, ALU.mult, [sg2, ppb], [Y[t]])
            post_norm(g_ple_post)
            for t in range(4):
                DMA("sp", y_own.ap()[r0 + t * 128:r0 + (t + 1) * 128, :], xs2[t].ap, [xs2[t]], [B_yout], "yout")

        S.final.append("yout")
        S.emit(nc, es)
    return nc, dbg_outs


_PROG = {}


def _core_inputs(c, inp):
    p, hf = c // 2, c % 2
    xp = inp["x_prompt"]
    xsm = inp["x_sample"]
    pp = inp["p_prompt"][0]
    psm = inp["p_sample"][0]
    d = {}
    d["x_own"] = np.ascontiguousarray(np.concatenate([xp[p, hf * 2048:(hf + 1) * 2048], xsm[0, c * 2048:(c + 1) * 2048]], 0), dtype=np.float32)
    d["p_own"] = np.ascontiguousarray(np.concatenate([pp[p, hf * 2048:(hf + 1) * 2048], psm[0, c * 2048:(c + 1) * 2048]], 0), dtype=np.float32)
    d["x_ctx"] = np.ascontiguousarray(np.concatenate([xp[p], xsm[0]], 0), dtype=np.float32)
    tok = np.concatenate([hf * 2048 + np.arange(2048), c * 2048 + np.arange(2048), np.arange(4096), np.arange(16384)])
    d["pos_all"] = np.stack([tok // 64, tok % 64], 1).astype(np.float32)
    m = np.zeros((40, 10), np.float32)
    for B in range(40):
        if B < 8:
            b, start = B, hf * 4
        else:
            b, start = B - 8, c * 4
        mf = 1.0 if b < start else 0.0
        m[B, 0] = mf
        m[B, 1] = 1.0 - mf
        for j in range(4):
            mb = 1.0 if b > start + j else 0.0
            m[B, 2 + j] = mb
            m[B, 6 + j] = 1.0 - mb
    d["masks"] = m.reshape(1, 400)
    for k in ("g_pre_mix", "w_in", "g_q", "g_k", "w_gf_up", "b_gf", "w_gb_up", "b_gb", "g_gla_norm", "w_out", "g_post_mix", "g_pre_ffn",
              "w_gate_up", "w_down", "g_post_ffn", "g_ple_pre", "w_ple_gate", "w_ple_proj", "g_ple_post"):
        a = np.asarray(inp[k], dtype=np.float32)[0]
        if a.ndim == 1:
            a = a[None, :]
        d[k] = np.ascontiguousarray(a)
    return d


def kernel(**inputs):
    inp = {k: np.asarray(v) for k, v in inputs.items()}
    if "nc" not in _PROG:
        _PROG["nc"] = build_program()[0]
    nc = _PROG["nc"]
    in_maps = [_core_inputs(c, inp) for c in range(8)]
    res = run_bass_kernel_spmd(nc, in_maps, core_ids=list(range(8)))
    y_prompt = np.zeros((4, 4096, D), np.float32)
    y_sample = np.zeros((1, 16384, D), np.float32)
    for c in range(8):
        y = np.asarray(res.results[c]["y_own"], dtype=np.float32)
        y_prompt[c // 2, (c % 2) * 2048:(c % 2 + 1) * 2048] = y[:2048]
        y_sample[0, c * 2048:(c + 1) * 2048] = y[2048:]
    return (y_prompt, y_sample)
```

```python
import os
import numpy as np
from contextlib import ExitStack
import concourse.bass as bass
import concourse.mybir as mybir
from concourse.bass_utils import run_bass_kernel_spmd

F32 = mybir.dt.float32
BF16 = mybir.dt.bfloat16
I32 = mybir.dt.int32
AF = mybir.ActivationFunctionType
ALU = mybir.AluOpType
AX = mybir.AxisListType

D = 2048
DIN = 4640
DFF = 5632
DPLE = 256
NOWN = 4096
NCTX = 20480
EPS = 1e-6
PI = float(np.pi)


class Buf:
    __slots__ = ("name", "lw", "rd")

    def __init__(self, name):
        self.name = name
        self.lw = None
        self.rd = {}


class Tile:
    __slots__ = ("ap", "bufs")

    def __init__(self, ap, bufs):
        self.ap = ap
        self.bufs = bufs


class Op:
    __slots__ = ("eng", "fn", "waits", "signal", "semkey", "pos", "semval", "isdma")


def _flat(lst):
    out = []
    for x in lst:
        if x is None:
            continue
        if isinstance(x, Buf):
            out.append(x)
        elif isinstance(x, Tile):
            out.extend(x.bufs)
        else:
            out.extend(_flat(x))
    return out


class Sched:
    ENGS = ("pe", "dve", "act", "pool", "sp")

    def __init__(self):
        self.streams = {e: [] for e in self.ENGS}
        self.dma_count = {}
        self.seen = {e: {} for e in self.ENGS}
        self.final = []

    def add(self, eng, fn, reads=(), writes=(), dma=None):
        reads = _flat(reads)
        writes = _flat(writes)
        op = Op()
        op.eng = eng
        op.fn = fn
        op.signal = False
        op.isdma = dma is not None
        op.waits = []
        op.semval = None
        stream = self.streams[eng]
        deps = {}
        seen = self.seen[eng]
        pe_compute = (eng == "pe") and not op.isdma

        def need(d):
            if d is None:
                return
            if pe_compute and (not d.isdma) and d.eng == "pe":
                return
            k = d.semkey
            p = self.dma_count[k[1]] if d.isdma else d.pos
            if p <= seen.get(k, -1):
                return
            cur = deps.get(k)
            if cur is None or p > cur[0]:
                deps[k] = (p, d)
            elif (not d.isdma) and d.pos > cur[1].pos:
                deps[k] = (p, d)

        for b in reads:
            need(b.lw)
        for b in writes:
            need(b.lw)
            for r in b.rd.values():
                need(r)
        for k, (p, d) in deps.items():
            if d.isdma:
                op.waits.append((k[1], p * 16))
            else:
                d.signal = True
                op.waits.append(d)
            seen[k] = p
        if op.isdma:
            n = self.dma_count.get(dma, 0) + 1
            self.dma_count[dma] = n
            op.semkey = ("dma", dma)
            op.pos = n
        else:
            op.semkey = ("eng", eng)
            op.pos = len(stream)
        stream.append(op)
        for b in writes:
            b.lw = op
            b.rd = {}
        for b in reads:
            b.rd[op.semkey] = op
        return op

    def emit(self, nc, es):
        engsem = {e: es.enter_context(nc.semaphore("s_" + e)) for e in ("pe", "dve", "act", "pool")}
        dmasem = {k: es.enter_context(nc.semaphore("d_" + k)) for k in self.dma_count}
        for e, stream in self.streams.items():
            c = 0
            for op in stream:
                if (not op.isdma) and op.signal:
                    c += 1
                    op.semval = c
        block = es.enter_context(nc.Block())
        streams = self.streams
        final_waits = [(k, self.dma_count[k] * 16) for k in self.final if k in self.dma_count]

        def run(engname, eng):
            for op in streams[engname]:
                for w in op.waits:
                    if isinstance(w, Op):
                        eng.wait_ge(engsem[w.eng], w.semval)
                    else:
                        eng.wait_ge(dmasem[w[0]], w[1])
                inst = op.fn(eng)
                if op.isdma:
                    inst.then_inc(dmasem[op.semkey[1]], 16)
                elif op.signal:
                    inst.then_inc(engsem[engname], 1)

        @block.tensor
        def _(e):
            run("pe", e)

        @block.vector
        def _(e):
            run("dve", e)

        @block.scalar
        def _(e):
            run("act", e)

        @block.gpsimd
        def _(e):
            run("pool", e)

        @block.sync
        def _(e):
            run("sp", e)
            for k, v in final_waits:
                e.wait_ge(dmasem[k], v)


GRAN = 1024


class Arena:
    def __init__(self, nc, es, name, nbytes):
        assert nbytes % 4 == 0
        self.nbytes = nbytes
        self.t = es.enter_context(nc.sbuf_tensor(name, [128, nbytes // 4], F32))
        self.gr = [Buf("%s_%d" % (name, i)) for i in range((nbytes + GRAN - 1) // GRAN)]

    def view(self, off, shape, dt, parts=128):
        esz = 2 if dt == BF16 else 4
        n = 1
        for s in shape:
            n *= s
        nb = n * esz
        assert off % 4 == 0 and nb % 4 == 0 and off + nb <= self.nbytes, (off, nb, self.nbytes)
        ap = self.t[0:parts, off // 4:(off + nb) // 4]
        if dt != F32:
            ap = ap.bitcast(dt)
        if len(shape) == 2:
            ap = ap.rearrange("p (a b) -> p a b", a=shape[0])
        elif len(shape) == 3:
            ap = ap.rearrange("p (a b c) -> p a b c", a=shape[0], b=shape[1])
        elif len(shape) == 4:
            ap = ap.rearrange("p (a b c d) -> p a b c d", a=shape[0], b=shape[1], c=shape[2])
        return Tile(ap, self.gr[off // GRAN:(off + nb - 1) // GRAN + 1])


KB = 1024
IN_GROUP_COLS = [0, 512, 1024, 1536, 2048, 2560, 3072, 3584, 4096]


def build_program(debug=(), n_ctx_blocks=40, n_own_blocks=8):
    nc = bass.Bass("TRN2", target_bir_lowering=False)
    S = Sched()
    dbg_outs = {}

    def din(name, shape):
        return nc.dram_tensor(name, list(shape), F32, kind="ExternalInput")

    x_own = din("x_own", [NOWN, D])
    p_own = din("p_own", [NOWN, DPLE])
    x_ctx = din("x_ctx", [NCTX, D])
    pos_all = din("pos_all", [NOWN + NCTX, 2])
    masks_d = din("masks", [1, 400])
    g_pre_mix = din("g_pre_mix", [1, D])
    w_in = din("w_in", [D, DIN])
    g_q = din("g_q", [1, 128])
    g_k = din("g_k", [1, 128])
    w_gf_up = din("w_gf_up", [16, 512])
    b_gf = din("b_gf", [1, 512])
    w_gb_up = din("w_gb_up", [16, 512])
    b_gb = din("b_gb", [1, 512])
    g_gla_norm = din("g_gla_norm", [1, 256])
    w_out = din("w_out", [D, D])
    g_post_mix = din("g_post_mix", [1, D])
    g_pre_ffn = din("g_pre_ffn", [1, D])
    w_gate_up = din("w_gate_up", [D, 2 * DFF])
    w_down = din("w_down", [DFF, D])
    g_post_ffn = din("g_post_ffn", [1, D])
    g_ple_pre = din("g_ple_pre", [1, D])
    w_ple_gate = din("w_ple_gate", [D, D])
    w_ple_proj = din("w_ple_proj", [DPLE, D])
    g_ple_post = din("g_ple_post", [1, D])
    y_own = nc.dram_tensor("y_own", [NOWN, D], F32, kind="ExternalOutput")

    def dscr(name, shape, dt):
        return nc.dram_tensor(name, list(shape), dt, kind="Internal")

    win_r = dscr("win_r", [9, 128, 16 * 512], BF16)
    wgates_r = dscr("wgates_r", [128, 16 * 32], BF16)
    wout_r = dscr("wout_r", [4, 128, 16 * 512], BF16)
    wgu_r = dscr("wgu_r", [22, 128, 2 * 16 * 256], BF16)
    wdn_r = dscr("wdn_r", [16, 128, 11 * 512], BF16)
    wpg_r = dscr("wpg_r", [4, 128, 16 * 512], BF16)
    wpp_r = dscr("wpp_r", [128, 2 * 2048], BF16)
    kT_ctx = dscr("kT_ctx", [2, 128, NCTX], BF16)
    v_r = dscr("v_r", [2, 128, 160 * 128], BF16)
    rope_d = dscr("rope_d", [NOWN + NCTX, 128], F32)
    sfb_d = dscr("sfb_d", [2, 5, 128, 1024], F32)

    B_win = [Buf("win%d" % i) for i in range(9)]
    B_wgates = Buf("wgates_r")
    B_wout = [Buf("wout%d" % i) for i in range(4)]
    B_wgu = [Buf("wgu%d" % i) for i in range(22)]
    B_wdn = [Buf("wdn%d" % i) for i in range(16)]
    B_wpg = [Buf("wpg%d" % i) for i in range(4)]
    B_wpp = Buf("wpp")
    B_kT = [Buf("kTctx%d" % i) for i in range(40)]
    B_v = [Buf("vctx%d" % i) for i in range(40)]
    B_rope = [Buf("rope%d" % i) for i in range(48)]
    B_sfb = [[Buf("sfb%d_%d" % (s, j)) for j in range(5)] for s in range(2)]
    B_yout = Buf("yout")

    es = ExitStack()
    with es:
        def sbt(name, shape, dt):
            t = es.enter_context(nc.sbuf_tensor(name, list(shape), dt))
            return Tile(t[:], [Buf(name)])

        add = S.add

        def TT(out, in0, in1, op, r, w, eng="dve"):
            add(eng, lambda e: e.tensor_tensor(out=out, in0=in0, in1=in1, op=op), r, w)

        def TS(out, in0, s1, op0, r, w, s2=None, op1=None, eng="dve"):
            if op1 is None:
                add(eng, lambda e: e.tensor_scalar(out=out, in0=in0, scalar1=s1, scalar2=None, op0=op0), r, w)
            else:
                add(eng, lambda e: e.tensor_scalar(out=out, in0=in0, scalar1=s1, scalar2=s2, op0=op0, op1=op1), r, w)

        def STT(out, in0, scalar, in1, op0, op1, r, w):
            add("dve", lambda e: e.scalar_tensor_tensor(out=out, in0=in0, scalar=scalar, in1=in1, op0=op0, op1=op1), r, w)

        def ACT(out, in_, func, r, w, scale=None, bias=None, accum=None):
            kw = {}
            if scale is not None:
                kw["scale"] = scale
            if bias is not None:
                kw["bias"] = bias
            if accum is not None:
                kw["accum_out"] = accum
            add("act", lambda e: e.activation(out=out, in_=in_, func=func, **kw), r, w)

        def MM(out, lhsT, rhs, start, stop, r, w):
            add("pe", lambda e: e.matmul(out, lhsT=lhsT, rhs=rhs, start=start, stop=stop), r, w)

        def TR(out, in_, r, w):
            add("pe", lambda e: e.transpose(out=out, in_=in_, identity=ident.ap), list(r) + [ident], w)

        def DMA(q, out, in_, r, w, key, slow=False):
            if slow:
                add(q, lambda e: e.dma_start(out=out, in_=in_, allow_slow_non_contiguous=True), r, w, key)
            else:
                add(q, lambda e: e.dma_start(out=out, in_=in_), r, w, key)

        ident = sbt("ident", [128, 128], BF16)
        ones_f = sbt("ones_f", [128, 512], F32)
        ones_b = sbt("ones_b", [128, 128], BF16)
        maskf = sbt("maskf", [128, 128], F32)
        maskb = sbt("maskb", [128, 128], F32)
        rmask = sbt("rmask", [128, 512], F32)
        g_fm = sbt("g_fm", [128, 3, 16], F32)
        gq_bc = sbt("gq_bc", [128, 128], F32)
        gk_bc = sbt("gk_bc", [128, 128], F32)
        ggla_bc = sbt("ggla_bc", [128, 256], F32)
        nbg = sbt("nbg", [128, 2, 4], F32)
        wg_up = sbt("wg_up", [16, 2, 512], BF16)
        masks = sbt("masks_sb", [128, 40, 10], F32)
        negshift = sbt("negshift", [128, 1], F32)
        wgates = sbt("wgates", [128, 16, 32], BF16)
        stat_t = es.enter_context(nc.sbuf_tensor("stat", [128, 1060], F32))
        stat2_t = es.enter_context(nc.sbuf_tensor("stat2", [128, 1060], F32))
        small = sbt("small", [128, 16], F32)
        posT = sbt("posT", [128, 192, 2], F32)
        freq4 = sbt("freq4", [128, 128], F32)
        phase4 = sbt("phase4", [128, 128], F32)
        fi = sbt("fi", [128, 32], I32)
        dacc = sbt("dacc", [128, 4, 4], F32)
        dtmp = sbt("dtmp", [128, 4, 4], F32)
        dtmp2 = sbt("dtmp2", [128, 4, 4], F32)
        Sf_own = sbt("Sf_own", [128, 4, 256], F32)
        dtl = sbt("dtl", [128, 2, 4, 4], F32)

        T16a = Arena(nc, es, "T16a", 16 * KB)
        T16b = Arena(nc, es, "T16b", 16 * KB)
        W = Arena(nc, es, "W", 48 * KB)
        R = Arena(nc, es, "R", 86 * KB)
        XN = Arena(nc, es, "XN", 8 * KB)

        hTa = T16a.view(0, [16, 512], BF16)
        hTb = T16b.view(0, [16, 512], BF16)
        wslot = [W.view(i * 16 * KB, [16, 512], BF16) for i in range(3)]
        xn_bf = [XN.view(i * 4 * KB, [2048], BF16) for i in range(2)]
        gtmp = XN.view(0, [2048], F32)

        banks = []
        dbanks = []
        for k in range(4):
            t = es.enter_context(nc.psum_tensor("dbank%d" % k, [128, 1024], F32))
            b0, b1 = Buf("bank%d" % (2 * k)), Buf("bank%d" % (2 * k + 1))
            banks.append(Tile(t[:, 0:512], [b0]))
            banks.append(Tile(t[:, 512:1024], [b1]))
            dbanks.append(Tile(t[:], [b0, b1]))

        def bank_bf(i):
            return banks[i].ap.bitcast(BF16)

        stat_col = [0]
        stat_memset = [None]

        def new_stat(n=1):
            c = stat_col[0]
            stat_col[0] += n
            assert stat_col[0] <= 1060
            b1 = Buf("st%d" % c)
            b1.lw = stat_memset[0]
            return Tile(stat_t[:, c:c + n], [b1]), Tile(stat2_t[:, c:c + n], [Buf("st2_%d" % c)])

        def dbg(name, tile, ap, shape):
            if name not in debug:
                return
            t = nc.dram_tensor("dbg_" + name, list(shape), F32, kind="ExternalOutput")
            dbg_outs[name] = shape
            add("pool", lambda e: e.dma_start(out=t.ap(), in_=ap), reads=[tile], dma="dbg_" + name)
            S.final.append("dbg_" + name)

        add("pool", lambda e: e.memset(ones_f.ap, 1.0), writes=[ones_f])
        add("pool", lambda e: e.memset(ones_b.ap, 1.0), writes=[ones_b])
        stat_memset[0] = add("pool", lambda e: e.memset(stat_t[:], 0.0), writes=[])
        add("pool", lambda e: e.memset(rmask.ap, 1.0), writes=[rmask])
        add("pool", lambda e: e.memset(rmask.ap.rearrange("p (c t) -> p c t", t=128)[:, :, 0:1], 0.0), writes=[rmask])
        add("pool", lambda e: e.affine_select(out=maskf.ap, in_=ones_f.ap[:, 0:128], pattern=[[1, 128]], compare_op=ALU.is_ge,
                                              fill=0.0, base=0, channel_multiplier=-1), reads=[ones_f], writes=[maskf])
        add("pool", lambda e: e.affine_select(out=maskb.ap, in_=ones_f.ap[:, 0:128], pattern=[[-1, 128]], compare_op=ALU.is_ge,
                                              fill=0.0, base=-1, channel_multiplier=1), reads=[ones_f], writes=[maskb])
        add("pool", lambda e: e.affine_select(out=freq4.ap, in_=ones_f.ap[:, 0:128], pattern=[[-1, 128]], compare_op=ALU.is_equal,
                                              fill=0.0, base=0, channel_multiplier=1), reads=[ones_f], writes=[freq4])
        add("dve", lambda e: e.tensor_copy(out=ident.ap, in_=freq4.ap), reads=[freq4], writes=[ident])

        add("sp", lambda e: e.dma_start(out=gq_bc.ap, in_=g_q.ap().partition_broadcast(128)[:, 0, :]), writes=[gq_bc], dma="c0")
        add("sp", lambda e: e.dma_start(out=gk_bc.ap, in_=g_k.ap().partition_broadcast(128)[:, 0, :]), writes=[gk_bc], dma="c0")
        add("sp", lambda e: e.dma_start(out=ggla_bc.ap, in_=g_gla_norm.ap().partition_broadcast(128)[:, 0, :]), writes=[ggla_bc], dma="c0")
        add("sp", lambda e: e.dma_start(out=masks.ap.rearrange("p a b -> p (a b)"), in_=masks_d.ap().partition_broadcast(128)[:, 0, :]),
            writes=[masks], dma="c0")
        for i, gsrc in enumerate((g_pre_mix, g_pre_ffn, g_ple_pre)):
            add("sp", lambda e, i=i, gsrc=gsrc: e.dma_start(out=g_fm.ap[:, i, :], in_=gsrc.ap().rearrange("o (c p) -> p (o c)", p=128),
                                                            allow_slow_non_contiguous=True), writes=[g_fm], dma="c0")
        for i, bsrc in enumerate((b_gf, b_gb)):
            add("sp", lambda e, i=i, bsrc=bsrc: e.dma_start(out=nbg.ap[:, i, :], in_=bsrc.ap().rearrange("o (c p) -> p (o c)", p=128),
                                                            allow_slow_non_contiguous=True), writes=[nbg], dma="c0")
        add("dve", lambda e: e.tensor_scalar(out=nbg.ap, in0=nbg.ap, scalar1=-1.0, scalar2=None, op0=ALU.mult), reads=[nbg], writes=[nbg])
        add("sp", lambda e: e.dma_start(out=posT.ap, in_=pos_all.ap().rearrange("(t p) c -> p t c", p=128),
                                        allow_slow_non_contiguous=True), writes=[posT], dma="c0")
        add("pool", lambda e: e.dma_start(out=wg_up.ap[:, 0, :], in_=w_gf_up.ap()), writes=[wg_up], dma="c1")
        add("pool", lambda e: e.dma_start(out=wg_up.ap[:, 1, :], in_=w_gb_up.ap()), writes=[wg_up], dma="c1")

        conv_list = []
        win_v = w_in.ap().rearrange("(kc p) n -> p kc n", p=128)
        conv_list.append((wgates_r.ap().rearrange("p (kc n) -> p kc n", n=32), win_v[:, :, 4608:4640], B_wgates))
        for gi in (2, 4, 5, 6, 3, 7, 8, 0, 1):
            c0 = IN_GROUP_COLS[gi]
            conv_list.append((win_r.ap()[gi].rearrange("p (kc n) -> p kc n", n=512), win_v[:, :, c0:c0 + 512], B_win[gi]))
        wout_v = w_out.ap().rearrange("(kc p) n -> p kc n", p=128)
        for cg in range(4):
            conv_list.append((wout_r.ap()[cg].rearrange("p (kc n) -> p kc n", n=512), wout_v[:, :, cg * 512:(cg + 1) * 512], B_wout[cg]))
        wgu_v = w_gate_up.ap().rearrange("(kc p) n -> p kc n", p=128)
        for f in range(22):
            dst = wgu_r.ap()[f].rearrange("p (s kc n) -> p s kc n", s=2, n=256)
            conv_list.append((dst[:, 0], wgu_v[:, :, f * 256:(f + 1) * 256], B_wgu[f]))
            conv_list.append((dst[:, 1], wgu_v[:, :, DFF + f * 256:DFF + (f + 1) * 256], B_wgu[f]))
        for half in range(2):
            for cg in range(4):
                for part in range(2):
                    idx = (half * 4 + cg) * 2 + part
                    r0 = half * 2816 + part * 1408
                    src = w_down.ap()[r0:r0 + 1408, cg * 512:(cg + 1) * 512].rearrange("(f p) n -> p f n", p=128)
                    conv_list.append((wdn_r.ap()[idx].rearrange("p (f n) -> p f n", n=512), src, B_wdn[idx]))
        wpg_v = w_ple_gate.ap().rearrange("(kc p) n -> p kc n", p=128)
        for cg in range(4):
            conv_list.append((wpg_r.ap()[cg].rearrange("p (kc n) -> p kc n", n=512), wpg_v[:, :, cg * 512:(cg + 1) * 512], B_wpg[cg]))
        conv_list.append((wpp_r.ap().rearrange("p (kc n) -> p kc n", n=2048), w_ple_proj.ap().rearrange("(kc p) n -> p kc n", p=128), B_wpp))
        conv_pos = [0]

        def do_conv(n):
            for _ in range(n):
                if conv_pos[0] >= len(conv_list):
                    return
                dst, src, bw = conv_list[conv_pos[0]]
                conv_pos[0] += 1
                add("pool", lambda e, dst=dst, src=src: e.dma_start(out=dst, in_=src), writes=[bw], dma="conv")

        do_conv(10)
        add("sp", lambda e: e.dma_start(out=wgates.ap, in_=wgates_r.ap().rearrange("p (kc n) -> p kc n", n=32)),
            reads=[B_wgates], writes=[wgates], dma="c2")

        add("dve", lambda e: e.tensor_reduce(out=small.ap[0:1, 0:1], in_=gq_bc.ap[0:1, :], axis=AX.X, op=ALU.max, apply_absolute_value=True),
            reads=[gq_bc], writes=[small])
        add("dve", lambda e: e.tensor_reduce(out=small.ap[0:1, 1:2], in_=gk_bc.ap[0:1, :], axis=AX.X, op=ALU.max, apply_absolute_value=True),
            reads=[gk_bc], writes=[small])
        add("dve", lambda e: e.tensor_tensor(out=small.ap[0:1, 2:3], in0=small.ap[0:1, 0:1], in1=small.ap[0:1, 1:2], op=ALU.mult),
            reads=[small], writes=[small])
        add("dve", lambda e: e.tensor_scalar(out=small.ap[0:1, 3:4], in0=small.ap[0:1, 2:3], scalar1=-float(np.sqrt(128.0)), scalar2=None, op0=ALU.mult),
            reads=[small], writes=[small])
        add("pe", lambda e: e.matmul(banks[0].ap[:, 0:1], lhsT=ones_f.ap[0:1, 0:128], rhs=small.ap[0:1, 3:4], start=True, stop=True),
            reads=[ones_f, small], writes=[banks[0]])
        add("dve", lambda e: e.tensor_copy(out=negshift.ap, in_=banks[0].ap[:, 0:1]), reads=[banks[0]], writes=[negshift])

        add("pool", lambda e: e.iota(fi.ap, pattern=[[1, 32]], base=0, channel_multiplier=0), writes=[fi])
        for q4 in range(4):
            add("dve", lambda e, q4=q4: e.tensor_copy(out=freq4.ap[:, q4 * 32:(q4 + 1) * 32], in_=fi.ap), reads=[fi, ident], writes=[freq4])
        add("act", lambda e: e.activation(out=freq4.ap, in_=freq4.ap, func=AF.Exp, scale=-float(np.log(10000.0)) / 32.0), reads=[freq4], writes=[freq4])
        add("pool", lambda e: e.memset(phase4.ap, 0.0), writes=[phase4])
        add("pool", lambda e: e.memset(phase4.ap[:, 32:64], PI / 2), writes=[phase4])
        add("pool", lambda e: e.memset(phase4.ap[:, 96:128], PI / 2), writes=[phase4])
        ang = sbt("ang", [128, 4, 128], F32)
        angi = sbt("angi", [128, 4, 128], I32)
        angm = sbt("angm", [128, 4, 128], F32)
        n_own_tiles = n_own_blocks * 4
        def rope_gen(gi):
            for t in range(4):
                T = gi * 4 + t
                add("dve", lambda e, T=T, t=t: e.scalar_tensor_tensor(out=ang.ap[:, t, 0:64], in0=freq4.ap[:, 0:64], scalar=posT.ap[:, T, 0:1],
                                                                      in1=phase4.ap[:, 0:64], op0=ALU.mult, op1=ALU.add),
                    reads=[freq4, posT, phase4], writes=[ang])
                add("dve", lambda e, T=T, t=t: e.scalar_tensor_tensor(out=ang.ap[:, t, 64:128], in0=freq4.ap[:, 64:128], scalar=posT.ap[:, T, 1:2],
                                                                      in1=phase4.ap[:, 64:128], op0=ALU.mult, op1=ALU.add),
                    reads=[freq4, posT, phase4], writes=[ang])
            add("dve", lambda e: e.tensor_scalar(out=angi.ap, in0=ang.ap, scalar1=1.0 / (2 * PI), scalar2=None, op0=ALU.mult), reads=[ang], writes=[angi])
            add("dve", lambda e: e.scalar_tensor_tensor(out=ang.ap, in0=angi.ap, scalar=-2 * PI, in1=ang.ap, op0=ALU.mult, op1=ALU.add),
                reads=[angi, ang], writes=[ang])
            add("dve", lambda e: e.tensor_scalar(out=angm.ap, in0=ang.ap, scalar1=PI, scalar2=2 * PI, op0=ALU.is_gt, op1=ALU.mult), reads=[ang], writes=[angm])
            add("dve", lambda e: e.tensor_tensor(out=ang.ap, in0=ang.ap, in1=angm.ap, op=ALU.subtract), reads=[ang, angm], writes=[ang])
            add("dve", lambda e: e.tensor_scalar(out=angm.ap, in0=ang.ap, scalar1=-PI, scalar2=2 * PI, op0=ALU.is_lt, op1=ALU.mult), reads=[ang], writes=[angm])
            add("dve", lambda e: e.tensor_tensor(out=ang.ap, in0=ang.ap, in1=angm.ap, op=ALU.add), reads=[ang, angm], writes=[ang])
            add("dve", lambda e: e.tensor_scalar(out=ang.ap, in0=ang.ap, scalar1=-3.14159, scalar2=3.14159, op0=ALU.max, op1=ALU.min), reads=[ang], writes=[ang])
            add("act", lambda e: e.activation(out=ang.ap, in_=ang.ap, func=AF.Sin), reads=[ang], writes=[ang])
            add("sp", lambda e, gi=gi: e.dma_start(out=rope_d.ap()[gi * 512:(gi + 1) * 512, :].rearrange("(t p) c -> p t c", p=128), in_=ang.ap),
                reads=[ang], writes=[B_rope[gi]], dma="ropew")

        def rmsnorm_to_hT(xt, dst_hT, t, gidx, xn_slot, pbanks):
            s1, s2 = new_stat()
            add("act", lambda e: e.activation(out=xn_slot.ap, in_=xt.ap, func=AF.Square, accum_out=s1.ap), reads=[xt], writes=[xn_slot, s1])
            add("act", lambda e: e.activation(out=s2.ap, in_=s1.ap, func=AF.Ln, scale=1.0 / D, bias=EPS), reads=[s1], writes=[s2])
            add("act", lambda e: e.activation(out=s2.ap, in_=s2.ap, func=AF.Exp, scale=-0.5), reads=[s2], writes=[s2])
            add("dve", lambda e: e.tensor_scalar(out=xn_slot.ap, in0=xt.ap, scalar1=s2.ap, scalar2=None, op0=ALU.mult),
                reads=[xt, s2], writes=[xn_slot])
            for half in range(2):
                pb = pbanks[half]
                for k8 in range(8):
                    kc = half * 8 + k8
                    add("pe", lambda e, kc=kc, k8=k8, pb=pb: e.transpose(out=bank_bf(pb)[:, k8 * 128:(k8 + 1) * 128],
                                                                         in_=xn_slot.ap[:, kc * 128:(kc + 1) * 128], identity=ident.ap),
                        reads=[xn_slot, ident], writes=[banks[pb]])
                add("dve", lambda e, half=half, pb=pb: e.tensor_tensor(
                    out=dst_hT.ap[:, half * 8:(half + 1) * 8, t * 128:(t + 1) * 128],
                    in0=bank_bf(pb).rearrange("p (a b) -> p a b", a=8),
                    in1=g_fm.ap[:, gidx, half * 8:(half + 1) * 8].unsqueeze(2).to_broadcast([128, 8, 128]), op=ALU.mult),
                    reads=[banks[pb], g_fm], writes=[dst_hT])

        def qk_norm_rope(srcT, src_ap, H, g_bc, ropeT, rope_ap, outT, out_ap, tmpA, tmpB):
            src3 = src_ap.rearrange("p (h d) -> p h d", h=H)
            A3 = tmpA.ap.rearrange("p (h d) -> p h d", h=H)
            s1, s2 = new_stat(H)
            add("act", lambda e: e.activation(out=A3, in_=src3, func=AF.Square), reads=[srcT], writes=[tmpA])
            add("dve", lambda e: e.tensor_reduce(out=s2.ap, in_=A3, axis=AX.X, op=ALU.add), reads=[tmpA], writes=[s2])
            add("act", lambda e: e.activation(out=s2.ap, in_=s2.ap, func=AF.Ln, scale=1.0 / 128, bias=EPS), reads=[s2], writes=[s2])
            add("act", lambda e: e.activation(out=s2.ap, in_=s2.ap, func=AF.Exp, scale=-0.5), reads=[s2], writes=[s2])
            add("dve", lambda e: e.tensor_tensor(out=A3, in0=src3, in1=s2.ap.unsqueeze(2).to_broadcast([128, H, 128]), op=ALU.mult),
                reads=[srcT, s2], writes=[tmpA])
            add("dve", lambda e: e.tensor_tensor(out=A3, in0=A3, in1=g_bc.ap.unsqueeze(1).to_broadcast([128, H, 128]), op=ALU.mult),
                reads=[tmpA, g_bc], writes=[tmpA])
            x5 = tmpA.ap.rearrange("p (h a b f) -> p h a b f", h=H, a=2, b=2)
            t5 = tmpB.ap.rearrange("p (h a b f) -> p h a b f", h=H, a=2, b=2)
            o5 = out_ap.rearrange("p h (a b f) -> p h a b f", a=2, b=2)
            r4 = rope_ap.rearrange("p (a b f) -> p a b f", a=2, b=2)
            sin_b = r4[:, :, 0, :].unsqueeze(1).to_broadcast([128, H, 2, 32])
            cos_b = r4[:, :, 1, :].unsqueeze(1).to_broadcast([128, H, 2, 32])
            x1 = x5[:, :, :, 0, :]
            x2 = x5[:, :, :, 1, :]
            add("dve", lambda e: e.tensor_tensor(out=t5[:, :, :, 0, :], in0=x2, in1=sin_b, op=ALU.mult), reads=[tmpA, ropeT], writes=[tmpB])
            add("dve", lambda e: e.tensor_tensor(out=t5[:, :, :, 1, :], in0=x1, in1=sin_b, op=ALU.mult), reads=[tmpA, ropeT], writes=[tmpB])
            add("dve", lambda e: e.tensor_tensor(out=x1, in0=x1, in1=cos_b, op=ALU.mult), reads=[tmpA, ropeT], writes=[tmpA])
            add("dve", lambda e: e.tensor_tensor(out=x2, in0=x2, in1=cos_b, op=ALU.mult), reads=[tmpA, ropeT], writes=[tmpA])
            add("dve", lambda e: e.tensor_tensor(out=o5[:, :, :, 0, :], in0=x1, in1=t5[:, :, :, 0, :], op=ALU.subtract), reads=[tmpA, tmpB], writes=[outT])
            add("dve", lambda e: e.tensor_tensor(out=o5[:, :, :, 1, :], in0=x2, in1=t5[:, :, :, 1, :], op=ALU.add), reads=[tmpA, tmpB], writes=[outT])

        def softplus_neg(ps_tile, ps_ap, dirn, h, Ldst):
            add("act", lambda e: e.activation(out=Ldst.ap, in_=ps_ap, func=AF.Exp, scale=-1.0, bias=nbg.ap[:, dirn, h:h + 1]),
                reads=[ps_tile, nbg], writes=[Ldst])
            add("act", lambda e: e.activation(out=Ldst.ap, in_=Ldst.ap, func=AF.Ln, bias=1.0), reads=[Ldst], writes=[Ldst])

        def gates_low(hT_t, pb, glow):
            for dirn in range(2):
                for kc in range(16):
                    add("pe", lambda e, kc=kc, dirn=dirn: e.matmul(banks[pb].ap[0:16, :], lhsT=wgates.ap[:, kc, dirn * 16:(dirn + 1) * 16],
                                                                  rhs=hT_t.ap[:, kc, :], start=(kc == 0), stop=(kc == 15)),
                        reads=[wgates, hT_t], writes=[banks[pb]])
                add("act", lambda e, dirn=dirn: e.activation(out=glow.ap[:, dirn, :], in_=banks[pb].ap[0:16, :], func=AF.Copy),
                    reads=[banks[pb]], writes=[glow])

        def mm_group(pb_ap, pbT, lhs_fn, rhs_fn, n, reads):
            for kc in range(n):
                MM(pb_ap, lhs_fn(kc), rhs_fn(kc), kc == 0, kc == n - 1, reads, [pbT])

        wkv = R.view(0, [16, 512], BF16)
        xs_a = [R.view(16 * KB + i * 8 * KB, [2048], F32) for i in range(2)]
        o = 32 * KB
        ropeA = [R.view(o + i * 2 * KB, [4, 128], F32) for i in range(2)]; o += 4 * KB
        kvA = R.view(o, [256], F32); o += 1 * KB
        kvB = R.view(o, [256], F32); o += 1 * KB
        k_bf = R.view(o, [2, 128], BF16); o += 512
        v_bf = R.view(o, [2, 128], BF16); o += 512
        kT_blk = R.view(o, [2, 512], BF16); o += 2 * KB
        glowA = R.view(o, [2, 512], BF16, parts=16); o += 2 * KB
        LtD = [R.view(o + i * 2 * KB, [512], F32) for i in range(2)]; o += 4 * KB
        CtD = [R.view(o + i * 2 * KB, [512], F32) for i in range(2)]; o += 4 * KB
        kt_fm = [R.view(o + i * KB, [512], BF16) for i in range(2)]; o += 2 * KB
        kt_tm = [R.view(o + i * KB, [4, 128], BF16) for i in range(2)]; o += 2 * KB
        v_tmA = R.view(o, [4, 1024], BF16); o += 8 * KB
        Sf_st = R.view(o, [4, 256], F32); o += 4 * KB
        Sb_st = R.view(o, [4, 4, 256], F32); o += 16 * KB
        assert o <= 86 * KB, o
        smallD = [sbt("smallD%d" % i, [128, 8], F32) for i in range(2)]

        def load_w(slot_view, slotT, src_ap, srcbuf, key):
            add("sp", lambda e: e.dma_start(out=slot_view, in_=src_ap), reads=[srcbuf], writes=[slotT], dma=key)

        def win_src(gi):
            return win_r.ap()[gi].rearrange("p (kc n) -> p kc n", n=512)

        if n_ctx_blocks > 0:
            load_w(wkv.ap, wkv, win_src(2), B_win[2], "wA")
            load_w(wslot[0].ap, wslot[0], win_src(4), B_win[4], "wA")
            load_w(wslot[1].ap, wslot[1], win_src(5), B_win[5], "wA")
            load_w(wslot[2].ap, wslot[2], win_src(6), B_win[6], "wA")

        def init_states():
            add("pool", lambda e: e.memset(Sf_st.ap, 0.0), writes=[Sf_st])
            add("pool", lambda e: e.memset(Sb_st.ap, 0.0), writes=[Sb_st])
            add("pool", lambda e: e.memset(dacc.ap, 1.0), writes=[dacc])

        def spill_states(seq):
            add("sp", lambda e: e.dma_start(out=sfb_d.ap()[seq, 0].rearrange("p (h v) -> p h v", h=4), in_=Sf_st.ap),
                reads=[Sf_st], writes=[B_sfb[seq][0]], dma="spill")
            for j in range(4):
                add("sp", lambda e, j=j: e.dma_start(out=sfb_d.ap()[seq, 1 + j].rearrange("p (h v) -> p h v", h=4), in_=Sb_st.ap[:, :, j, :]),
                    reads=[Sb_st], writes=[B_sfb[seq][1 + j]], dma="spill")

        def run_interleaved(gens):
            gens = list(gens)
            while gens:
                for g_ in list(gens):
                    try:
                        next(g_)
                    except StopIteration:
                        gens.remove(g_)

        def gen_F(B):
            hT_t = hTa if B % 2 == 0 else hTb
            r0 = NOWN + B * 512
            rp = ropeA[B % 2]
            DMA("sp", rp.ap, rope_d.ap()[r0:r0 + 512, :].rearrange("(t p) c -> p t c", p=128), [B_rope[r0 // 512]], [rp], "ropeA%d" % (B % 2))
            for t in range(4):
                T = B * 4 + t
                xt = xs_a[T % 2]
                DMA("sp", xt.ap, x_ctx.ap()[T * 128:(T + 1) * 128, :], [], [xt], "xa%d" % (T % 2))
                yield
                rmsnorm_to_hT(xt, hT_t, t, 0, xn_bf[T % 2], (0, 0))
                yield
            if B == 0:
                dbg("hT0", hT_t, hT_t.ap[:, 0, :], [128, 512])

        def gen_KV(B):
            hT_t = hTa if B % 2 == 0 else hTb
            rp = ropeA[B % 2]
            for t in range(4):
                mm_group(banks[3].ap, banks[3], lambda kc: hT_t.ap[:, kc, t * 128:(t + 1) * 128], lambda kc: wkv.ap[:, kc, :], 16, [hT_t, wkv])
                yield
                ACT(v_bf.ap, banks[3].ap[:, 256:512].rearrange("p (g d) -> p g d", g=2), AF.Copy, [banks[3]], [v_bf])
                qk_norm_rope(banks[3], banks[3].ap[:, 0:256], 2, gk_bc, rp, rp.ap[:, t, :], k_bf, k_bf.ap, kvA, kvB)
                yield
                for g in range(2):
                    TR(bank_bf(0)[:, g * 128:(g + 1) * 128], k_bf.ap[:, g, :], [k_bf], [banks[0]])
                ACT(kT_blk.ap[:, :, t * 128:(t + 1) * 128], bank_bf(0)[:, 0:256].rearrange("p (g d) -> p g d", g=2), AF.Copy, [banks[0]], [kT_blk])
                chunk = B * 4 + t
                DMA("sp", v_r.ap()[:, :, chunk * 128:(chunk + 1) * 128].rearrange("g p d -> p g d"), v_bf.ap, [v_bf], [B_v[B]], "vst")
                yield
            DMA("sp", kT_ctx.ap()[:, :, B * 512:(B + 1) * 512].rearrange("g p n -> p g n"), kT_blk.ap, [kT_blk], [B_kT[B]], "kst")
            if B == 0:
                dbg("kT0", kT_blk, kT_blk.ap[:, 0, :], [128, 512])

        def gen_GKd(B, h, dirn, pk):
            pb = banks[5 + dirn]
            Lt, Ct, sd = LtD[dirn], CtD[dirn], smallD[dirn]
            mB = masks.ap[:, B, :]
            MM(pb.ap, wg_up.ap[:, dirn, h * 128:(h + 1) * 128], glowA.ap[:, dirn, :], True, True, [wg_up, glowA], [pb])
            softplus_neg(pb, pb.ap, dirn, h, Lt)
            yield
            add("dve", lambda e: e.tensor_tensor_scan(out=Ct.ap, data0=ones_f.ap, data1=Lt.ap, initial=0.0, op0=ALU.mult, op1=ALU.add),
                [ones_f, Lt], [Ct])
            TS(sd.ap[:, 0:1], Ct.ap[:, 511:512], -1.0 / 16, ALU.mult, [Ct], [sd])
            ACT(sd.ap[:, 1:2], sd.ap[:, 0:1], AF.Exp, [sd], [sd])
            yield
            if dirn == 0:
                ACT(Ct.ap, Ct.ap, AF.Exp, [Ct, sd], [Ct], scale=1.0 / 16, bias=sd.ap[:, 0:1])
            else:
                TT(Ct.ap, Ct.ap, Lt.ap, ALU.subtract, [Ct, Lt], [Ct])
                ACT(Ct.ap, Ct.ap, AF.Exp, [Ct], [Ct], scale=-1.0 / 16)
            TT(kt_fm[dirn].ap, banks[pk].ap, Ct.ap, ALU.mult, [banks[pk], Ct], [kt_fm[dirn]])
            yield
            for t in range(4):
                TR(bank_bf(5 + dirn)[:, t * 128:(t + 1) * 128], kt_fm[dirn].ap[:, t * 128:(t + 1) * 128], [kt_fm[dirn]], [pb])
            ACT(kt_tm[dirn].ap, bank_bf(5 + dirn)[:, 0:512].rearrange("p (t d) -> p t d", t=4), AF.Copy, [pb], [kt_tm[dirn]])
            yield
            kv = pb.ap[:, 256:512]
            for t in range(4):
                MM(kv, kt_tm[dirn].ap[:, t, :], v_tmA.ap[:, t, h * 256:(h + 1) * 256], t == 0, t == 3, [kt_tm[dirn], v_tmA], [pb])
            yield
            if dirn == 0:
                STT(sd.ap[:, 2:3], sd.ap[:, 1:2], mB[:, 0:1], mB[:, 1:2], ALU.mult, ALU.add, [sd, masks], [sd])
                TS(Sf_st.ap[:, h, :], Sf_st.ap[:, h, :], sd.ap[:, 2:3], ALU.mult, [Sf_st, sd], [Sf_st])
                STT(Sf_st.ap[:, h, :], kv, mB[:, 0:1], Sf_st.ap[:, h, :], ALU.mult, ALU.add, [pb, masks, Sf_st], [Sf_st])
            else:
                TT(dtmp.ap[:, h, :], dacc.ap[:, h, :], mB[:, 2:6], ALU.mult, [dacc, masks], [dtmp])
                for j in range(4):
                    STT(Sb_st.ap[:, h, j, :], kv, dtmp.ap[:, h, j:j + 1], Sb_st.ap[:, h, j, :], ALU.mult, ALU.add, [pb, dtmp, Sb_st], [Sb_st])
                    if j == 1:
                        yield
                STT(dtmp2.ap[:, h, :], mB[:, 2:6], sd.ap[:, 1:2], mB[:, 6:10], ALU.mult, ALU.add, [masks, sd], [dtmp2])
                TT(dacc.ap[:, h, :], dacc.ap[:, h, :], dtmp2.ap[:, h, :], ALU.mult, [dacc, dtmp2], [dacc])
            yield

        def gen_GV(B):
            hT_t = hTa if B % 2 == 0 else hTb
            gates_low(hT_t, 1, glowA)
            yield
            for t in range(4):
                for c2 in range(2):
                    pb = banks[1 + c2]
                    mm_group(pb.ap, pb, lambda kc: hT_t.ap[:, kc, t * 128:(t + 1) * 128], lambda kc: wslot[1 + c2].ap[:, kc, :], 16, [hT_t, wslot[1 + c2]])
                    ACT(v_tmA.ap[:, t, c2 * 512:(c2 + 1) * 512], pb.ap, AF.Copy, [pb], [v_tmA])
                    yield

        def gen_G(B):
            hT_t = hTa if B % 2 == 0 else hTb
            if B == 8:
                spill_states(0)
                init_states()
            yield
            yield
            for h in range(4):
                pk = 4 if h % 2 == 0 else 7
                mm_group(banks[pk].ap, banks[pk], lambda kc: wslot[0].ap[:, kc, h * 128:(h + 1) * 128], lambda kc: hT_t.ap[:, kc, :], 16, [hT_t, wslot[0]])
                yield
                g0 = gen_GKd(B, h, 0, pk)
                g1 = gen_GKd(B, h, 1, pk)
                live = [g0, g1]
                while live:
                    for g_ in list(live):
                        try:
                            next(g_)
                        except StopIteration:
                            live.remove(g_)
                    yield

        for b_ in range(min(2, n_ctx_blocks)):
            rope_gen(8 + b_)
        if n_ctx_blocks > 0:
            init_states()
            run_interleaved([gen_F(0)])
        for B in range(n_ctx_blocks):
            do_conv(3)
            if B + 2 < n_ctx_blocks:
                rope_gen(8 + B + 2)
            gens = [gen_GV(B), gen_G(B), gen_KV(B)]
            if B + 1 < n_ctx_blocks:
                gens.append(gen_F(B + 1))
            run_interleaved(gens)
        if n_ctx_blocks > 8:
            spill_states(1)
        elif n_ctx_blocks > 0:
            spill_states(0)
        if n_ctx_blocks > 0:
            dbg("Sf_last", Sf_st, Sf_st.ap[:, 0, :], [128, 256])
            dbg("Sb_last", Sb_st, Sb_st.ap[:, 0, 0, :], [128, 256])
        do_conv(1000)
        for b_ in range(n_own_blocks):
            rope_gen(b_)

        o = 0
        qT = R.view(o, [8, 512], BF16); o += 8 * KB
        ropeO = R.view(o, [4, 128], F32); o += 2 * KB
        tA = R.view(o, [512], F32); o += 2 * KB
        tB = R.view(o, [512], F32); o += 2 * KB
        q_bf = R.view(o, [4, 128], BF16); o += 1 * KB
        glowO = R.view(o, [2, 512], BF16, parts=16); o += 2 * KB
        att_base = o
        Lg = [R.view(o + i * 2 * KB, [512], F32) for i in range(2)]; o += 4 * KB
        Cg = [R.view(o + i * 2 * KB, [512], F32) for i in range(2)]; o += 4 * KB
        E1 = R.view(o, [512], F32); o += 2 * KB
        E2 = R.view(o, [512], F32); o += 2 * KB
        qd = [R.view(o + i * 4 * KB, [4, 512], BF16) for i in range(2)]; o += 8 * KB
        kd = [R.view(o + i * 4 * KB, [4, 512], BF16) for i in range(2)]; o += 8 * KB
        v_tm = R.view(o, [4, 1024], BF16); o += 8 * KB
        gsil = R.view(o, [4, 1024], BF16); o += 8 * KB
        ktm = [R.view(o + i * KB, [4, 128], BF16) for i in range(2)]; o += 2 * KB
        Tb_bf = R.view(o, [4, 4, 256], BF16); o += 8 * KB
        Sf_bf = R.view(o, [4, 256], BF16); o += 2 * KB
        Sb_cur = R.view(o, [4, 256], F32); o += 4 * KB
        Am = [R.view(o + i * KB, [4, 128], BF16) for i in range(2)]; o += 2 * KB
        mix_bf = R.view(o, [1024], BF16); o += 2 * KB
        assert o <= 86 * KB, o
        xs_o = [T16b.view(i * 8 * KB, [2048], F32) for i in range(2)]
        o = att_base
        kts = [R.view(o + i * 2 * KB, [1024], BF16) for i in range(3)]; o += 6 * KB
        vts = [R.view(o + i * 2 * KB, [8, 128], BF16) for i in range(3)]; o += 6 * KB
        PTP = [R.view(o + i * 2 * KB, [1024], BF16) for i in range(4)]; o += 8 * KB
        accs2 = [R.view(o + i * 4 * KB, [1024], F32) for i in range(2)]; o += 8 * KB
        rsum = R.view(o, [512], F32, parts=1); o += 2 * KB
        bcs = R.view(o, [512], F32); o += 2 * KB
        xs2 = [R.view(i * 8 * KB, [2048], F32) for i in range(4)]
        Y = [R.view(32 * KB + i * 8 * KB, [2048], F32) for i in range(4)]
        Hh = R.view(64 * KB, [22, 512], BF16)
        junk2 = R.view(64 * KB, [2048], BF16)
        p32 = R.view(64 * KB, [4, 256], F32)
        p_bf = R.view(68 * KB, [4, 256], BF16)
        pT = R.view(70 * KB, [2, 512], BF16)
        sgm = [R.view(72 * KB + i * 2 * KB, [512], F32) for i in range(2)]
        wpp_sb = R.view(76 * KB, [2, 2048], BF16)
        sgt = [XN.view(i * 2 * KB, [512], F32) for i in range(2)]
        wslot_gu = [W.view(i * 16 * KB, [2, 16, 256], BF16) for i in range(3)]
        wslot_dn = [W.view(i * 16 * KB, [11, 512], BF16) for i in range(3)]

        blk_uses = []
        for gi in (3, 4, 5, 6, 7, 8, 0, 1):
            blk_uses.append(("in", gi))
        for cg in range(4):
            blk_uses.append(("out", cg))
        for half in range(2):
            for f in range(11):
                blk_uses.append(("gu", half * 11 + f))
            for cg in range(4):
                for part in range(2):
                    blk_uses.append(("dn", (half * 4 + cg) * 2 + part))
        for cg in range(4):
            blk_uses.append(("pg", cg))
        NU = len(blk_uses)
        ws_issued = [0]
        total_uses = NU * n_own_blocks

        def ws_issue(i):
            kind, idx = blk_uses[i % NU]
            sl = i % 3
            key = "ws%d" % sl
            if kind == "in":
                DMA("sp", wslot[sl].ap, win_src(idx), [B_win[idx]], [wslot[sl]], key)
            elif kind == "out":
                DMA("sp", wslot[sl].ap, wout_r.ap()[idx].rearrange("p (kc n) -> p kc n", n=512), [B_wout[idx]], [wslot[sl]], key)
            elif kind == "gu":
                DMA("sp", wslot_gu[sl].ap, wgu_r.ap()[idx].rearrange("p (s kc n) -> p s kc n", s=2, n=256), [B_wgu[idx]], [wslot_gu[sl]], key)
            elif kind == "dn":
                DMA("sp", wslot_dn[sl].ap, wdn_r.ap()[idx].rearrange("p (f n) -> p f n", n=512), [B_wdn[idx]], [wslot_dn[sl]], key)
            elif kind == "pg":
                DMA("sp", wslot[sl].ap, wpg_r.ap()[idx].rearrange("p (kc n) -> p kc n", n=512), [B_wpg[idx]], [wslot[sl]], key)

        def ws_get(i):
            while ws_issued[0] <= i:
                ws_issue(ws_issued[0])
                ws_issued[0] += 1
            return i % 3

        QS = float(128.0 ** -0.5)

        for j in range(n_own_blocks):
            seq = j // 4
            jj = j % 4
            r0 = j * 512
            cb = 0 if seq == 0 else 32
            nkc = 32 if seq == 0 else 128
            ub = j * NU
            DMA("sp", Sb_cur.ap, sfb_d.ap()[seq, 1 + jj].rearrange("p (h v) -> p h v", h=4), [B_sfb[seq][1 + jj]], [Sb_cur], "sbl")
            if jj == 0:
                DMA("sp", Sf_own.ap, sfb_d.ap()[seq, 0].rearrange("p (h v) -> p h v", h=4), [B_sfb[seq][0]], [Sf_own], "sbl")
            DMA("sp", ropeO.ap, rope_d.ap()[r0:r0 + 512, :].rearrange("(t p) c -> p t c", p=128), [B_rope[j]], [ropeO], "ropeO")
            for t in range(4):
                xt = xs_o[t % 2]
                DMA("sp", xt.ap, x_own.ap()[r0 + t * 128:r0 + (t + 1) * 128, :], [], [xt], "xo%d" % (t % 2))
                rmsnorm_to_hT(xt, hTa, t, 0, xn_bf[t % 2], (0, 1))
            if j == 0:
                dbg("hTo", hTa, hTa.ap[:, 0, :], [128, 512])
            gates_low(hTa, 5, glowO)
            sq = ws_get(ub + 0)
            sk = ws_get(ub + 1)
            for h in range(4):
                for dirn in range(2):
                    MM(banks[4].ap, wg_up.ap[:, dirn, h * 128:(h + 1) * 128], glowO.ap[:, dirn, :], True, True, [wg_up, glowO], [banks[4]])
                    softplus_neg(banks[4], banks[4].ap, dirn, h, Lg[dirn])
                    add("dve", lambda e, dirn=dirn: e.tensor_tensor_scan(out=Cg[dirn].ap, data0=rmask.ap, data1=Lg[dirn].ap, initial=0.0,
                                                                         op0=ALU.mult, op1=ALU.add), [rmask, Lg[dirn]], [Cg[dirn]])
                mm_group(banks[2].ap, banks[2], lambda kc: wslot[sq].ap[:, kc, h * 128:(h + 1) * 128], lambda kc: hTa.ap[:, kc, :], 16, [hTa, wslot[sq]])
                mm_group(banks[3].ap, banks[3], lambda kc: wslot[sk].ap[:, kc, h * 128:(h + 1) * 128], lambda kc: hTa.ap[:, kc, :], 16, [hTa, wslot[sk]])
                ACT(E1.ap, Cg[0].ap, AF.Exp, [Cg[0]], [E1], scale=-1.0 / 16)
                ACT(E2.ap, Cg[0].ap, AF.Exp, [Cg[0]], [E2], scale=1.0 / 16)
                STT(qd[0].ap[:, h, :], banks[2].ap, QS, E1.ap, ALU.mult, ALU.mult, [banks[2], E1], [qd[0]])
                TT(kd[0].ap[:, h, :], banks[3].ap, E2.ap, ALU.mult, [banks[3], E2], [kd[0]])
                add("dve", lambda e, h=h: e.tensor_copy(out=dtl.ap[:, 0, h, :], in_=E1.ap.rearrange("p (t c) -> p t c", c=128)[:, :, 127]), [E1], [dtl])
                ACT(dtl.ap[:, 1, h, :], Cg[1].ap.rearrange("p (t c) -> p t c", c=128)[:, :, 127], AF.Exp, [Cg[1]], [dtl], scale=-1.0 / 16)
                TT(E1.ap, Cg[1].ap, Lg[1].ap, ALU.subtract, [Cg[1], Lg[1]], [E1])
                ACT(E2.ap, E1.ap, AF.Exp, [E1], [E2], scale=1.0 / 16)
                ACT(E1.ap, E1.ap, AF.Exp, [E1], [E1], scale=-1.0 / 16)
                STT(qd[1].ap[:, h, :], banks[2].ap, QS, E2.ap, ALU.mult, ALU.mult, [banks[2], E2], [qd[1]])
                TT(kd[1].ap[:, h, :], banks[3].ap, E1.ap, ALU.mult, [banks[3], E1], [kd[1]])
            sv = [ws_get(ub + 2), ws_get(ub + 3)]
            for t in range(4):
                for c2 in range(2):
                    pb = 6 + c2
                    mm_group(banks[pb].ap, banks[pb], lambda kc: hTa.ap[:, kc, t * 128:(t + 1) * 128], lambda kc: wslot[sv[c2]].ap[:, kc, :], 16,
                             [hTa, wslot[sv[c2]]])
                    ACT(v_tm.ap[:, t, c2 * 512:(c2 + 1) * 512], banks[pb].ap, AF.Copy, [banks[pb]], [v_tm])
            sg_ = [ws_get(ub + 4), ws_get(ub + 5)]
            for t in range(4):
                for c2 in range(2):
                    pb = 6 + c2
                    mm_group(banks[pb].ap, banks[pb], lambda kc: hTa.ap[:, kc, t * 128:(t + 1) * 128], lambda kc: wslot[sg_[c2]].ap[:, kc, :], 16,
                             [hTa, wslot[sg_[c2]]])
                    ACT(tA.ap, banks[pb].ap, AF.Silu, [banks[pb]], [tA])
                    TT(gsil.ap[:, t, c2 * 512:(c2 + 1) * 512].rearrange("p (a b) -> p a b", a=2), tA.ap.rearrange("p (a b) -> p a b", a=2),
                       ggla_bc.ap.unsqueeze(1).to_broadcast([128, 2, 256]), ALU.mult, [tA, ggla_bc], [gsil])
            for t in (3, 2, 1, 0):
                kt = ktm[t % 2]
                for h in range(4):
                    TR(bank_bf(5)[:, h * 128:(h + 1) * 128], kd[1].ap[:, h, t * 128:(t + 1) * 128], [kd[1]], [banks[5]])
                ACT(kt.ap, bank_bf(5)[:, 0:512].rearrange("p (a b) -> p a b", a=4), AF.Copy, [banks[5]], [kt])
                for h in range(4):
                    pbk = banks[6 + h // 2]
                    MM(pbk.ap[:, (h % 2) * 256:(h % 2 + 1) * 256], kt.ap[:, h, :], v_tm.ap[:, t, h * 256:(h + 1) * 256], True, True, [kt, v_tm], [pbk])
                for h in range(4):
                    pbk = banks[6 + h // 2]
                    ACT(Tb_bf.ap[:, t, h, :], Sb_cur.ap[:, h, :], AF.Copy, [Sb_cur, dtl], [Tb_bf], scale=dtl.ap[:, 1, h, t:t + 1])
                    STT(Sb_cur.ap[:, h, :], Sb_cur.ap[:, h, :], dtl.ap[:, 1, h, t:t + 1], pbk.ap[:, (h % 2) * 256:(h % 2 + 1) * 256], ALU.mult, ALU.add,
                        [Sb_cur, dtl, pbk], [Sb_cur])
            for t in range(4):
                kt = ktm[t % 2]
                am = Am[t % 2]
                for h in range(4):
                    TR(bank_bf(5)[:, h * 128:(h + 1) * 128], kd[0].ap[:, h, t * 128:(t + 1) * 128], [kd[0]], [banks[5]])
                ACT(kt.ap, bank_bf(5)[:, 0:512].rearrange("p (a b) -> p a b", a=4), AF.Copy, [banks[5]], [kt])
                for h in range(4):
                    pbk = banks[6 + h // 2]
                    MM(pbk.ap[:, (h % 2) * 256:(h % 2 + 1) * 256], kt.ap[:, h, :], v_tm.ap[:, t, h * 256:(h + 1) * 256], True, True, [kt, v_tm], [pbk])
                ACT(Sf_bf.ap, Sf_own.ap, AF.Copy, [Sf_own], [Sf_bf])
                for h in range(4):
                    MM(banks[2].ap[:, h * 128:(h + 1) * 128], kd[0].ap[:, h, t * 128:(t + 1) * 128], qd[0].ap[:, h, t * 128:(t + 1) * 128], True, True,
                       [kd[0], qd[0]], [banks[2]])
                    MM(banks[3].ap[:, h * 128:(h + 1) * 128], kd[1].ap[:, h, t * 128:(t + 1) * 128], qd[1].ap[:, h, t * 128:(t + 1) * 128], True, True,
                       [kd[1], qd[1]], [banks[3]])
                TT(tA.ap.rearrange("p (a b) -> p a b", a=4), banks[2].ap.rearrange("p (a b) -> p a b", a=4),
                   maskf.ap.unsqueeze(1).to_broadcast([128, 4, 128]), ALU.mult, [banks[2], maskf], [tA])
                TT(tB.ap.rearrange("p (a b) -> p a b", a=4), banks[3].ap.rearrange("p (a b) -> p a b", a=4),
                   maskb.ap.unsqueeze(1).to_broadcast([128, 4, 128]), ALU.mult, [banks[3], maskb], [tB])
                TT(am.ap.rearrange("p a b -> p (a b)"), tA.ap, tB.ap, ALU.add, [tA, tB], [am])
                for h in range(4):
                    pbo = banks[h // 2]
                    oap = pbo.ap[:, (h % 2) * 256:(h % 2 + 1) * 256]
                    MM(oap, am.ap[:, h, :], v_tm.ap[:, t, h * 256:(h + 1) * 256], True, False, [am, v_tm], [pbo])
                    MM(oap, qd[0].ap[:, h, t * 128:(t + 1) * 128], Sf_bf.ap[:, h, :], False, False, [qd[0], Sf_bf], [pbo])
                    MM(oap, qd[1].ap[:, h, t * 128:(t + 1) * 128], Tb_bf.ap[:, t, h, :], False, True, [qd[1], Tb_bf], [pbo])
                for h in range(4):
                    pbk = banks[6 + h // 2]
                    TT(Sf_own.ap[:, h, :], Sf_own.ap[:, h, :], pbk.ap[:, (h % 2) * 256:(h % 2 + 1) * 256], ALU.add, [Sf_own, pbk], [Sf_own])
                    TS(Sf_own.ap[:, h, :], Sf_own.ap[:, h, :], dtl.ap[:, 0, h, t:t + 1], ALU.mult, [Sf_own, dtl], [Sf_own])
                s1, s2 = new_stat(4)
                for h in range(4):
                    pbo = banks[h // 2]
                    oap = pbo.ap[:, (h % 2) * 256:(h % 2 + 1) * 256]
                    ACT(tB.ap[:, 0:256], oap, AF.Square, [pbo], [tB, s1], accum=s1.ap[:, h:h + 1])
                ACT(s2.ap, s1.ap, AF.Ln, [s1], [s2], scale=1.0 / 256, bias=EPS)
                ACT(s2.ap, s2.ap, AF.Exp, [s2], [s2], scale=-0.5)
                for h in range(4):
                    pbo = banks[h // 2]
                    oap = pbo.ap[:, (h % 2) * 256:(h % 2 + 1) * 256]
                    STT(mix_bf.ap[:, h * 256:(h + 1) * 256], oap, s2.ap[:, h:h + 1], gsil.ap[:, t, h * 256:(h + 1) * 256], ALU.mult, ALU.mult,
                        [pbo, s2, gsil], [mix_bf])
                if j == 0 and t == 0:
                    dbg("gla0", mix_bf, mix_bf.ap[:, 0:512], [128, 512])
                for c in range(8):
                    TR(bank_bf(4)[:, c * 128:(c + 1) * 128], mix_bf.ap[:, c * 128:(c + 1) * 128], [mix_bf], [banks[4]])
                ACT(hTb.ap[:, 8:16, t * 128:(t + 1) * 128], bank_bf(4).rearrange("p (a b) -> p a b", a=8), AF.Copy, [banks[4]], [hTb])
            sa = [ws_get(ub + 6), ws_get(ub + 7)]
            for t in range(4):
                for cg in range(2):
                    pb = 2 + cg
                    mm_group(banks[pb].ap, banks[pb], lambda kc: hTa.ap[:, kc, t * 128:(t + 1) * 128], lambda kc: wslot[sa[cg]].ap[:, kc, :], 16,
                             [hTa, wslot[sa[cg]]])
                    qk_norm_rope(banks[pb], banks[pb].ap, 4, gq_bc, ropeO, ropeO.ap[:, t, :], q_bf, q_bf.ap, tA, tB)
                    for h4 in range(4):
                        TR(bank_bf(5)[:, h4 * 128:(h4 + 1) * 128], q_bf.ap[:, h4, :], [q_bf], [banks[5]])
                    ACT(qT.ap[:, cg * 4:(cg + 1) * 4, t * 128:(t + 1) * 128], bank_bf(5)[:, 0:512].rearrange("p (a b) -> p a b", a=4), AF.Copy,
                        [banks[5]], [qT])
            if j == 0:
                dbg("qT0", qT, qT.ap[:, 0, :], [128, 512])
            ngrp = nkc // 8
            kv_n = [0]
            for g in range(2):
                for pair in range(2):
                    heads = (4 * g + 2 * pair, 4 * g + 2 * pair + 1)
                    niter = ngrp * 8
                    slot_of = {}

                    def stage_S(n):
                        G, c = n // 8, n % 8
                        if c == 0:
                            sl = kv_n[0] % 3
                            kv_n[0] += 1
                            slot_of[G] = sl
                            c0 = cb + G * 8
                            blks = [B_kT[c0 // 4], B_kT[c0 // 4 + 1], B_v[c0 // 4], B_v[c0 // 4 + 1]]
                            DMA("sp", kts[sl].ap, kT_ctx.ap()[g, :, c0 * 128:(c0 + 8) * 128], blks, [kts[sl]], "kv%d" % sl)
                            DMA("sp", vts[sl].ap, v_r.ap()[g, :, c0 * 128:(c0 + 8) * 128].rearrange("p (c d) -> p c d", d=128), blks, [vts[sl]],
                                "kv%d" % sl)
                        sl = slot_of[G]
                        k3 = 1 + n % 3
                        db = dbanks[k3]
                        for i in range(2):
                            MM(db.ap[:, i * 512:(i + 1) * 512], kts[sl].ap[:, c * 128:(c + 1) * 128], qT.ap[:, heads[i], :], True, True,
                               [kts[sl], qT], [banks[2 * k3 + i]])

                    def stage_EV(n):
                        G, c = n // 8, n % 8
                        sl = slot_of[G]
                        db = dbanks[1 + n % 3]
                        pt = PTP[n % 4]
                        ACT(pt.ap, db.ap, AF.Exp, [db, negshift], [pt], scale=QS, bias=negshift.ap)
                        first = (n == 0)
                        last = (n == niter - 1)
                        for i in range(2):
                            MM(banks[i].ap, vts[sl].ap[:, c, :], pt.ap[:, i * 512:(i + 1) * 512], first, last, [vts[sl], pt], [banks[i]])
                        accs = accs2[n % 2]
                        if n < 2:
                            add("dve", lambda e, pt=pt, accs=accs: e.tensor_copy(out=accs.ap, in_=pt.ap), [pt], [accs])
                        else:
                            TT(accs.ap, accs.ap, pt.ap, ALU.add, [accs, pt], [accs])

                    stage_S(0)
                    if niter > 1:
                        stage_S(1)
                    for n in range(niter):
                        if n + 2 < niter:
                            stage_S(n + 2)
                        stage_EV(n)
                    accs = accs2[0]
                    if niter > 1:
                        TT(accs.ap, accs.ap, accs2[1].ap, ALU.add, [accs, accs2[1]], [accs])
                    for i in range(2):
                        hq = heads[i]
                        MM(banks[2 + i].ap[0:1, :], ones_f.ap[:, 0:1], accs.ap[:, i * 512:(i + 1) * 512], True, True, [ones_f, accs], [banks[2 + i]])
                        ACT(rsum.ap, banks[2 + i].ap[0:1, :], AF.Ln, [banks[2 + i]], [rsum])
                        ACT(rsum.ap, rsum.ap, AF.Exp, [rsum], [rsum], scale=-1.0)
                        MM(banks[4 + i].ap, ones_f.ap[0:1, 0:128], rsum.ap, True, True, [ones_f, rsum], [banks[4 + i]])
                        ACT(bcs.ap, banks[4 + i].ap, AF.Copy, [banks[4 + i]], [bcs])
                        TT(hTb.ap[:, hq, :], banks[i].ap, bcs.ap, ALU.mult, [banks[i], bcs], [hTb])
            if j == 0:
                dbg("mixT_a", hTb, hTb.ap[:, 0, :], [128, 512])
                dbg("mixT_g", hTb, hTb.ap[:, 8, :], [128, 512])
            for t in range(4):
                DMA("sp", xs2[t].ap, x_own.ap()[r0 + t * 128:r0 + (t + 1) * 128, :], [], [xs2[t]], "xs2_%d" % t)
            n = 0
            for cg in range(4):
                sl = ws_get(ub + 8 + cg)
                for t in range(4):
                    pb = banks[n % 8]
                    n += 1
                    mm_group(pb.ap, pb, lambda kc: hTb.ap[:, kc, t * 128:(t + 1) * 128], lambda kc: wslot[sl].ap[:, kc, :], 16, [hTb, wslot[sl]])
                    ACT(Y[t].ap[:, cg * 512:(cg + 1) * 512], pb.ap, AF.Copy, [pb], [Y[t]])

            def post_norm(gsrc):
                DMA("sp", gtmp.ap, gsrc.ap().partition_broadcast(128)[:, 0, :], [], [gtmp], "gtmp")
                for t in range(4):
                    s1, s2 = new_stat()
                    ACT(junk2.ap, Y[t].ap, AF.Square, [Y[t]], [junk2, s1], accum=s1.ap)
                    ACT(s2.ap, s1.ap, AF.Ln, [s1], [s2], scale=1.0 / D, bias=EPS)
                    ACT(s2.ap, s2.ap, AF.Exp, [s2], [s2], scale=-0.5)
                    STT(Y[t].ap, Y[t].ap, s2.ap, gtmp.ap, ALU.mult, ALU.mult, [Y[t], s2, gtmp], [Y[t]])
                    TT(xs2[t].ap, xs2[t].ap, Y[t].ap, ALU.add, [xs2[t], Y[t]], [xs2[t]])

            post_norm(g_post_mix)
            if j == 0:
                dbg("x1", xs2[0], xs2[0].ap[:, 0:512], [128, 512])
            for t in range(4):
                rmsnorm_to_hT(xs2[t], hTa, t, 1, xn_bf[t % 2], (0, 1))
            ui = ub + 12
            nff = 0
            for half in range(2):
                for f in range(11):
                    sl = ws_get(ui)
                    ui += 1
                    for c in range(2):
                        pg_ = banks[(nff % 2) * 2]
                        pu_ = banks[(nff % 2) * 2 + 1]
                        st_ = sgt[nff % 2]
                        nff += 1
                        mm_group(pg_.ap, pg_, lambda kc: wslot_gu[sl].ap[:, 0, kc, c * 128:(c + 1) * 128], lambda kc: hTa.ap[:, kc, :], 16,
                                 [hTa, wslot_gu[sl]])
                        mm_group(pu_.ap, pu_, lambda kc: wslot_gu[sl].ap[:, 1, kc, c * 128:(c + 1) * 128], lambda kc: hTa.ap[:, kc, :], 16,
                                 [hTa, wslot_gu[sl]])
                        ACT(st_.ap, pg_.ap, AF.Silu, [pg_], [st_])
                        TT(Hh.ap[:, f * 2 + c, :], st_.ap, pu_.ap, ALU.mult, [st_, pu_], [Hh])
                for cg in range(4):
                    for part in range(2):
                        sl = ws_get(ui)
                        ui += 1
                        for t in range(4):
                            pd = banks[4 + t]
                            for ffc in range(11):
                                MM(pd.ap, Hh.ap[:, part * 11 + ffc, t * 128:(t + 1) * 128], wslot_dn[sl].ap[:, ffc, :],
                                   part == 0 and ffc == 0, part == 1 and ffc == 10, [Hh, wslot_dn[sl]], [pd])
                    for t in range(4):
                        pd = banks[4 + t]
                        if half == 0:
                            ACT(Y[t].ap[:, cg * 512:(cg + 1) * 512], pd.ap, AF.Copy, [pd], [Y[t]])
                        else:
                            TT(Y[t].ap[:, cg * 512:(cg + 1) * 512], pd.ap, Y[t].ap[:, cg * 512:(cg + 1) * 512], ALU.add, [pd, Y[t]], [Y[t]])
            post_norm(g_post_ffn)
            if j == 0:
                dbg("x2", xs2[0], xs2[0].ap[:, 0:512], [128, 512])
            for t in range(4):
                rmsnorm_to_hT(xs2[t], hTa, t, 2, xn_bf[t % 2], (0, 1))
            DMA("sp", p32.ap, p_own.ap()[r0:r0 + 512, :].rearrange("(t p) c -> p t c", p=128), [], [p32], "p32")
            DMA("sp", wpp_sb.ap, wpp_r.ap().rearrange("p (kc n) -> p kc n", n=2048), [B_wpp], [wpp_sb], "wpp")
            ACT(p_bf.ap, p32.ap, AF.Copy, [p32], [p_bf])
            for t in range(4):
                for c in range(2):
                    TR(bank_bf(0)[:, c * 128:(c + 1) * 128], p_bf.ap[:, t, c * 128:(c + 1) * 128], [p_bf], [banks[0]])
                ACT(pT.ap[:, :, t * 128:(t + 1) * 128], bank_bf(0)[:, 0:256].rearrange("p (a b) -> p a b", a=2), AF.Copy, [banks[0]], [pT])
            n = 0
            for cg in range(4):
                sl = ws_get(ub + 50 + cg)
                for t in range(4):
                    pgb = banks[n % 4]
                    ppb = banks[4 + n % 4]
                    sg2 = sgm[n % 2]
                    n += 1
                    mm_group(pgb.ap, pgb, lambda kc: hTa.ap[:, kc, t * 128:(t + 1) * 128], lambda kc: wslot[sl].ap[:, kc, :], 16, [hTa, wslot[sl]])
                    mm_group(ppb.ap, ppb, lambda kc: pT.ap[:, kc, t * 128:(t + 1) * 128], lambda kc: wpp_sb.ap[:, kc, cg * 512:(cg + 1) * 512], 2,
                             [pT, wpp_sb])
                    ACT(sg2.ap, pgb.ap, AF.Sigmoid, [pgb], [sg2])
                    TT(Y[t].ap[:, cg * 512:(cg + 1) * 512], sg2.ap, ppb.ap, ALU.mult, [sg2, ppb], [Y[t]])
            post_norm(g_ple_post)
            for t in range(4):
                DMA("sp", y_own.ap()[r0 + t * 128:r0 + (t + 1) * 128, :], xs2[t].ap, [xs2[t]], [B_yout], "yout")

        S.final.append("yout")
        S.emit(nc, es)
    return nc, dbg_outs


_PROG = {}


def _core_inputs(c, inp):
    p, hf = c // 2, c % 2
    xp = inp["x_prompt"]
    xsm = inp["x_sample"]
    pp = inp["p_prompt"][0]
    psm = inp["p_sample"][0]
    d = {}
    d["x_own"] = np.ascontiguousarray(np.concatenate([xp[p, hf * 2048:(hf + 1) * 2048], xsm[0, c * 2048:(c + 1) * 2048]], 0), dtype=np.float32)
    d["p_own"] = np.ascontiguousarray(np.concatenate([pp[p, hf * 2048:(hf + 1) * 2048], psm[0, c * 2048:(c + 1) * 2048]], 0), dtype=np.float32)
    d["x_ctx"] = np.ascontiguousarray(np.concatenate([xp[p], xsm[0]], 0), dtype=np.float32)
    tok = np.concatenate([hf * 2048 + np.arange(2048), c * 2048 + np.arange(2048), np.arange(4096), np.arange(16384)])
    d["pos_all"] = np.stack([tok // 64, tok % 64], 1).astype(np.float32)
    m = np.zeros((40, 10), np.float32)
    for B in range(40):
        if B < 8:
            b, start = B, hf * 4
        else:
            b, start = B - 8, c * 4
        mf = 1.0 if b < start else 0.0
        m[B, 0] = mf
        m[B, 1] = 1.0 - mf
        for j in range(4):
            mb = 1.0 if b > start + j else 0.0
            m[B, 2 + j] = mb
            m[B, 6 + j] = 1.0 - mb
    d["masks"] = m.reshape(1, 400)
    for k in ("g_pre_mix", "w_in", "g_q", "g_k", "w_gf_up", "b_gf", "w_gb_up", "b_gb", "g_gla_norm", "w_out", "g_post_mix", "g_pre_ffn",
              "w_gate_up", "w_down", "g_post_ffn", "g_ple_pre", "w_ple_gate", "w_ple_proj", "g_ple_post"):
        a = np.asarray(inp[k], dtype=np.float32)[0]
        if a.ndim == 1:
            a = a[None, :]
        d[k] = np.ascontiguousarray(a)
    return d


def kernel(**inputs):
    inp = {k: np.asarray(v) for k, v in inputs.items()}
    if "nc" not in _PROG:
        _PROG["nc"] = build_program()[0]
    nc = _PROG["nc"]
    in_maps = [_core_inputs(c, inp) for c in range(8)]
    res = run_bass_kernel_spmd(nc, in_maps, core_ids=list(range(8)))
    y_prompt = np.zeros((4, 4096, D), np.float32)
    y_sample = np.zeros((1, 16384, D), np.float32)
    for c in range(8):
        y = np.asarray(res.results[c]["y_own"], dtype=np.float32)
        y_prompt[c // 2, (c % 2) * 2048:(c % 2 + 1) * 2048] = y[:2048]
        y_sample[0, c * 2048:(c + 1) * 2048] = y[2048:]
    return (y_prompt, y_sample)
```
